# Optimizing a Trainium2 kernel written in Bass

```python
import math
import jax, jax.numpy as jnp
from jax import lax
import numpy as np

D_MODEL = 1024
BATCH = 8
SEQ = 4096
DEPTH = 4

N_EVEN = (DEPTH + 1) // 2
N_ODD = DEPTH // 2

ATTN_HEADS = 4
QK_DIM = 64
V_DIM = 2 * QK_DIM
ATTN_WIDTH = ATTN_HEADS * V_DIM
Q_BLOCK = 128

LRU_WIDTH = D_MODEL // 2
LRU_HEADS = 4
LRU_HEAD_DIM = LRU_WIDTH // LRU_HEADS
CONV_WIDTH = 4
LRU_C = 8.0
LRU_MIN_RAD = 0.9
LRU_MAX_RAD = 0.999

Q_COLS = ATTN_HEADS * 2 * QK_DIM
K_COLS = ATTN_HEADS * 2 * QK_DIM
V_COLS = ATTN_WIDTH
IN_COLS = Q_COLS + K_COLS + V_COLS + 2 * LRU_WIDTH
MIX_WIDTH = ATTN_WIDTH + LRU_WIDTH

FOURIER_GROUPS = 4
FOURIER_GROUP_DIM = D_MODEL // FOURIER_GROUPS

D_FF = -(-8 * D_MODEL // (3 * 256)) * 256
EPS = 1e-6

kernel_name = 'hybrid_diffattn_rglru_fnet_encoder'


def rmsnorm(x, g):
    xf = x.astype(jnp.float32)
    y = xf * lax.rsqrt(jnp.mean(xf * xf, axis=-1, keepdims=True) + EPS) * g.astype(jnp.float32)
    return y.astype(x.dtype)


def alibi_slopes(n_heads):
    return jnp.exp2(-8.0 * jnp.arange(1, n_heads + 1, dtype=jnp.float32) / n_heads)


def diff_attention(q, k, v, lam, lam_init, subln_g):
    B, S = q.shape[0], q.shape[1]
    nb = S // Q_BLOCK
    scale = QK_DIM ** -0.5
    slopes = alibi_slopes(ATTN_HEADS)[None, :, None, None, None]
    kpos = jnp.arange(S)
    qb = q.reshape(B, nb, Q_BLOCK, ATTN_HEADS, 2, QK_DIM).transpose(1, 0, 2, 3, 4, 5)

    def block(args):
        qi, bi = args
        s = jnp.einsum('bqhmd,bkhmd->bhmqk', qi, k,
                       preferred_element_type=jnp.float32) * scale
        qpos = bi * Q_BLOCK + jnp.arange(Q_BLOCK)
        dist = jnp.abs(qpos[:, None] - kpos[None, :]).astype(jnp.float32)
        p = jax.nn.softmax(s - slopes * dist, axis=-1)
        a = p[:, :, 0] - lam * p[:, :, 1]
        return jnp.einsum('bhqk,bkhd->bqhd', a.astype(v.dtype), v)

    o = lax.map(block, (qb, jnp.arange(nb)))
    o = o.transpose(1, 0, 2, 3, 4).reshape(B, S, ATTN_HEADS, V_DIM)
    o = rmsnorm(o, subln_g) * (1.0 - lam_init)
    return o.reshape(B, S, ATTN_WIDTH)


def centred_depthwise_conv(x, w, b):
    C = x.shape[-1]
    pad_l = CONV_WIDTH // 2
    pad_r = CONV_WIDTH - 1 - pad_l
    y = lax.conv_general_dilated(x, w[:, None, :], window_strides=(1,),
                                 padding=[(pad_l, pad_r)],
                                 dimension_numbers=('NWC', 'WIO', 'NWC'),
                                 feature_group_count=C)
    return y + b


def _lin_combine(left, right):
    a1, b1 = left
    a2, b2 = right
    return a1 * a2, a2 * b1 + b2


def rg_lru(x, w_a, b_a, w_i, b_i, lam, reverse):
    B, S, C = x.shape
    xh = x.reshape(B, S, LRU_HEADS, LRU_HEAD_DIM)
    r = jax.nn.sigmoid(jnp.einsum('bshi,hij->bshj', xh, w_a).reshape(B, S, C) + b_a)
    i = jax.nn.sigmoid(jnp.einsum('bshi,hij->bshj', xh, w_i).reshape(B, S, C) + b_i)
    log_a = LRU_C * r.astype(jnp.float32) * jax.nn.log_sigmoid(lam.astype(jnp.float32))
    a = jnp.exp(log_a)
    u = jnp.sqrt(-jnp.expm1(2.0 * log_a)) * (i * x).astype(jnp.float32)
    _, h = lax.associative_scan(_lin_combine, (a, u), axis=1, reverse=reverse)
    return h.astype(x.dtype)


def hybrid_mixer(h, w_in, w_out, lq1, lk1, lq2, lk2, subln_g, conv_w, conv_b,
                 wa, ba, wi, bi, lru_lam, lam_init):
    B, S, _ = h.shape
    z = h @ w_in
    c1 = Q_COLS
    c2 = c1 + K_COLS
    c3 = c2 + V_COLS
    c4 = c3 + LRU_WIDTH
    q, k, v, xr, yg = jnp.split(z, [c1, c2, c3, c4], axis=-1)
    q = q.reshape(B, S, ATTN_HEADS, 2, QK_DIM)
    k = k.reshape(B, S, ATTN_HEADS, 2, QK_DIM)
    v = v.reshape(B, S, ATTN_HEADS, V_DIM)
    lam = (jnp.exp(jnp.sum(lq1.astype(jnp.float32) * lk1.astype(jnp.float32)))
           - jnp.exp(jnp.sum(lq2.astype(jnp.float32) * lk2.astype(jnp.float32)))
           + lam_init)
    attn = diff_attention(q, k, v, lam, lam_init, subln_g)
    xc = centred_depthwise_conv(xr, conv_w, conv_b)
    h_fwd = rg_lru(xc, wa[0], ba[0], wi[0], bi[0], lru_lam[0], False)
    h_bwd = rg_lru(xc, wa[1], ba[1], wi[1], bi[1], lru_lam[1], True)
    rec = (h_fwd + h_bwd) * jax.nn.gelu(yg, approximate=True)
    return jnp.concatenate([attn, rec], axis=-1) @ w_out


def fourier_mixer(h, w):
    B, S, D = h.shape
    g = h.astype(jnp.float32).reshape(B, S, FOURIER_GROUPS, FOURIER_GROUP_DIM)
    f = jnp.fft.fft2(g, axes=(1, 3), norm='ortho').real
    return f.reshape(B, S, D).astype(h.dtype) @ w


def swiglu(h, wg, wu, wd):
    return (jax.nn.silu(h @ wg) * (h @ wu)) @ wd


def setup_inputs(seed: int = 0) -> dict:
    key = jax.random.key(seed)
    ks = jax.random.split(key, 24)
    f32 = jnp.float32
    D = D_MODEL
    nrm = lambda k, shape, s: jax.random.normal(k, shape, f32) * s
    gain = lambda k, shape: 1.0 + 0.05 * jax.random.normal(k, shape, f32)
    rad2 = jax.random.uniform(ks[20], (N_EVEN, 2, LRU_WIDTH), f32,
                              LRU_MIN_RAD ** 2, LRU_MAX_RAD ** 2)
    a0 = jnp.sqrt(rad2)
    return {
        'x': jax.random.normal(ks[0], (BATCH, SEQ, D), f32),
        'ln_mix_pre': gain(ks[1], (DEPTH, D)),
        'ln_mix_post': gain(ks[2], (DEPTH, D)),
        'ln_ffn_pre': gain(ks[3], (DEPTH, D)),
        'ln_ffn_post': gain(ks[4], (DEPTH, D)),
        'w_in': nrm(ks[5], (N_EVEN, D, IN_COLS), D ** -0.5),
        'w_mix_out': nrm(ks[6], (N_EVEN, MIX_WIDTH, D), MIX_WIDTH ** -0.5),
        'lambda_q1': nrm(ks[7], (N_EVEN, QK_DIM), 0.1),
        'lambda_k1': nrm(ks[8], (N_EVEN, QK_DIM), 0.1),
        'lambda_q2': nrm(ks[9], (N_EVEN, QK_DIM), 0.1),
        'lambda_k2': nrm(ks[10], (N_EVEN, QK_DIM), 0.1),
        'attn_subln': gain(ks[11], (N_EVEN, V_DIM)),
        'conv_w': nrm(ks[12], (N_EVEN, CONV_WIDTH, LRU_WIDTH), CONV_WIDTH ** -0.5),
        'conv_b': nrm(ks[13], (N_EVEN, LRU_WIDTH), 0.01),
        'lru_w_a': nrm(ks[14], (N_EVEN, 2, LRU_HEADS, LRU_HEAD_DIM, LRU_HEAD_DIM), LRU_HEAD_DIM ** -0.5),
        'lru_b_a': nrm(ks[15], (N_EVEN, 2, LRU_WIDTH), 0.01),
        'lru_w_i': nrm(ks[16], (N_EVEN, 2, LRU_HEADS, LRU_HEAD_DIM, LRU_HEAD_DIM), LRU_HEAD_DIM ** -0.5),
        'lru_b_i': nrm(ks[17], (N_EVEN, 2, LRU_WIDTH), 0.01),
        'lru_lambda': jnp.log(a0) - jnp.log1p(-a0),
        'w_fourier_out': nrm(ks[18], (N_ODD, D, D), D ** -0.5),
        'w_ffn_gate': nrm(ks[19], (DEPTH, D, D_FF), D ** -0.5),
        'w_ffn_up': nrm(ks[21], (DEPTH, D, D_FF), D ** -0.5),
        'w_ffn_down': nrm(ks[22], (DEPTH, D_FF, D), D_FF ** -0.5),
    }


def reference(x, ln_mix_pre, ln_mix_post, ln_ffn_pre, ln_ffn_post, w_in, w_mix_out,
              lambda_q1, lambda_k1, lambda_q2, lambda_k2, attn_subln, conv_w, conv_b,
              lru_w_a, lru_b_a, lru_w_i, lru_b_i, lru_lambda, w_fourier_out,
              w_ffn_gate, w_ffn_up, w_ffn_down):
    for l in range(DEPTH):
        hn = rmsnorm(x, ln_mix_pre[l])
        if l % 2 == 0:
            e = l // 2
            lam_init = 0.8 - 0.6 * math.exp(-0.3 * l)
            m = hybrid_mixer(hn, w_in[e], w_mix_out[e], lambda_q1[e], lambda_k1[e],
                             lambda_q2[e], lambda_k2[e], attn_subln[e], conv_w[e], conv_b[e],
                             lru_w_a[e], lru_b_a[e], lru_w_i[e], lru_b_i[e], lru_lambda[e],
                             lam_init)
        else:
            m = fourier_mixer(hn, w_fourier_out[l // 2])
        x = x + rmsnorm(m, ln_mix_post[l])
        hn = rmsnorm(x, ln_ffn_pre[l])
        x = x + rmsnorm(swiglu(hn, w_ffn_gate[l], w_ffn_up[l], w_ffn_down[l]), ln_ffn_post[l])
    return x
```

```python
import math
from contextlib import ExitStack

import numpy as np
import ml_dtypes

import concourse.bass as bass
import concourse.mybir as mybir
from concourse.bass_utils import run_bass_kernel_spmd

F32 = mybir.dt.float32
BF16 = mybir.dt.bfloat16
AF = mybir.ActivationFunctionType
ALU = mybir.AluOpType

D = 1024
DEPTH = 4
NH = 4
QK = 64
DV = 128
LRU_W = 512
IN_COLS = 2560
D_FF = 2816
NF = D_FF // 128
EPS = 1e-6
SLOPES = [2.0 ** (-8.0 * (h + 1) / NH) for h in range(NH)]
VA = DV + 1


class Res:
    __slots__ = ("w", "r")

    def __init__(self):
        self.w = {}
        self.r = {}


def _merge(dst, toks):
    for k, (sem, val) in toks.items():
        cur = dst.get(k)
        if cur is None or cur[1] < val:
            dst[k] = (sem, val)


class Ctx:
    def __init__(self, nc):
        self.nc = nc
        self.es = ExitStack()
        self.eng = {}
        for name, e in (("pe", nc.tensor), ("act", nc.scalar), ("dve", nc.vector),
                        ("pool", nc.gpsimd), ("sp", nc.sync)):
            sem = self.es.enter_context(nc.semaphore("sem_" + name))
            self.eng[name] = {"e": e, "sem": sem, "cnt": 0, "waited": {}, "key": "E" + name}
        self.dq = {}
        for q, n in (("sp", 16), ("pool", 8)):
            pool = []
            for i in range(n):
                sem = self.es.enter_context(nc.semaphore(f"dq_{q}_{i}"))
                pool.append([sem, 0, f"Q{q}{i}"])
            self.dq[q] = {"pool": pool, "idx": 0}
        self.n_inst = 0

    def _wait(self, E, key, sem, val):
        if E["waited"].get(key, 0) >= val:
            return
        E["e"].wait_ge(sem, val)
        E["waited"][key] = val
        self.n_inst += 1

    def _deps(self, E, reads, writes, skip_key=None):
        need = {}
        for r in reads:
            _merge(need, r.w)
        for w in writes:
            _merge(need, w.w)
            _merge(need, w.r)
        for k, (sem, val) in need.items():
            if k == skip_key:
                continue
            self._wait(E, k, sem, val)

    def _commit(self, key, tok, reads, writes):
        for r in reads:
            cur = r.r.get(key)
            if cur is None or cur[1] < tok[1]:
                r.r[key] = tok
        for w in writes:
            w.w = {key: tok}
            w.r = {}

    def op(self, eng, fn, reads=(), writes=()):
        E = self.eng[eng]
        self._deps(E, reads, writes, skip_key=E["key"] if eng == "pe" else None)
        ins = fn()
        E["cnt"] += 1
        ins.then_inc(E["sem"], 1)
        tok = (E["sem"], E["cnt"])
        self._commit(E["key"], tok, reads, writes)
        self.n_inst += 1
        return tok

    def dma(self, q, out, in_, reads=(), writes=()):
        E = self.eng[q]
        Q = self.dq[q]
        slot = Q["pool"][Q["idx"]]
        Q["idx"] = (Q["idx"] + 1) % len(Q["pool"])
        sem, cnt, key = slot
        if cnt > 0:
            self._wait(E, key, sem, cnt)
        self._deps(E, reads, writes)
        E["e"].dma_start(out=out, in_=in_).then_inc(sem, 16)
        slot[1] = cnt + 16
        tok = (sem, cnt + 16)
        self._commit(key, tok, reads, writes)
        self.n_inst += 1
        return tok

    def barrier(self, skip=()):
        toks = {}
        for name, E in self.eng.items():
            if E["cnt"] > 0:
                toks[E["key"]] = (E["sem"], E["cnt"])
        for q, Q in self.dq.items():
            for sem, cnt, key in Q["pool"]:
                if cnt > 0:
                    toks[key] = (sem, cnt)
        for name, E in self.eng.items():
            if name in skip:
                continue
            for k, (sem, val) in toks.items():
                if k == E["key"]:
                    continue
                self._wait(E, k, sem, val)
        for name in ("act", "dve", "pool"):
            E = self.eng[name]
            if name in skip:
                continue
            if E["cnt"] > 0:
                self._wait(E, E["key"], E["sem"], E["cnt"])


class Prog:
    def __init__(self, S, plan):
        self.S = S
        self.NB = S // 128
        self.NT = S // 512
        self.plan = plan
        nc = bass.Bass("TRN2", target_bir_lowering=False)
        self.nc = nc
        self.cx = Ctx(nc)
        self.dram = {}
        self._declare()

    def _dt(self, name, shape, dtype, kind):
        t = self.nc.dram_tensor(name, list(shape), dtype, kind=kind).ap()
        self.dram[name] = t
        return t

    def _declare(self):
        S = self.S
        ne, no = 2, 2
        I = "ExternalInput"
        self._dt("x", [S, D], F32, I)
        for n in ("ln_mix_pre", "ln_mix_post", "ln_ffn_pre", "ln_ffn_post"):
            self._dt(n, [DEPTH, D], F32, I)
        self._dt("w_in", [ne, D, IN_COLS], F32, I)
        self._dt("w_mix_out", [ne, D, D], F32, I)
        for n in ("lambda_q1", "lambda_k1", "lambda_q2", "lambda_k2"):
            self._dt(n, [ne, QK], F32, I)
        self._dt("attn_subln", [ne, DV], F32, I)
        self._dt("conv_w", [ne, 4, LRU_W], F32, I)
        self._dt("conv_b", [ne, LRU_W], F32, I)
        self._dt("lru_w_a", [ne, 2, 4, 128, 128], F32, I)
        self._dt("lru_b_a", [ne, 2, LRU_W], F32, I)
        self._dt("lru_w_i", [ne, 2, 4, 128, 128], F32, I)
        self._dt("lru_b_i", [ne, 2, LRU_W], F32, I)
        self._dt("lru_lambda", [ne, 2, LRU_W], F32, I)
        self._dt("w_fourier_out", [no, D, D], F32, I)
        self._dt("w_ffn_gate", [DEPTH, D, D_FF], F32, I)
        self._dt("w_ffn_up", [DEPTH, D, D_FF], F32, I)
        self._dt("w_ffn_down", [DEPTH, D_FF, D], F32, I)
        self._dt("c_ident", [128, 128], BF16, I)
        self._dt("c_qaug", [NH, 2, 4, S], BF16, I)
        self._dt("c_kaug", [4, S], BF16, I)
        self._dt("c_dist", [128, 4, 512], F32, I)
        self._dt("c_dftc", [S // 512, 128, S // 128, 257], BF16, I)
        self._dt("c_dfts", [S // 512, 128, S // 128, 257], BF16, I)
        self._dt("c_cdft", [128, 3, 2, 256], BF16, I)
        self._dt("out", [S, D], F32, "ExternalOutput")
        self._dt("xs", [S, D], F32, "Internal")
        self._dt("xr_s", [LRU_W, S], F32, "Internal")
        self._dt("yg_s", [LRU_W, S], F32, "Internal")
        self._dt("attn_s", [4, 128, S], BF16, "Internal")
        self.res = {}
        for n in ("x", "out", "xs"):
            self.res[n] = [Res() for _ in range(self.NB)]
        self.res["xr_s"] = [[Res() for _ in range(self.NT)] for _ in range(4)]
        self.res["yg_s"] = [[Res() for _ in range(self.NT)] for _ in range(4)]
        self.res["attn_s"] = [[Res() for _ in range(self.NT)] for _ in range(4)]

    def sb(self, es, name, shape, dtype):
        self._uid = getattr(self, "_uid", 0) + 1
        return es.enter_context(self.nc.sbuf_tensor(f"{name}_{self._uid}", list(shape), dtype))

    def ps(self, es, name, shape, dtype):
        self._uid = getattr(self, "_uid", 0) + 1
        return es.enter_context(self.nc.psum_tensor(f"{name}_{self._uid}", list(shape), dtype))

    def load_bcast(self, q, dst, dst_res, src_row):
        self.cx.dma(q, dst, src_row.partition_broadcast(128), writes=[dst_res])

    def rstd_from_ss(self, ss, rstd, ss_res, rstd_res, n):
        v = self.nc.vector
        cx = self.cx
        w = ss.shape[-1]
        sv = self.rs_scr[:, 0:w]
        st = self.rs_scr[:, 8:8 + w]
        R = self.rs_res
        I32 = mybir.dt.int32
        cx.op("dve", lambda: v.tensor_scalar(out=sv, in0=ss, scalar1=1.0 / n, scalar2=EPS,
                                             op0=ALU.mult, op1=ALU.add),
              reads=[ss_res], writes=[R])
        cx.op("dve", lambda: v.tensor_scalar(out=st.bitcast(I32), in0=sv.bitcast(I32), scalar1=1, scalar2=None,
                                             op0=ALU.logical_shift_right),
              reads=[R], writes=[R])
        cx.op("dve", lambda: v.tensor_scalar(out=rstd.bitcast(I32), in0=st.bitcast(I32), scalar1=-1.0,
                                             scalar2=float(0x5f3759df), op0=ALU.mult, op1=ALU.add),
              reads=[R], writes=[rstd_res])
        for _ in range(3):
            cx.op("dve", lambda: v.tensor_tensor(out=st, in0=rstd, in1=rstd, op=ALU.mult),
                  reads=[rstd_res], writes=[R])
            cx.op("dve", lambda: v.scalar_tensor_tensor(out=st, in0=st, scalar=-0.5, in1=sv, op0=ALU.mult, op1=ALU.mult),
                  reads=[R], writes=[R])
            cx.op("dve", lambda: v.scalar_tensor_tensor(out=rstd, in0=st, scalar=1.5, in1=rstd, op0=ALU.add, op1=ALU.mult),
                  reads=[R, rstd_res], writes=[rstd_res])

    def build(self):
        nc, cx = self.nc, self.cx
        with ExitStack() as es:
            self.ident = self.sb(es, "ident", [128, 128], BF16)
            self.ident_r = Res()
            self.rs_scr = self.sb(es, "rs_scr", [128, 16], F32)
            self.rs_res = Res()
            self.ssn = self.sb(es, "ssn", [128, self.NB], F32)
            self.rstdn = self.sb(es, "rstdn", [128, self.NB], F32)
            self.ssn_r = [Res() for _ in range(self.NT)]
            self.rstdn_r = [Res() for _ in range(self.NT)]
            with nc.named_scope("prepass"):
                self.stats_prepass(self.plan[0][2])
                pool_free = False
                cx.barrier(skip=("pool",) if pool_free else ())
            cx.dma("sp", self.ident[:], self.dram["c_ident"], writes=[self.ident_r])
            for ph in self.plan:
                kind = ph[0]
                if kind == "ffn":
                    with nc.named_scope(f"ffn{ph[1]}"):
                        self.phase_ffn(*ph[1:])
                elif kind == "fnet":
                    with nc.named_scope(f"fnet{ph[1]}"):
                        self.phase_fnet(*ph[1:])
                elif kind == "even":
                    self.phase_even(*ph[1:])
                else:
                    raise ValueError(kind)
                cx.barrier()
        self.cx.es.close()
        return nc

    def prenorm_T(self, xstream, b0, g_bc, g_r, hn, hn_r, ss, ss_r, rstd, rstd_r, pT, pT_r, hnT, hnT_r, nj=4):
        nc, cx = self.nc, self.cx

        def emit_hn(j):
            b = b0 + j
            xj, xj_r = xstream.get(b)
            H, H_r = hn[j % 2], hn_r[j % 2]
            cx.op("dve", lambda: nc.vector.scalar_tensor_tensor(
                out=H[:], in0=xj, scalar=self.rstdn[:, b:b + 1], in1=g_bc[:],
                op0=ALU.mult, op1=ALU.mult),
                reads=[xj_r, self.rstdn_r[b // 4], g_r], writes=[H_r])

        emit_hn(0)
        for j in range(nj):
            H, H_r = hn[j % 2], hn_r[j % 2]
            P, P_r = pT[j % 2], pT_r[j % 2]
            for c in range(8):
                cx.op("pe", lambda c=c, H=H, P=P: nc.tensor.transpose(
                    out=P[:, c * 128:(c + 1) * 128], in_=H[:, c * 128:(c + 1) * 128], identity=self.ident[:]),
                    reads=[H_r, self.ident_r], writes=[P_r])
            if j + 1 < nj:
                emit_hn(j + 1)
            src = P[:].rearrange("p (c t) -> p c t", c=8)
            dst = hnT[:, :, j * 128:(j + 1) * 128]
            cx.op("act", lambda src=src, dst=dst: nc.scalar.copy(out=dst, in_=src), reads=[P_r], writes=[hnT_r])

    def pre_hn(self, xstream, b, g_bc, g_r, H, H_r):
        nc, cx = self.nc, self.cx
        xj, xj_r = xstream.get(b)
        cx.op("dve", lambda: nc.vector.scalar_tensor_tensor(
            out=H[:], in0=xj, scalar=self.rstdn[:, b:b + 1], in1=g_bc[:], op0=ALU.mult, op1=ALU.mult),
            reads=[xj_r, self.rstdn_r[b // 4], g_r], writes=[H_r])

    def pre_tr(self, H, H_r, P, P_r, hnT, hnT_r, j):
        nc, cx = self.nc, self.cx
        for c in range(8):
            cx.op("pe", lambda c=c: nc.tensor.transpose(
                out=P[:, c * 128:(c + 1) * 128], in_=H[:, c * 128:(c + 1) * 128], identity=self.ident[:]),
                reads=[H_r, self.ident_r], writes=[P_r])
        src = P[:].rearrange("p (c t) -> p c t", c=8)
        dst = hnT[:, :, j * 128:(j + 1) * 128]
        cx.op("act", lambda: nc.scalar.copy(out=dst, in_=src), reads=[P_r], writes=[hnT_r])

    def next_stats(self, xj, xj_r, junk, junk_r, b):
        nc, cx = self.nc, self.cx
        g = b // 4
        cx.op("act", lambda: nc.scalar.activation(out=junk[:], in_=xj, func=AF.Square, accum_out=self.ssn[:, b:b + 1]),
              reads=[xj_r], writes=[junk_r, self.ssn_r[g]])
        if b % 4 == 3:
            self.rstd_from_ss(self.ssn[:, 4 * g:4 * g + 4], self.rstdn[:, 4 * g:4 * g + 4], self.ssn_r[g], self.rstdn_r[g], D)

    def postnorm_residual(self, ps_halves, ps_res, xj, xj_r, g_bc, g_r, junk, junk_r, ss2, ss2_r, rstd1, rstd1_r,
                          tmp, tmp_r, b=None, want_next=True):
        nc, cx = self.nc, self.cx
        for hf in range(2):
            cx.op("dve", lambda hf=hf: nc.vector.tensor_copy(out=tmp[:, hf * 512:(hf + 1) * 512], in_=ps_halves[hf]),
                  reads=[ps_res[hf]], writes=[tmp_r])
        cx.op("act", lambda: nc.scalar.activation(out=junk[:], in_=tmp[:], func=AF.Square, accum_out=ss2[:, 2:3]),
              reads=[tmp_r], writes=[junk_r, ss2_r])
        self.rstd_from_ss(ss2[:, 2:3], rstd1[:], ss2_r, rstd1_r, D)
        cx.op("dve", lambda: nc.vector.scalar_tensor_tensor(
            out=tmp[:], in0=tmp[:], scalar=rstd1[:], in1=g_bc[:], op0=ALU.mult, op1=ALU.mult),
            reads=[tmp_r, rstd1_r, g_r], writes=[tmp_r])
        cx.op("dve", lambda: nc.vector.tensor_tensor(out=xj, in0=tmp[:], in1=xj, op=ALU.add),
              reads=[tmp_r, xj_r], writes=[xj_r])
        if want_next and b is not None:
            self.next_stats(xj, xj_r, junk, junk_r, b)

    def proj_post_stage(self, es, get_lhs, w, w_r, g2, g2_r, src, dst, want_next):
        nc, cx = self.nc, self.cx
        NB = self.NB
        R = Res
        dstv = self.dram[dst].rearrange("(b p) d -> b p d", p=128)
        xb = XStream(self, es, "xb", src, 4)
        Ys = [self.sb(es, f"Y{i}", [128, 4, D], F32) for i in range(2)]
        Ys_r = [[R() for _ in range(4)] for _ in range(2)]
        junk = self.sb(es, "junk", [128, D], BF16)
        junk_r = R()
        junk2 = self.sb(es, "junk2", [128, D], BF16)
        junk2_r = R()
        ssx = [self.sb(es, f"ssx{i}", [128, 8], F32) for i in range(2)]
        ssx_r = [R(), R()]
        rsx = self.sb(es, "rsx", [128, 8], F32)
        rsx_r = R()
        py = [self.ps(es, f"py{i}", [128, 512], F32) for i in range(4)]
        py_r = [R() for _ in range(4)]
        NG = NB // 4

        def front(g):
            Y, Y_r = Ys[g % 2], Ys_r[g % 2]
            for jj in range(4):
                b = 4 * g + jj
                pr = [py[(b % 2) * 2], py[(b % 2) * 2 + 1]]
                pr_r = [py_r[(b % 2) * 2], py_r[(b % 2) * 2 + 1]]
                lhs = get_lhs(b)
                for hf in range(2):
                    for c in range(8):
                        la, lr = lhs[c]
                        cx.op("pe", lambda la=la, c=c, hf=hf, pr=pr: nc.tensor.matmul(
                            pr[hf][:], lhsT=la, rhs=w[:, c, hf * 512:(hf + 1) * 512], start=(c == 0), stop=(c == 7)),
                            reads=[lr, w_r], writes=[pr_r[hf]])
                for hf in range(2):
                    cx.op("act", lambda hf=hf, jj=jj, pr=pr, Y=Y: nc.scalar.copy(
                        out=Y[:, jj, hf * 512:(hf + 1) * 512], in_=pr[hf][:]),
                        reads=[pr_r[hf]], writes=[Y_r[jj]])
                cx.op("act", lambda jj=jj, Y=Y, g=g: nc.scalar.activation(
                    out=junk[:], in_=Y[:, jj, :], func=AF.Square, accum_out=ssx[g % 2][:, jj:jj + 1]),
                    reads=[Y_r[jj]], writes=[junk_r, ssx_r[g % 2]])

        def back(g):
            Y, Y_r = Ys[g % 2], Ys_r[g % 2]
            wdt = 8 if (want_next and g > 0) else 4
            self.rstd_from_ss(ssx[g % 2][:, 0:wdt], rsx[:, 0:wdt], ssx_r[g % 2], rsx_r, D)
            if wdt == 8:
                cx.op("dve", lambda: nc.vector.tensor_copy(out=self.rstdn[:, 4 * (g - 1):4 * g], in_=rsx[:, 4:8]),
                      reads=[rsx_r], writes=[self.rstdn_r[g - 1]])
            for jj in range(4):
                b = 4 * g + jj
                xj, xj_r = xb.get(b)
                cx.op("dve", lambda jj=jj, Y=Y: nc.vector.scalar_tensor_tensor(
                    out=Y[:, jj, :], in0=Y[:, jj, :], scalar=rsx[:, jj:jj + 1], in1=g2[:], op0=ALU.mult, op1=ALU.mult),
                    reads=[Y_r[jj], rsx_r, g2_r], writes=[Y_r[jj]])
                cx.op("dve", lambda jj=jj, xj=xj, Y=Y: nc.vector.tensor_tensor(out=xj, in0=Y[:, jj, :], in1=xj, op=ALU.add),
                      reads=[Y_r[jj], xj_r], writes=[xj_r])
                if want_next:
                    cx.op("act", lambda jj=jj, xj=xj, g=g: nc.scalar.activation(
                        out=junk2[:], in_=xj, func=AF.Square, accum_out=ssx[(g + 1) % 2][:, 4 + jj:5 + jj]),
                        reads=[xj_r], writes=[junk2_r, ssx_r[(g + 1) % 2]])
                cx.dma("sp", dstv[b], xj, reads=[xj_r], writes=[self.res[dst][b]])

        front(0)
        for g in range(NG):
            if g + 1 < NG:
                front(g + 1)
            back(g)
        if want_next:
            self.rstd_from_ss(ssx[NG % 2][:, 4:8], rsx[:, 4:8], ssx_r[NG % 2], rsx_r, D)
            cx.op("dve", lambda: nc.vector.tensor_copy(out=self.rstdn[:, 4 * (NG - 1):4 * NG], in_=rsx[:, 4:8]),
                  reads=[rsx_r], writes=[self.rstdn_r[NG - 1]])

    def stats_prepass(self, src):
        with ExitStack() as es:
            xp = XStream(self, es, "xp", src, 4)
            junk = self.sb(es, "junkp", [128, D], BF16)
            junk_r = Res()
            for b in range(self.NB):
                xj, xj_r = xp.get(b)
                self.next_stats(xj, xj_r, junk, junk_r, b)

    def phase_ffn(self, l, src, dst):
        nc, cx = self.nc, self.cx
        S, NT, NB = self.S, self.NT, self.NB
        dstv = self.dram[dst].rearrange("(b p) d -> b p d", p=128)
        wg_d = self.dram["w_ffn_gate"][l].rearrange("(c p) n -> p c n", p=128)
        wu_d = self.dram["w_ffn_up"][l].rearrange("(c p) n -> p c n", p=128)
        wd_d = self.dram["w_ffn_down"][l].rearrange("(f p) n -> p f n", p=128)
        R = Res
        with ExitStack() as es:
            wg = self.sb(es, "wg", [128, 8, D_FF], BF16)
            wu = self.sb(es, "wu", [128, 8, D_FF], BF16)
            wd = self.sb(es, "wd", [128, NF, D], BF16)
            g1 = self.sb(es, "g1", [128, D], F32)
            g2 = self.sb(es, "g2", [128, D], F32)
            xa = XStream(self, es, "xa", src, 2)
            xb = XStream(self, es, "xb", src, 2)
            hn = [self.sb(es, f"hn{i}", [128, D], BF16) for i in range(2)]
            hnTs = [self.sb(es, f"hnT{i}", [128, 8, 512], BF16) for i in range(2)]
            hT = self.sb(es, "hT", [128, NF, 512], BF16)
            sg = [self.sb(es, "sg0", [128, 512], F32)] * 2
            tmp = self.sb(es, "tmp", [128, D], F32)
            junk = self.sb(es, "junk", [128, D], BF16)
            ss2 = self.sb(es, "ss2", [128, 4], F32)
            rstd1 = self.sb(es, "rstd1", [128, 1], F32)
            pT = [self.ps(es, f"pT{i}", [128, D], BF16) for i in range(2)]
            pg = [self.ps(es, f"pg{i}", [128, 512], F32) for i in range(2)]
            pu = [self.ps(es, f"pu{i}", [128, 512], F32) for i in range(2)]
            py = [self.ps(es, f"py{i}", [128, 512], F32) for i in range(2)]
            wg_r = [R() for _ in range(6)]
            wu_r = [R() for _ in range(6)]
            wd_r = [R() for _ in range(NF)]
            g1_r, g2_r = R(), R()
            tmp_r, junk_r, ss2_r, rstd1_r = (R() for _ in range(4))
            hnTs_r = [R(), R()]
            hn_r = [R(), R()]
            hT_r = [R() for _ in range(NF)]
            sg_r = [R()] * 2
            pT_r = [R(), R()]
            pg_r = [R(), R()]
            pu_r = [R(), R()]
            py_r = [R(), R()]

            self.load_bcast("sp", g1[:], g1_r, self.dram["ln_ffn_pre"][l])
            self.load_bcast("sp", g2[:], g2_r, self.dram["ln_ffn_post"][l])
            xa.get(0)
            for gi in range(6):
                c0 = gi * 512
                c1 = min(D_FF, c0 + 512)
                cx.dma("pool", wg[:, :, c0:c1], wg_d[:, :, c0:c1], writes=[wg_r[gi]])
                cx.dma("pool", wu[:, :, c0:c1], wu_d[:, :, c0:c1], writes=[wu_r[gi]])
            for f in range(NF):
                cx.dma("pool", wd[:, f, :], wd_d[:, f, :], writes=[wd_r[f]])

            for j in range(4):
                self.pre_hn(xa, j, g1, g1_r, hn[j % 2], hn_r[j % 2])
                self.pre_tr(hn[j % 2], hn_r[j % 2], pT[j % 2], pT_r[j % 2], hnTs[0], hnTs_r[0], j)
            for t in range(NT):
                hnT, hnT_r = hnTs[t % 2], hnTs_r[t % 2]
                for f in range(NF):
                    gi = f // 4
                    G, U = pg[f % 2], pu[f % 2]
                    G_r, U_r = pg_r[f % 2], pu_r[f % 2]
                    for c in range(8):
                        cx.op("pe", lambda c=c, G=G: nc.tensor.matmul(G[:], lhsT=wg[:, c, f * 128:(f + 1) * 128],
                                                                       rhs=hnT[:, c, :], start=(c == 0), stop=(c == 7)),
                              reads=[wg_r[gi], hnT_r], writes=[G_r])
                    for c in range(8):
                        cx.op("pe", lambda c=c, U=U: nc.tensor.matmul(U[:], lhsT=wu[:, c, f * 128:(f + 1) * 128],
                                                                       rhs=hnT[:, c, :], start=(c == 0), stop=(c == 7)),
                              reads=[wu_r[gi], hnT_r], writes=[U_r])
                    sgt, sgt_r = sg[f % 2], sg_r[f % 2]
                    cx.op("act", lambda G=G, sgt=sgt: nc.scalar.activation(out=sgt[:], in_=G[:], func=AF.Silu),
                          reads=[G_r], writes=[sgt_r])
                    cx.op("dve", lambda U=U, sgt=sgt, f=f: nc.vector.tensor_tensor(out=hT[:, f, :], in0=sgt[:], in1=U[:],
                                                                                     op=ALU.mult),
                          reads=[sgt_r, U_r], writes=[hT_r[f]])
                    if t + 1 < NT and f >= 4 and f < 20:
                        j = (f - 4) // 4
                        if (f - 4) % 4 == 0:
                            self.pre_hn(xa, 4 * (t + 1) + j, g1, g1_r, hn[j % 2], hn_r[j % 2])
                        elif (f - 4) % 4 == 2:
                            self.pre_tr(hn[j % 2], hn_r[j % 2], pT[j % 2], pT_r[j % 2],
                                        hnTs[(t + 1) % 2], hnTs_r[(t + 1) % 2], j)
                for j in range(4):
                    b = 4 * t + j
                    for hf in range(2):
                        for f in range(NF):
                            cx.op("pe", lambda f=f, hf=hf, j=j: nc.tensor.matmul(
                                py[hf][:], lhsT=hT[:, f, j * 128:(j + 1) * 128], rhs=wd[:, f, hf * 512:(hf + 1) * 512],
                                start=(f == 0), stop=(f == NF - 1)),
                                reads=[hT_r[f], wd_r[f]], writes=[py_r[hf]])
                    xj, xj_r = xb.get(b)
                    self.postnorm_residual([py[0][:], py[1][:]], py_r, xj, xj_r, g2, g2_r, junk, junk_r,
                                           ss2, ss2_r, rstd1, rstd1_r, tmp, tmp_r, b=b, want_next=(dst != "out"))
                    cx.dma("sp", dstv[b], xj, reads=[xj_r], writes=[self.res[dst][b]])


    def phase_fnet(self, l, src, dst):
        nc, cx = self.nc, self.cx
        S, NB = self.S, self.NB
        NTH = S // 512
        o = l // 2
        wf_d = self.dram["w_fourier_out"][o].rearrange("(c p) n -> p c n", p=128)
        R = Res
        sc1 = 1.0 / math.sqrt(S)
        sc2 = 1.0 / 16.0
        with ExitStack() as es0:
            FT = self.sb(es0, "FTall", [128, 8, S], BF16)
            FT_r = [R() for _ in range(8)]
            with ExitStack() as es:
                hnA = self.sb(es, "hnA", [128, NB, D], BF16)
                hnA_r = [R() for _ in range(NB)]
                cd = self.sb(es, "cd", [128, 3, 2, 256], BF16)
                cd_r = R()
                g1 = self.sb(es, "g1", [128, D], F32)
                g1_r = R()
                xa = XStream(self, es, "xa", src, 4)
                tabs = [self.sb(es, f"tab{i}", [128, NB, 257], BF16) for i in range(3)]
                tab_r = [R() for _ in range(3)]
                YT = self.sb(es, "YT", [128, 2, 8, 258], BF16)
                YT_r = [[R() for _ in range(8)] for _ in range(2)]
                pY = [self.ps(es, f"pY{i}", [128, 512], F32) for i in range(4)]
                pY_r = [R() for _ in range(4)]
                pF = [self.ps(es, f"pF{i}", [128, 512], F32) for i in range(4)]
                pF_r = [R() for _ in range(4)]

                self.load_bcast("sp", g1[:], g1_r, self.dram["ln_mix_pre"][l])
                cx.dma("sp", cd[:], self.dram["c_cdft"], writes=[cd_r])
                xa.get(0)
                tabsrc = (self.dram["c_dftc"], self.dram["c_dfts"])

                def load_tab(u):
                    if u < 2 * NTH:
                        cx.dma("sp", tabs[u % 3][:], tabsrc[u % 2][u // 2], writes=[tab_r[u % 3]])

                load_tab(0)
                load_tab(1)
                for b in range(NB):
                    xj, xj_r = xa.get(b)
                    cx.op("dve", lambda xj=xj, b=b: nc.vector.scalar_tensor_tensor(
                        out=hnA[:, b, :], in0=xj, scalar=self.rstdn[:, b:b + 1], in1=g1[:], op0=ALU.mult, op1=ALU.mult),
                        reads=[xj_r, self.rstdn_r[b // 4], g1_r], writes=[hnA_r[b]])
                ev = 0
                for st in range(NTH):
                    w = 257 if st == NTH - 1 else 256
                    if st == 0 and NTH > 1:
                        banks = pY + pF
                        banks_r = pY_r + pF_r
                        load_tab(2)
                        for b in range(NB):
                            for a16 in range(16):
                                c, cs = a16 // 2, a16 % 2
                                bk, bk_r = banks[a16 // 2], banks_r[a16 // 2]
                                off = (a16 % 2) * 256
                                cx.op("pe", lambda b=b, c=c, cs=cs, bk=bk, off=off, a16=a16: nc.tensor.matmul(
                                    bk[:, off:off + 256], lhsT=hnA[:, b, c * 128:(c + 1) * 128], rhs=tabs[cs][:, b, 0:256],
                                    start=(b == 0 and a16 % 2 == 0), stop=(b == NB - 1), skip_group_check=True),
                                    reads=[hnA_r[b], tab_r[cs]], writes=[bk_r])
                        load_tab(3)
                        for a16 in range(16):
                            c, cs = a16 // 2, a16 % 2
                            bk, bk_r = banks[a16 // 2], banks_r[a16 // 2]
                            off = (a16 % 2) * 256
                            if (a16 // 2) % 2 == 0:
                                cx.op("act", lambda bk=bk, off=off, cs=cs, c=c: nc.scalar.activation(
                                    out=YT[:, cs, c, 0:256], in_=bk[:, off:off + 256], func=AF.Copy, scale=sc1),
                                    reads=[bk_r], writes=[YT_r[cs][c]])
                            else:
                                cx.op("dve", lambda bk=bk, off=off, cs=cs, c=c: nc.vector.tensor_scalar(
                                    out=YT[:, cs, c, 0:256], in0=bk[:, off:off + 256], scalar1=sc1, scalar2=None, op0=ALU.mult),
                                    reads=[bk_r], writes=[YT_r[cs][c]])
                    for cs in range(2):
                        if st == 0 and NTH > 1:
                            break
                        u = 2 * st + cs
                        i = u % 3
                        load_tab(u + 2)
                        for c in range(8):
                            P, P_r = pY[(c % 2) * 2 + cs], pY_r[(c % 2) * 2 + cs]
                            for b in range(NB):
                                cx.op("pe", lambda b=b, c=c, i=i, P=P: nc.tensor.matmul(
                                    P[:, 0:w], lhsT=hnA[:, b, c * 128:(c + 1) * 128], rhs=tabs[i][:, b, 0:w],
                                    start=(b == 0), stop=(b == NB - 1)),
                                    reads=[hnA_r[b], tab_r[i]], writes=[P_r])
                            if ev % 2 == 0:
                                cx.op("act", lambda P=P, cs=cs, c=c: nc.scalar.activation(
                                    out=YT[:, cs, c, 0:w], in_=P[:, 0:w], func=AF.Copy, scale=sc1),
                                    reads=[P_r], writes=[YT_r[cs][c]])
                            else:
                                cx.op("dve", lambda P=P, cs=cs, c=c: nc.vector.tensor_scalar(
                                    out=YT[:, cs, c, 0:w], in0=P[:, 0:w], scalar1=sc1, scalar2=None, op0=ALU.mult),
                                    reads=[P_r], writes=[YT_r[cs][c]])
                            ev += 1
                    for c2 in range(8):
                        g, half = c2 // 2, c2 % 2
                        for mir in range(2):
                            P, P_r = pF[(c2 % 2) * 2 + mir], pF_r[(c2 % 2) * 2 + mir]
                            k = 0
                            for cs in range(2):
                                ti = 0 if cs == 0 else (1 + mir)
                                for dh in range(2):
                                    cx.op("pe", lambda cs=cs, dh=dh, P=P, ti=ti, k=k: nc.tensor.matmul(
                                        P[:, 0:w], lhsT=cd[:, ti, dh, half * 128:(half + 1) * 128],
                                        rhs=YT[:, cs, 2 * g + dh, 0:w], start=(k == 0), stop=(k == 3)),
                                        reads=[cd_r, YT_r[cs][2 * g + dh]], writes=[P_r])
                                    k += 1
                            if mir == 0:
                                lo = st * 256
                                cx.op("act", lambda P=P, c2=c2, lo=lo: nc.scalar.activation(
                                    out=FT[:, c2, lo:lo + w], in_=P[:, 0:w], func=AF.Copy, scale=sc2),
                                    reads=[P_r], writes=[FT_r[c2]])
                            else:
                                i0 = 1 if st == 0 else 0
                                n = 256 - i0
                                t_hi = S - 256 * st - i0
                                dsto = FT[:, c2, t_hi - n + 1:t_hi + 1]
                                cx.op("dve", lambda P=P, dsto=dsto, i0=i0, n=n: nc.vector.tensor_scalar(
                                    out=dsto[:, ::-1], in0=P[:, i0:i0 + n], scalar1=sc2, scalar2=None, op0=ALU.mult),
                                    reads=[P_r], writes=[FT_r[c2]])
                cx.barrier()
            with ExitStack() as es:
                wf = self.sb(es, "wf", [128, 8, D], BF16)
                wf_r = R()
                g2 = self.sb(es, "g2", [128, D], F32)
                g2_r = R()
                cx.dma("pool", wf[:], wf_d, writes=[wf_r])
                self.load_bcast("sp", g2[:], g2_r, self.dram["ln_mix_post"][l])

                def get_lhs(b):
                    return [(FT[:, c, b * 128:(b + 1) * 128], FT_r[c]) for c in range(8)]

                self.proj_post_stage(es, get_lhs, wf, wf_r, g2, g2_r, src, dst, want_next=(dst != "out"))

    def phase_even(self, l, src, dst):
        cx = self.cx
        S = self.S
        with ExitStack() as es1:
            qT = self.sb(es1, "qT", [128, 4, S], BF16)
            kaug = [self.sb(es1, f"kaug{m}", [128, 4, S], BF16) for m in range(2)]
            vaug = self.sb(es1, "vaug", [128, self.NB, 4, VA], BF16)
            qT_r = [Res() for _ in range(self.NT)]
            kaug_r = [Res() for _ in range(self.NT)]
            v_r = [Res() for _ in range(self.NB)]
            with self.nc.named_scope(f"even_proj{l}"):
                self.even_proj(l, src, qT, qT_r, kaug, kaug_r, vaug, v_r)
                cx.barrier()
            with self.nc.named_scope(f"even_attn{l}"):
                self.even_attn(l, qT, qT_r, kaug, kaug_r, vaug, v_r)
                cx.barrier()
        with ExitStack() as es2:
            recT = self.sb(es2, "recT", [128, 4, S], BF16)
            recT_r = [Res() for _ in range(4)]
            with self.nc.named_scope(f"even_lru{l}"):
                self.even_lru(l, recT, recT_r)
                cx.barrier()
            with self.nc.named_scope(f"even_out{l}"):
                self.even_out(l, src, dst, recT, recT_r)
                cx.barrier()

    def even_proj(self, l, src, qT, qT_r, kaug, kaug_r, vaug, v_r):
        nc, cx = self.nc, self.cx
        S, NT, NB = self.S, self.NT, self.NB
        e = l // 2
        R = Res
        win_d = self.dram["w_in"][e].rearrange("(c p) n -> p c n", p=128)
        with ExitStack() as es:
            win = self.sb(es, "win", [128, 8, IN_COLS], BF16)
            win_r = [R() for _ in range(5)]
            g1 = self.sb(es, "g1", [128, D], F32)
            g1_r = R()
            xa = XStream(self, es, "xa", src, 2)
            hn = [self.sb(es, f"hn{i}", [128, D], BF16) for i in range(2)]
            hn_r = [R(), R()]
            hnTs = [self.sb(es, f"hnT{i}", [128, 8, 512], BF16) for i in range(2)]
            hnTs_r = [R(), R()]
            stg = [self.sb(es, f"stg{i}", [128, 512], F32) for i in range(3)]
            stg_r = [R() for _ in range(3)]
            stg_i = [0]
            pT = [self.ps(es, f"pT{i}", [128, D], BF16) for i in range(2)]
            pT_r = [R(), R()]
            pp = [self.ps(es, f"pp{i}", [128, 512], F32) for i in range(6)]
            pp_r = [R() for _ in range(6)]

            self.load_bcast("sp", g1[:], g1_r, self.dram["ln_mix_pre"][l])
            xa.get(0)
            for gi in range(5):
                cx.dma("pool", win[:, :, gi * 512:(gi + 1) * 512], win_d[:, :, gi * 512:(gi + 1) * 512],
                       writes=[win_r[gi]])
            cx.op("dve", lambda: nc.vector.memset(vaug[:, :, :, DV:VA], 1.0), writes=v_r)
            cx.op("pool", lambda: nc.gpsimd.memset(kaug[0][64:128, :, :], 0.0), writes=kaug_r)
            cx.op("pool", lambda: nc.gpsimd.memset(kaug[1][0:64, :, :], 0.0), writes=kaug_r)
            for h in range(NH):
                cx.dma("pool", kaug[0][64:68, h, :], self.dram["c_kaug"], writes=kaug_r)
                cx.dma("pool", kaug[1][0:4, h, :], self.dram["c_kaug"], writes=kaug_r)
            k = 0
            for j in range(4):
                self.pre_hn(xa, j, g1, g1_r, hn[j % 2], hn_r[j % 2])
                self.pre_tr(hn[j % 2], hn_r[j % 2], pT[j % 2], pT_r[j % 2], hnTs[0], hnTs_r[0], j)
            gcount = [0]

            def interleave(t):
                gi_ = gcount[0]
                gcount[0] += 1
                if t + 1 >= NT or gi_ < 2 or gi_ >= 18:
                    return
                j = (gi_ - 2) // 4
                if (gi_ - 2) % 4 == 0:
                    self.pre_hn(xa, 4 * (t + 1) + j, g1, g1_r, hn[j % 2], hn_r[j % 2])
                elif (gi_ - 2) % 4 == 2:
                    self.pre_tr(hn[j % 2], hn_r[j % 2], pT[j % 2], pT_r[j % 2], hnTs[(t + 1) % 2], hnTs_r[(t + 1) % 2], j)

            for t in range(NT):
                hnT, hnT_r = hnTs[t % 2], hnTs_r[t % 2]
                gcount[0] = 0
                tsl = slice(t * 512, (t + 1) * 512)
                for grp in (0, 1, 3, 4):
                    for m in range(4):
                        col = grp * 512 + m * 128
                        P, P_r = pp[k % 6], pp_r[k % 6]
                        for c in range(8):
                            cx.op("pe", lambda c=c, P=P, col=col: nc.tensor.matmul(
                                P[:], lhsT=win[:, c, col:col + 128], rhs=hnT[:, c, :], start=(c == 0), stop=(c == 7)),
                                reads=[win_r[grp], hnT_r], writes=[P_r])
                        if grp == 1:
                            if k % 2 == 0:
                                cx.op("act", lambda P=P, m=m: nc.scalar.copy(out=kaug[0][0:64, m, tsl], in_=P[0:64, :]),
                                      reads=[P_r], writes=[kaug_r[t]])
                                cx.op("act", lambda P=P, m=m: nc.scalar.copy(out=kaug[1][64:128, m, tsl], in_=P[64:128, :]),
                                      reads=[P_r], writes=[kaug_r[t]])
                            else:
                                cx.op("dve", lambda P=P, m=m: nc.vector.tensor_copy(out=kaug[0][0:64, m, tsl], in_=P[0:64, :]),
                                      reads=[P_r], writes=[kaug_r[t]])
                                cx.op("dve", lambda P=P, m=m: nc.vector.tensor_copy(out=kaug[1][64:128, m, tsl], in_=P[64:128, :]),
                                      reads=[P_r], writes=[kaug_r[t]])
                            k += 1
                            interleave(t)
                            continue
                        if grp == 0:
                            dst_ap, dst_r = qT[:, m, tsl], qT_r[t]
                        else:
                            si = stg_i[0] % 3
                            stg_i[0] += 1
                            dst_ap, dst_r = stg[si][:], stg_r[si]
                        if k % 2 == 0:
                            cx.op("act", lambda P=P, dst_ap=dst_ap: nc.scalar.copy(out=dst_ap, in_=P[:]),
                                  reads=[P_r], writes=[dst_r])
                        else:
                            cx.op("dve", lambda P=P, dst_ap=dst_ap: nc.vector.tensor_copy(out=dst_ap, in_=P[:]),
                                  reads=[P_r], writes=[dst_r])
                        if grp >= 3:
                            nm = "xr_s" if grp == 3 else "yg_s"
                            cx.dma("sp", self.dram[nm][m * 128:(m + 1) * 128, tsl], dst_ap,
                                   reads=[dst_r], writes=[self.res[nm][m][t]])
                        k += 1
                        interleave(t)
                for j in range(4):
                    b = 4 * t + j
                    P, P_r = pp[k % 6], pp_r[k % 6]
                    for c in range(8):
                        cx.op("pe", lambda c=c, P=P, j=j: nc.tensor.matmul(
                            P[:], lhsT=hnT[:, c, j * 128:(j + 1) * 128], rhs=win[:, c, 1024:1536],
                            start=(c == 0), stop=(c == 7)),
                            reads=[win_r[2], hnT_r], writes=[P_r])
                    srcv = P[:].rearrange("p (h d) -> p h d", h=4)
                    if k % 2 == 0:
                        cx.op("act", lambda srcv=srcv, b=b: nc.scalar.copy(out=vaug[:, b, :, 0:DV], in_=srcv),
                              reads=[P_r], writes=[v_r[b]])
                    else:
                        cx.op("dve", lambda srcv=srcv, b=b: nc.vector.tensor_copy(out=vaug[:, b, :, 0:DV], in_=srcv),
                              reads=[P_r], writes=[v_r[b]])
                    k += 1
                    interleave(t)

    def even_attn(self, l, qT, qT_r, kaug, kaug_r, vaug, v_r):
        nc, cx = self.nc, self.cx
        S, NT, NB = self.S, self.NT, self.NB
        e = l // 2
        lam_init = 0.8 - 0.6 * math.exp(-0.3 * l)
        R = Res
        NACC = 8 * VA
        with ExitStack() as es:
            pS2 = [self.ps(es, f"pS{i}", [128, 2, 512], F32) for i in range(2)]
            pS2_r = [R(), R()]
            pA = [self.ps(es, f"pA{i}", [128, 512], F32) for i in range(3)]
            pA_r = [R() for _ in range(3)]
            pTr = self.ps(es, "pTr", [128, D], BF16)
            pTr_r = R()
            dist = self.sb(es, "dist", [128, 4, 512], F32)
            dist_r = R()
            lqk = self.sb(es, "lqk", [128, 4, QK], F32)
            lqk_r = R()
            lsm = self.sb(es, "lsm", [128, 8], F32)
            lsm_r = R()
            lam = self.sb(es, "lam", [128, 1], F32)
            lam_r = R()
            gsub = self.sb(es, "gsub", [128, DV], F32)
            gsub_r = R()
            qz = [self.sb(es, f"qz{i}", [128, 2, 2, 512], BF16) for i in range(2)]
            qzd_r = [R(), R()]
            qza_r = [R(), R()]
            PT2 = [self.sb(es, f"PT{i}", [128, 2, 512], BF16) for i in range(3)]
            PT2_r = [R() for _ in range(3)]
            dtmp = [self.sb(es, f"dtmp{i}", [128, 2, 512], F32) for i in range(2)]
            dtmp_r = [R(), R()]
            acc = [self.sb(es, f"acc{i}", [128, NACC], F32) for i in range(2)]
            acc_r = [R(), R()]
            rl = self.sb(es, "rl", [128, 2, 4], F32)
            rl_r = R()
            c1 = self.sb(es, "c1", [128, 4], F32)
            c1_r = R()
            Ot = self.sb(es, "Ot", [128, 4, DV], F32)
            Ot_r = [R() for _ in range(4)]
            tO = self.sb(es, "tO", [128, DV], F32)
            tO_r = R()
            jk = self.sb(es, "jk", [128, DV], F32)
            jk_r = R()
            ssa = self.sb(es, "ssa", [128, 4], F32)
            ssa_r = R()
            rsa = self.sb(es, "rsa", [128, 4], F32)
            rsa_r = R()
            ab = self.sb(es, "ab", [128, 4, DV], BF16)
            ab_r = [R() for _ in range(4)]
            ats = [self.sb(es, f"ats{i}", [128, 512], BF16) for i in range(2)]
            ats_r = [R(), R()]

            cx.dma("sp", dist[:], self.dram["c_dist"], writes=[dist_r])
            for i, nm in enumerate(("lambda_q1", "lambda_k1", "lambda_q2", "lambda_k2")):
                self.load_bcast("sp", lqk[:, i, :], lqk_r, self.dram[nm][e])
            self.load_bcast("sp", gsub[:], gsub_r, self.dram["attn_subln"][e])
            for i in range(2):
                cx.op("pool", lambda i=i: nc.gpsimd.memset(qz[i][:], 0.0), writes=[qzd_r[i], qza_r[i]])
            for i in range(2):
                cx.op("dve", lambda i=i: nc.vector.scalar_tensor_tensor(
                    out=lqk[:, 2 * i, :], in0=lqk[:, 2 * i, :], scalar=1.0, in1=lqk[:, 2 * i + 1, :],
                    op0=ALU.mult, op1=ALU.mult, accum_out=lsm[:, i:i + 1]),
                    reads=[lqk_r], writes=[lqk_r, lsm_r])
            cx.op("act", lambda: nc.scalar.activation(out=lsm[:, 2:4], in_=lsm[:, 0:2], func=AF.Exp),
                  reads=[lsm_r], writes=[lsm_r])
            cx.op("dve", lambda: nc.vector.tensor_tensor(out=lsm[:, 4:5], in0=lsm[:, 2:3], in1=lsm[:, 3:4], op=ALU.subtract),
                  reads=[lsm_r], writes=[lsm_r])
            cx.op("dve", lambda: nc.vector.tensor_scalar(out=lam[:], in0=lsm[:, 4:5], scalar1=lam_init, scalar2=None,
                                                         op0=ALU.add),
                  reads=[lsm_r], writes=[lam_r])
            cx.op("dve", lambda: nc.vector.tensor_scalar(out=gsub[:], in0=gsub[:], scalar1=(1.0 - lam_init), scalar2=None,
                                                         op0=ALU.mult),
                  reads=[gsub_r], writes=[gsub_r])

            pairs = [(h, j) for h in range(NH) for j in range(NT)]

            def build_qz(pi):
                h, j = pairs[pi]
                Z = qz[pi % 2]
                tsl = slice(j * 512, (j + 1) * 512)
                for v in range(2):
                    cx.dma("sp", Z[0:64, 0, v, :], qT[0:64, h, tsl], reads=[qT_r[j]], writes=[qzd_r[pi % 2]])
                    cx.dma("sp", Z[64:128, 1, v, :], qT[64:128, h, tsl], reads=[qT_r[j]], writes=[qzd_r[pi % 2]])
                    cx.dma("sp", Z[64:68, 0, v, :], self.dram["c_qaug"][h, v, :, tsl], writes=[qza_r[pi % 2]])
                    cx.dma("sp", Z[0:4, 1, v, :], self.dram["c_qaug"][h, v, :, tsl], writes=[qza_r[pi % 2]])

            steps = []
            for pi, (h, j) in enumerate(pairs):
                order = [i for i in range(NB) if not (4 * j <= i < 4 * j + 4)] + list(range(4 * j, 4 * j + 4))
                for n, i in enumerate(order):
                    steps.append((pi, i, n == 0, n == NB - 1))

            def emit_qk(s):
                pi, i, first, last = steps[s]
                h, j = pairs[pi]
                Z = qz[pi % 2]
                v = 1 if i >= 4 * j + 4 else 0
                P = pS2[s % 2]
                for m in range(2):
                    cx.op("pe", lambda m=m: nc.tensor.matmul(
                        P[:, m, :], lhsT=kaug[m][:, h, i * 128:(i + 1) * 128], rhs=Z[:, m, v, :], start=True, stop=True),
                        reads=[kaug_r[i // 4], qzd_r[pi % 2], qza_r[pi % 2]], writes=[pS2_r[s % 2]])

            def emit_exp(s):
                pi, i, first, last = steps[s]
                h, j = pairs[pi]
                P, P_r = pS2[s % 2], pS2_r[s % 2]
                T, T_r = PT2[s % 3], PT2_r[s % 3]
                if 4 * j <= i < 4 * j + 4:
                    r = i - 4 * j
                    dt_, dt_r = dtmp[s % 2], dtmp_r[s % 2]
                    for m in range(2):
                        cx.op("dve", lambda m=m: nc.vector.scalar_tensor_tensor(
                            out=dt_[:, m, :], in0=dist[:, r, :], scalar=-8.0 * SLOPES[h], in1=P[:, m, :],
                            op0=ALU.mult, op1=ALU.add),
                            reads=[dist_r, P_r], writes=[dt_r])
                    cx.op("act", lambda: nc.scalar.activation(out=T[:], in_=dt_[:], func=AF.Exp, scale=0.125),
                          reads=[dt_r], writes=[T_r])
                else:
                    cx.op("act", lambda: nc.scalar.activation(out=T[:], in_=P[:], func=AF.Exp, scale=0.125),
                          reads=[P_r], writes=[T_r])

            def emit_pv(s):
                pi, i, first, last = steps[s]
                h, j = pairs[pi]
                T, T_r = PT2[s % 3], PT2_r[s % 3]
                for m in range(2):
                    for qs in range(4):
                        a = m * 4 + qs
                        bk, bk_r = pA[a // 3], pA_r[a // 3]
                        o = (a % 3) * VA
                        cx.op("pe", lambda m=m, qs=qs, bk=bk, o=o, a=a: nc.tensor.matmul(
                            bk[:, o:o + VA], lhsT=T[:, m, qs * 128:(qs + 1) * 128], rhs=vaug[:, i, h, :],
                            start=(first and a % 3 == 0), stop=last, skip_group_check=True),
                            reads=[T_r, v_r[i]], writes=[bk_r])

            def finish_pair(pi):
                h, j = pairs[pi]
                A, A_r = acc[pi % 2], acc_r[pi % 2]
                A4 = A[:].rearrange("p (m q v) -> p m q v", m=2, q=4)
                for bki, n in ((0, 3 * VA), (1, 3 * VA), (2, 2 * VA)):
                    cx.op("dve", lambda bki=bki, n=n: nc.vector.tensor_copy(
                        out=A[:, bki * 3 * VA:bki * 3 * VA + n], in_=pA[bki][:, 0:n]),
                        reads=[pA_r[bki]], writes=[A_r])
                cx.op("dve", lambda: nc.vector.reciprocal(out=rl[:], in_=A4[:, :, :, DV]),
                      reads=[A_r], writes=[rl_r])
                cx.op("dve", lambda: nc.vector.tensor_scalar(out=c1[:], in0=rl[:, 1, :], scalar1=lam[:], scalar2=None,
                                                             op0=ALU.mult),
                      reads=[rl_r, lam_r], writes=[c1_r])
                for qs in range(4):
                    cx.op("dve", lambda qs=qs: nc.vector.tensor_scalar(out=tO[:], in0=A4[:, 1, qs, 0:DV],
                                                                       scalar1=c1[:, qs:qs + 1], scalar2=None, op0=ALU.mult),
                          reads=[A_r, c1_r], writes=[tO_r])
                    cx.op("dve", lambda qs=qs: nc.vector.scalar_tensor_tensor(
                        out=Ot[:, qs, :], in0=A4[:, 0, qs, 0:DV], scalar=rl[:, 0, qs:qs + 1], in1=tO[:],
                        op0=ALU.mult, op1=ALU.subtract),
                        reads=[A_r, rl_r, tO_r], writes=[Ot_r[qs]])
                    cx.op("dve", lambda qs=qs: nc.vector.scalar_tensor_tensor(
                        out=jk[:], in0=Ot[:, qs, :], scalar=1.0, in1=Ot[:, qs, :], op0=ALU.mult, op1=ALU.mult,
                        accum_out=ssa[:, qs:qs + 1]),
                        reads=[Ot_r[qs]], writes=[jk_r, ssa_r])
                self.rstd_from_ss(ssa[:], rsa[:], ssa_r, rsa_r, DV)
                for qs in range(4):
                    cx.op("dve", lambda qs=qs: nc.vector.scalar_tensor_tensor(
                        out=ab[:, qs, :], in0=Ot[:, qs, :], scalar=rsa[:, qs:qs + 1], in1=gsub[:],
                        op0=ALU.mult, op1=ALU.mult),
                        reads=[Ot_r[qs], rsa_r, gsub_r], writes=[ab_r[qs]])

            def finish_pair_b(pi):
                h, j = pairs[pi]
                for qs in range(4):
                    cx.op("pe", lambda qs=qs: nc.tensor.transpose(
                        out=pTr[:, qs * 128:(qs + 1) * 128], in_=ab[:, qs, :], identity=self.ident[:]),
                        reads=[ab_r[qs], self.ident_r], writes=[pTr_r])
                st_, st_r = ats[pi % 2], ats_r[pi % 2]
                cx.op("dve", lambda: nc.vector.tensor_copy(out=st_[:], in_=pTr[:, 0:512]),
                      reads=[pTr_r], writes=[st_r])
                cx.dma("sp", self.dram["attn_s"][h, :, j * 512:(j + 1) * 512], st_[:], reads=[st_r],
                       writes=[self.res["attn_s"][h][j]])

            build_qz(0)
            emit_qk(0)
            nst = len(steps)
            if nst > 1:
                emit_qk(1)
            pending = None
            for s in range(nst):
                pi, i, first, last = steps[s]
                if first and pi + 1 < len(pairs):
                    build_qz(pi + 1)
                emit_exp(s)
                if s + 2 < nst:
                    emit_qk(s + 2)
                emit_pv(s)
                if pending is not None and s >= pending[1]:
                    finish_pair_b(pending[0])
                    pending = None
                if last:
                    finish_pair(pi)
                    pending = (pi, s + min(12, NB // 2))
            if pending is not None:
                finish_pair_b(pending[0])

    def even_lru(self, l, recT, recT_r):
        nc, cx = self.nc, self.cx
        S, NT = self.S, self.NT
        e = l // 2
        R = Res
        with ExitStack() as es:
            xr = self.sb(es, "xr", [128, S + 4], F32)
            xcs = [self.sb(es, f"xc{i}", [128, S], F32) for i in range(2)]
            xcb = self.sb(es, "xcb", [128, S], BF16)
            yg = self.sb(es, "yg", [128, S], F32)
            ra = [self.sb(es, f"ra{d}", [128, S], F32) for d in range(2)]
            iu = [self.sb(es, f"iu{d}", [128, S], F32) for d in range(2)]
            t1 = [self.sb(es, f"t1{d}", [128, S], F32) for d in range(2)]
            hh = t1
            xr_r, yg_r = R(), R()
            xcs_r = [R(), R()]
            xcb_r = R()
            ra_r, iu_r, t1_r = ([R(), R()] for _ in range(3))
            hh_r = t1_r
            wai = self.sb(es, "wai", [128, 2, 2, 4, 128], BF16)
            wai_r = R()
            cw = self.sb(es, "cw", [128, 4, 4], F32)
            cb = self.sb(es, "cb", [128, 4], F32)
            bai = self.sb(es, "bai", [128, 2, 2, 4], F32)
            lml = self.sb(es, "lml", [128, 2, 2, 4], F32)
            prm_r = R()
            pG = [self.ps(es, f"pG{i}", [128, 512], F32) for i in range(4)]
            pG_r = [R() for _ in range(4)]
            with nc.allow_non_contiguous_dma(reason="tiny per-channel parameter vectors"):
                def ldc(dst_ap, src_vec):
                    cx.dma("sp", dst_ap, src_vec.rearrange("(c p) -> p c", p=128), writes=[prm_r])
                for tap in range(4):
                    ldc(cw[:, :, tap], self.dram["conv_w"][e][tap])
                ldc(cb[:], self.dram["conv_b"][e])
                for d in range(2):
                    ldc(bai[:, 0, d, :], self.dram["lru_b_a"][e][d])
                    ldc(bai[:, 1, d, :], self.dram["lru_b_i"][e][d])
                    ldc(lml[:, 0, d, :], self.dram["lru_lambda"][e][d])
            for d in range(2):
                cx.dma("pool", wai[:, 0, d], self.dram["lru_w_a"][e][d].rearrange("h i j -> i h j"), writes=[wai_r])
                cx.dma("pool", wai[:, 1, d], self.dram["lru_w_i"][e][d].rearrange("h i j -> i h j"), writes=[wai_r])
            cx.op("act", lambda: nc.scalar.activation(out=lml[:, 0], in_=lml[:, 0], func=AF.Exp, scale=-1.0),
                  reads=[prm_r], writes=[prm_r])
            cx.op("act", lambda: nc.scalar.activation(out=lml[:, 0], in_=lml[:, 0], func=AF.Ln, bias=1.0),
                  reads=[prm_r], writes=[prm_r])
            cx.op("dve", lambda: nc.vector.tensor_scalar(out=lml[:, 1], in0=lml[:, 0], scalar1=-16.0, scalar2=None, op0=ALU.mult),
                  reads=[prm_r], writes=[prm_r])
            cx.op("dve", lambda: nc.vector.tensor_scalar(out=lml[:, 0], in0=lml[:, 0], scalar1=-8.0, scalar2=None, op0=ALU.mult),
                  reads=[prm_r], writes=[prm_r])
            cx.op("dve", lambda: nc.vector.memset(xr[:, 0:2], 0.0), writes=[xr_r])
            cx.op("dve", lambda: nc.vector.memset(xr[:, S + 2:S + 4], 0.0), writes=[xr_r])
            k = 0

            def conv_chunk(c):
                xc, xc_r = xcs[c % 2], xcs_r[c % 2]
                rows = slice(c * 128, (c + 1) * 128)
                cx.dma("sp", xr[:, 2:S + 2], self.dram["xr_s"][rows, :], reads=self.res["xr_s"][c], writes=[xr_r])
                cx.op("dve", lambda: nc.vector.tensor_scalar(out=xc[:], in0=xr[:, 2:S + 2], scalar1=cw[:, c, 2:3],
                                                             scalar2=cb[:, c:c + 1], op0=ALU.mult, op1=ALU.add),
                      reads=[xr_r, prm_r], writes=[xc_r])
                for tap, off in ((0, 0), (1, 1), (3, 3)):
                    cx.op("dve", lambda tap=tap, off=off: nc.vector.scalar_tensor_tensor(
                        out=xc[:], in0=xr[:, off:off + S], scalar=cw[:, c, tap:tap + 1], in1=xc[:],
                        op0=ALU.mult, op1=ALU.add),
                        reads=[xr_r, prm_r, xc_r], writes=[xc_r])

            def cast_chunk(c):
                cx.op("act", lambda: nc.scalar.copy(out=xcb[:], in_=xcs[c % 2][:]), reads=[xcs_r[c % 2]], writes=[xcb_r])

            conv_chunk(0)
            cast_chunk(0)
            for c in range(4):
                xc, xc_r = xcs[c % 2], xcs_r[c % 2]
                rows = slice(c * 128, (c + 1) * 128)
                cx.dma("sp", yg[:], self.dram["yg_s"][rows, :], reads=self.res["yg_s"][c], writes=[yg_r])
                if c + 1 < 4:
                    conv_chunk(c + 1)
                for d in range(2):
                    for n in range(NT):
                        tsl = slice(n * 512, (n + 1) * 512)
                        for ai, (dstt, dstt_r) in enumerate(((ra[d], ra_r[d]), (iu[d], iu_r[d]))):
                            P, P_r = pG[k % 4], pG_r[k % 4]
                            k += 1
                            cx.op("pe", lambda P=P, ai=ai, d=d, tsl=tsl: nc.tensor.matmul(
                                P[:], lhsT=wai[:, ai, d, c, :], rhs=xcb[:, tsl], start=True, stop=True),
                                reads=[wai_r, xcb_r], writes=[P_r])
                            cx.op("act", lambda P=P, ai=ai, dstt=dstt, d=d, tsl=tsl: nc.scalar.activation(
                                out=dstt[:, tsl], in_=P[:], func=AF.Sigmoid, bias=bai[:, ai, d, c:c + 1]),
                                reads=[P_r, prm_r], writes=[dstt_r])
                    cx.op("act", lambda d=d: nc.scalar.activation(out=t1[d][:], in_=ra[d][:], func=AF.Exp,
                                                                  scale=lml[:, 1, d, c:c + 1]),
                          reads=[ra_r[d], prm_r], writes=[t1_r[d]])
                    cx.op("act", lambda d=d: nc.scalar.activation(out=t1[d][:], in_=t1[d][:], func=AF.Sqrt, scale=-1.0, bias=1.0),
                          reads=[t1_r[d]], writes=[t1_r[d]])
                    cx.op("act", lambda d=d: nc.scalar.activation(out=ra[d][:], in_=ra[d][:], func=AF.Exp,
                                                                  scale=lml[:, 0, d, c:c + 1]),
                          reads=[ra_r[d], prm_r], writes=[ra_r[d]])
                    cx.op("dve", lambda d=d: nc.vector.tensor_tensor(out=iu[d][:], in0=iu[d][:], in1=xc[:], op=ALU.mult),
                          reads=[iu_r[d], xc_r], writes=[iu_r[d]])
                if c + 1 < 4:
                    cast_chunk(c + 1)
                cx.op("act", lambda: nc.scalar.activation(out=yg[:], in_=yg[:], func=AF.Gelu_apprx_tanh),
                      reads=[yg_r], writes=[yg_r])
                for d in range(2):
                    cx.op("dve", lambda d=d: nc.vector.tensor_tensor(out=iu[d][:], in0=iu[d][:], in1=t1[d][:], op=ALU.mult),
                          reads=[iu_r[d], t1_r[d]], writes=[iu_r[d]])
                    if d == 0:
                        cx.op("dve", lambda: nc.vector.tensor_tensor_scan(
                            out=hh[0][:], data0=ra[0][:], data1=iu[0][:], initial=0.0, op0=ALU.mult, op1=ALU.add),
                            reads=[ra_r[0], iu_r[0]], writes=[hh_r[0]])
                    else:
                        cx.op("dve", lambda: nc.vector.tensor_tensor_scan(
                            out=hh[1][:, ::-1], data0=ra[1][:, ::-1], data1=iu[1][:, ::-1], initial=0.0,
                            op0=ALU.mult, op1=ALU.add),
                            reads=[ra_r[1], iu_r[1]], writes=[hh_r[1]])
                cx.op("dve", lambda: nc.vector.tensor_tensor(out=hh[0][:], in0=hh[0][:], in1=hh[1][:], op=ALU.add),
                      reads=[hh_r[0], hh_r[1]], writes=[hh_r[0]])
                cx.op("dve", lambda: nc.vector.tensor_tensor(out=recT[:, c, :], in0=hh[0][:], in1=yg[:], op=ALU.mult),
                      reads=[hh_r[0], yg_r], writes=[recT_r[c]])

    def even_out(self, l, src, dst, recT, recT_r):
        nc, cx = self.nc, self.cx
        S = self.S
        e = l // 2
        R = Res
        wo_d = self.dram["w_mix_out"][e].rearrange("(c p) n -> p c n", p=128)
        with ExitStack() as es:
            wo = self.sb(es, "wo", [128, 8, D], BF16)
            wo_r = R()
            attnT = self.sb(es, "attnT", [128, 4, S], BF16)
            attnT_r = [R() for _ in range(4)]
            g2 = self.sb(es, "g2", [128, D], F32)
            g2_r = R()
            cx.dma("pool", wo[:], wo_d, writes=[wo_r])
            for h in range(4):
                cx.dma("sp", attnT[:, h, :], self.dram["attn_s"][h], reads=self.res["attn_s"][h], writes=[attnT_r[h]])
            self.load_bcast("sp", g2[:], g2_r, self.dram["ln_mix_post"][l])

            def get_lhs(b):
                bsl = slice(b * 128, (b + 1) * 128)
                return ([(attnT[:, c, bsl], attnT_r[c]) for c in range(4)]
                        + [(recT[:, c, bsl], recT_r[c]) for c in range(4)])

            self.proj_post_stage(es, get_lhs, wo, wo_r, g2, g2_r, src, dst, want_next=(dst != "out"))


class XStream:
    def __init__(self, prog, es, name, src, nbuf):
        self.p = prog
        self.src = src
        self.view = prog.dram[src].rearrange("(b p) d -> b p d", p=128)
        self.bufs = [prog.sb(es, f"{name}{i}", [128, D], F32) for i in range(nbuf)]
        self.res = [Res() for _ in range(nbuf)]
        self.loaded = set()
        self.n = nbuf

    def _load(self, b):
        i = b % self.n
        self.p.cx.dma("sp", self.bufs[i][:], self.view[b], reads=[self.p.res[self.src][b]], writes=[self.res[i]])
        self.loaded.add(b)

    def get(self, b):
        for bb in range(b, min(b + self.n, self.p.NB)):
            if bb not in self.loaded:
                self._load(bb)
        i = b % self.n
        return self.bufs[i][:], self.res[i]


ALIBI_COLS = 512


def _consts(S):
    c = {}
    bf = ml_dtypes.bfloat16
    c["c_ident"] = np.eye(128, dtype=np.float32).astype(bf)
    p = np.arange(128)[:, None, None]
    r = np.arange(4)[None, :, None]
    q = np.arange(512)[None, None, :]
    c["c_dist"] = (2.0 * np.maximum((128 * r + p) - q, 0)).astype(np.float32)
    pos = np.arange(S)
    hi, lo = (pos // 64).astype(np.float64), (pos % 64).astype(np.float64)
    c["c_kaug"] = np.stack([np.ones(S), np.ones(S), 64.0 * hi, lo]).astype(np.float32).astype(bf)
    qa = np.zeros((NH, 2, 4, S), np.float64)
    for h in range(NH):
        s8 = 8.0 * SLOPES[h]
        left = np.stack([-s8 * 64.0 * hi, -s8 * lo, s8 * np.ones(S), s8 * np.ones(S)])
        qa[h, 0] = left
        qa[h, 1] = -left
    c["c_qaug"] = qa.astype(np.float32).astype(bf)
    NB, NTH = S // 128, S // 512
    s_in = (np.arange(NB)[None, :, None] * 128 + np.arange(128)[:, None, None]).astype(np.int64)
    tc = np.empty((NTH, 128, NB, 257), bf)
    ts = np.empty((NTH, 128, NB, 257), bf)
    for st in range(NTH):
        s_out = st * 256 + np.arange(257, dtype=np.int64)[None, None, :]
        ang = (2.0 * np.pi / S) * ((s_in * s_out) % S).astype(np.float64)
        tc[st] = np.cos(ang).astype(np.float32).astype(bf)
        sn = np.sin(ang)
        sn[np.broadcast_to(((s_in * s_out) % S) * 2 % S == 0, sn.shape)] = 0.0
        ts[st] = sn.astype(np.float32).astype(bf)
    c["c_dftc"] = tc
    c["c_dfts"] = ts
    d_in = (np.arange(2)[None, :, None] * 128 + np.arange(128)[:, None, None]).astype(np.int64)
    d_out = np.arange(256, dtype=np.int64)[None, None, :]
    ang = (2.0 * np.pi / 256) * ((d_in * d_out) % 256).astype(np.float64)
    cdt = np.stack([np.cos(ang), -np.sin(ang), np.sin(ang)], axis=1)
    c["c_cdft"] = cdt.astype(np.float32).astype(bf)
    return c


def _alibi_table():
    t = np.zeros((128, ALIBI_COLS), np.float64)
    p = np.arange(128, dtype=np.float64)
    for h in range(NH):
        sl = SLOPES[h]
        for dl in range(1, 32):
            t[:, h * 32 + dl] = sl * (p - 128.0 * dl)
        for dr in range(1, 32):
            t[:, 128 + h * 32 + dr] = -sl * (128.0 * dr + p - 127.0)
        for qs in range(4):
            t[:, 256 + qs * 4 + h] = -sl * (128.0 * qs + p)
            t[:, 272 + qs * 4 + h] = -sl * (511.0 - 128.0 * qs - p)
    return t.astype(np.float32)


_CACHE = {}


def run(inputs_per_core, S, plan, trace=False):
    key = (S, tuple(plan))
    if key not in _CACHE:
        _CACHE[key] = Prog(S, plan)
        _CACHE[key].build()
    prog = _CACHE[key]
    consts = _consts(S)
    in_maps = []
    for m in inputs_per_core:
        mm = dict(consts)
        mm.update(m)
        in_maps.append(mm)
    res = run_bass_kernel_spmd(prog.nc, in_maps, core_ids=list(range(len(in_maps))), trace=trace)
    return res


def full_plan():
    plan = []
    for l in range(DEPTH):
        src = "x" if l == 0 else "xs"
        if l % 2 == 0:
            plan.append(("even", l, src, "xs"))
        else:
            plan.append(("fnet", l, src, "xs"))
        plan.append(("ffn", l, "xs", "out" if l == DEPTH - 1 else "xs"))
    return plan


def kernel(**inputs):
    x = np.ascontiguousarray(inputs["x"], dtype=np.float32)
    B, S, _ = x.shape
    shared = {k: np.ascontiguousarray(v, dtype=np.float32) for k, v in inputs.items() if k != "x"}
    per_core = []
    for b in range(B):
        m = dict(shared)
        m["x"] = x[b]
        per_core.append(m)
    res = run(per_core, S, full_plan())
    return np.stack([r["out"] for r in res.results], axis=0).astype(np.float32)
```

```python
import math
from contextlib import ExitStack

import numpy as np
import ml_dtypes

import concourse.bass as bass
import concourse.mybir as mybir
from concourse.bass_utils import run_bass_kernel_spmd

F32 = mybir.dt.float32
BF16 = mybir.dt.bfloat16
AF = mybir.ActivationFunctionType
ALU = mybir.AluOpType

D = 1024
DEPTH = 4
NH = 4
QK = 64
DV = 128
LRU_W = 512
IN_COLS = 2560
D_FF = 2816
NF = D_FF // 128
EPS = 1e-6
SLOPES = [2.0 ** (-8.0 * (h + 1) / NH) for h in range(NH)]
VA = DV + 1


class Res:
    __slots__ = ("w", "r")

    def __init__(self):
        self.w = {}
        self.r = {}


def _merge(dst, toks):
    for k, (sem, val) in toks.items():
        cur = dst.get(k)
        if cur is None or cur[1] < val:
            dst[k] = (sem, val)


class Ctx:
    def __init__(self, nc):
        self.nc = nc
        self.es = ExitStack()
        self.eng = {}
        for name, e in (("pe", nc.tensor), ("act", nc.scalar), ("dve", nc.vector),
                        ("pool", nc.gpsimd), ("sp", nc.sync)):
            sem = self.es.enter_context(nc.semaphore("sem_" + name))
            self.eng[name] = {"e": e, "sem": sem, "cnt": 0, "waited": {}, "key": "E" + name}
        self.dq = {}
        for q, n in (("sp", 24), ("pool", 12)):
            pool = []
            for i in range(n):
                sem = self.es.enter_context(nc.semaphore(f"dq_{q}_{i}"))
                pool.append([sem, 0, f"Q{q}{i}"])
            self.dq[q] = {"pool": pool, "idx": 0}
        self.n_inst = 0

    def _wait(self, E, key, sem, val):
        if E["waited"].get(key, 0) >= val:
            return
        E["e"].wait_ge(sem, val)
        E["waited"][key] = val
        self.n_inst += 1

    def _deps(self, E, reads, writes, skip_key=None):
        need = {}
        for r in reads:
            _merge(need, r.w)
        for w in writes:
            _merge(need, w.w)
            _merge(need, w.r)
        for k, (sem, val) in need.items():
            if k == skip_key:
                continue
            self._wait(E, k, sem, val)

    def _commit(self, key, tok, reads, writes):
        for r in reads:
            cur = r.r.get(key)
            if cur is None or cur[1] < tok[1]:
                r.r[key] = tok
        for w in writes:
            w.w = {key: tok}
            w.r = {}

    def op(self, eng, fn, reads=(), writes=()):
        E = self.eng[eng]
        self._deps(E, reads, writes, skip_key=E["key"] if eng == "pe" else None)
        ins = fn()
        E["cnt"] += 1
        ins.then_inc(E["sem"], 1)
        tok = (E["sem"], E["cnt"])
        self._commit(E["key"], tok, reads, writes)
        self.n_inst += 1
        return tok

    def dma(self, q, out, in_, reads=(), writes=()):
        E = self.eng[q]
        Q = self.dq[q]
        slot = Q["pool"][Q["idx"]]
        Q["idx"] = (Q["idx"] + 1) % len(Q["pool"])
        sem, cnt, key = slot
        if cnt > 0:
            self._wait(E, key, sem, cnt)
        self._deps(E, reads, writes)
        E["e"].dma_start(out=out, in_=in_).then_inc(sem, 16)
        slot[1] = cnt + 16
        tok = (sem, cnt + 16)
        self._commit(key, tok, reads, writes)
        self.n_inst += 1
        return tok

    def barrier(self, skip=()):
        toks = {}
        for name, E in self.eng.items():
            if E["cnt"] > 0:
                toks[E["key"]] = (E["sem"], E["cnt"])
        for q, Q in self.dq.items():
            for sem, cnt, key in Q["pool"]:
                if cnt > 0:
                    toks[key] = (sem, cnt)
        for name, E in self.eng.items():
            if name in skip:
                continue
            for k, (sem, val) in toks.items():
                if k == E["key"]:
                    continue
                self._wait(E, k, sem, val)
        for name in ("act", "dve", "pool"):
            E = self.eng[name]
            if name in skip:
                continue
            if E["cnt"] > 0:
                self._wait(E, E["key"], E["sem"], E["cnt"])


class Prog:
    def __init__(self, S, plan):
        self.S = S
        self.NB = S // 128
        self.NT = S // 512
        self.plan = plan
        nc = bass.Bass("TRN2", target_bir_lowering=False)
        self.nc = nc
        self.cx = Ctx(nc)
        self.dram = {}
        self._declare()

    def _dt(self, name, shape, dtype, kind):
        t = self.nc.dram_tensor(name, list(shape), dtype, kind=kind).ap()
        self.dram[name] = t
        return t

    def _declare(self):
        S = self.S
        ne, no = 2, 2
        I = "ExternalInput"
        self._dt("x", [S, D], F32, I)
        for n in ("ln_mix_pre", "ln_mix_post", "ln_ffn_pre", "ln_ffn_post"):
            self._dt(n, [DEPTH, D], F32, I)
        self._dt("w_in", [ne, D, IN_COLS], F32, I)
        self._dt("w_mix_out", [ne, D, D], F32, I)
        for n in ("lambda_q1", "lambda_k1", "lambda_q2", "lambda_k2"):
            self._dt(n, [ne, QK], F32, I)
        self._dt("attn_subln", [ne, DV], F32, I)
        self._dt("conv_w", [ne, 4, LRU_W], F32, I)
        self._dt("conv_b", [ne, LRU_W], F32, I)
        self._dt("lru_w_a", [ne, 2, 4, 128, 128], F32, I)
        self._dt("lru_b_a", [ne, 2, LRU_W], F32, I)
        self._dt("lru_w_i", [ne, 2, 4, 128, 128], F32, I)
        self._dt("lru_b_i", [ne, 2, LRU_W], F32, I)
        self._dt("lru_lambda", [ne, 2, LRU_W], F32, I)
        self._dt("w_fourier_out", [no, D, D], F32, I)
        self._dt("w_ffn_gate", [DEPTH, D, D_FF], F32, I)
        self._dt("w_ffn_up", [DEPTH, D, D_FF], F32, I)
        self._dt("w_ffn_down", [DEPTH, D_FF, D], F32, I)
        self._dt("c_ident", [128, 128], BF16, I)
        self._dt("c_qaug", [NH, 2, 4, S], BF16, I)
        self._dt("c_kaug", [4, S], BF16, I)
        self._dt("c_dist", [128, 4, 512], F32, I)
        self._dt("c_dftc", [S // 512, 128, S // 128, 257], BF16, I)
        self._dt("c_dfts", [S // 512, 128, S // 128, 257], BF16, I)
        self._dt("c_cdft", [128, 3, 2, 256], BF16, I)
        self._dt("out", [S, D], F32, "ExternalOutput")
        self._dt("xs", [S, D], F32, "Internal")
        self._dt("xr_s", [LRU_W, S], F32, "Internal")
        self._dt("yg_s", [LRU_W, S], F32, "Internal")
        self._dt("attn_s", [4, 128, S], BF16, "Internal")
        self.res = {}
        for n in ("x", "out", "xs"):
            self.res[n] = [Res() for _ in range(self.NB)]
        self.res["xr_s"] = [[Res() for _ in range(self.NT)] for _ in range(4)]
        self.res["yg_s"] = [[Res() for _ in range(self.NT)] for _ in range(4)]
        self.res["attn_s"] = [[Res() for _ in range(self.NT)] for _ in range(4)]

    def sb(self, es, name, shape, dtype):
        self._uid = getattr(self, "_uid", 0) + 1
        return es.enter_context(self.nc.sbuf_tensor(f"{name}_{self._uid}", list(shape), dtype))

    def ps(self, es, name, shape, dtype):
        self._uid = getattr(self, "_uid", 0) + 1
        return es.enter_context(self.nc.psum_tensor(f"{name}_{self._uid}", list(shape), dtype))

    def load_bcast(self, q, dst, dst_res, src_row):
        self.cx.dma(q, dst, src_row.partition_broadcast(128), writes=[dst_res])

    def rstd_from_ss(self, ss, rstd, ss_res, rstd_res, n):
        v = self.nc.vector
        cx = self.cx
        w = ss.shape[-1]
        sv = self.rs_scr[:, 0:w]
        st = self.rs_scr[:, 8:8 + w]
        R = self.rs_res
        I32 = mybir.dt.int32
        cx.op("dve", lambda: v.tensor_scalar(out=sv, in0=ss, scalar1=1.0 / n, scalar2=EPS,
                                             op0=ALU.mult, op1=ALU.add),
              reads=[ss_res], writes=[R])
        cx.op("dve", lambda: v.tensor_scalar(out=st.bitcast(I32), in0=sv.bitcast(I32), scalar1=1, scalar2=None,
                                             op0=ALU.logical_shift_right),
              reads=[R], writes=[R])
        cx.op("dve", lambda: v.tensor_scalar(out=rstd.bitcast(I32), in0=st.bitcast(I32), scalar1=-1.0,
                                             scalar2=float(0x5f3759df), op0=ALU.mult, op1=ALU.add),
              reads=[R], writes=[rstd_res])
        for _ in range(3):
            cx.op("dve", lambda: v.tensor_tensor(out=st, in0=rstd, in1=rstd, op=ALU.mult),
                  reads=[rstd_res], writes=[R])
            cx.op("dve", lambda: v.scalar_tensor_tensor(out=st, in0=st, scalar=-0.5, in1=sv, op0=ALU.mult, op1=ALU.mult),
                  reads=[R], writes=[R])
            cx.op("dve", lambda: v.scalar_tensor_tensor(out=rstd, in0=st, scalar=1.5, in1=rstd, op0=ALU.add, op1=ALU.mult),
                  reads=[R, rstd_res], writes=[rstd_res])

    def build(self):
        nc, cx = self.nc, self.cx
        with ExitStack() as es:
            self.ident = self.sb(es, "ident", [128, 128], BF16)
            self.ident_r = Res()
            self.rs_scr = self.sb(es, "rs_scr", [128, 16], F32)
            self.rs_res = Res()
            self.ssn = self.sb(es, "ssn", [128, self.NB], F32)
            self.rstdn = self.sb(es, "rstdn", [128, self.NB], F32)
            self.ssn_r = [Res() for _ in range(self.NT)]
            self.rstdn_r = [Res() for _ in range(self.NT)]
            with nc.named_scope("prepass"):
                self.stats_prepass(self.plan[0][2])
                pool_free = False
                cx.barrier(skip=("pool",) if pool_free else ())
            cx.dma("sp", self.ident[:], self.dram["c_ident"], writes=[self.ident_r])
            for ph in self.plan:
                kind = ph[0]
                if kind == "ffn":
                    with nc.named_scope(f"ffn{ph[1]}"):
                        self.phase_ffn(*ph[1:])
                elif kind == "fnet":
                    with nc.named_scope(f"fnet{ph[1]}"):
                        self.phase_fnet(*ph[1:])
                elif kind == "even":
                    self.phase_even(*ph[1:])
                else:
                    raise ValueError(kind)
                cx.barrier()
        self.cx.es.close()
        return nc

    def prenorm_T(self, xstream, b0, g_bc, g_r, hn, hn_r, ss, ss_r, rstd, rstd_r, pT, pT_r, hnT, hnT_r, nj=4):
        nc, cx = self.nc, self.cx

        def emit_hn(j):
            b = b0 + j
            xj, xj_r = xstream.get(b)
            H, H_r = hn[j % 2], hn_r[j % 2]
            cx.op("dve", lambda: nc.vector.scalar_tensor_tensor(
                out=H[:], in0=xj, scalar=self.rstdn[:, b:b + 1], in1=g_bc[:],
                op0=ALU.mult, op1=ALU.mult),
                reads=[xj_r, self.rstdn_r[b // 4], g_r], writes=[H_r])

        emit_hn(0)
        for j in range(nj):
            H, H_r = hn[j % 2], hn_r[j % 2]
            P, P_r = pT[j % 2], pT_r[j % 2]
            for c in range(8):
                cx.op("pe", lambda c=c, H=H, P=P: nc.tensor.transpose(
                    out=P[:, c * 128:(c + 1) * 128], in_=H[:, c * 128:(c + 1) * 128], identity=self.ident[:]),
                    reads=[H_r, self.ident_r], writes=[P_r])
            if j + 1 < nj:
                emit_hn(j + 1)
            src = P[:].rearrange("p (c t) -> p c t", c=8)
            dst = hnT[:, :, j * 128:(j + 1) * 128]
            cx.op("act", lambda src=src, dst=dst: nc.scalar.copy(out=dst, in_=src), reads=[P_r], writes=[hnT_r])

    def pre_hn(self, xstream, b, g_bc, g_r, H, H_r):
        nc, cx = self.nc, self.cx
        xj, xj_r = xstream.get(b)
        cx.op("dve", lambda: nc.vector.scalar_tensor_tensor(
            out=H[:], in0=xj, scalar=self.rstdn[:, b:b + 1], in1=g_bc[:], op0=ALU.mult, op1=ALU.mult),
            reads=[xj_r, self.rstdn_r[b // 4], g_r], writes=[H_r])

    def pre_tr(self, H, H_r, P, P_r, hnT, hnT_r, j):
        nc, cx = self.nc, self.cx
        for c in range(8):
            cx.op("pe", lambda c=c: nc.tensor.transpose(
                out=P[:, c * 128:(c + 1) * 128], in_=H[:, c * 128:(c + 1) * 128], identity=self.ident[:]),
                reads=[H_r, self.ident_r], writes=[P_r])
        src = P[:].rearrange("p (c t) -> p c t", c=8)
        dst = hnT[:, :, j * 128:(j + 1) * 128]
        cx.op("act", lambda: nc.scalar.copy(out=dst, in_=src), reads=[P_r], writes=[hnT_r])

    def next_stats(self, xj, xj_r, junk, junk_r, b):
        nc, cx = self.nc, self.cx
        g = b // 4
        cx.op("act", lambda: nc.scalar.activation(out=junk[:], in_=xj, func=AF.Square, accum_out=self.ssn[:, b:b + 1]),
              reads=[xj_r], writes=[junk_r, self.ssn_r[g]])
        if b % 4 == 3:
            self.rstd_from_ss(self.ssn[:, 4 * g:4 * g + 4], self.rstdn[:, 4 * g:4 * g + 4], self.ssn_r[g], self.rstdn_r[g], D)

    def postnorm_residual(self, ps_halves, ps_res, xj, xj_r, g_bc, g_r, junk, junk_r, ss2, ss2_r, rstd1, rstd1_r,
                          tmp, tmp_r, b=None, want_next=True):
        nc, cx = self.nc, self.cx
        for hf in range(2):
            cx.op("dve", lambda hf=hf: nc.vector.tensor_copy(out=tmp[:, hf * 512:(hf + 1) * 512], in_=ps_halves[hf]),
                  reads=[ps_res[hf]], writes=[tmp_r])
        cx.op("act", lambda: nc.scalar.activation(out=junk[:], in_=tmp[:], func=AF.Square, accum_out=ss2[:, 2:3]),
              reads=[tmp_r], writes=[junk_r, ss2_r])
        self.rstd_from_ss(ss2[:, 2:3], rstd1[:], ss2_r, rstd1_r, D)
        cx.op("dve", lambda: nc.vector.scalar_tensor_tensor(
            out=tmp[:], in0=tmp[:], scalar=rstd1[:], in1=g_bc[:], op0=ALU.mult, op1=ALU.mult),
            reads=[tmp_r, rstd1_r, g_r], writes=[tmp_r])
        cx.op("dve", lambda: nc.vector.tensor_tensor(out=xj, in0=tmp[:], in1=xj, op=ALU.add),
              reads=[tmp_r, xj_r], writes=[xj_r])
        if want_next and b is not None:
            self.next_stats(xj, xj_r, junk, junk_r, b)

    def proj_post_stage(self, es, get_lhs, w, w_r, g2, g2_r, src, dst, want_next):
        nc, cx = self.nc, self.cx
        NB = self.NB
        R = Res
        dstv = self.dram[dst].rearrange("(b p) d -> b p d", p=128)
        xb = XStream(self, es, "xb", src, 4)
        Ys = [self.sb(es, f"Y{i}", [128, 4, D], F32) for i in range(2)]
        Ys_r = [[R() for _ in range(4)] for _ in range(2)]
        junk = self.sb(es, "junk", [128, D], BF16)
        junk_r = R()
        junk2 = self.sb(es, "junk2", [128, D], BF16)
        junk2_r = R()
        ssx = [self.sb(es, f"ssx{i}", [128, 8], F32) for i in range(2)]
        ssx_r = [R(), R()]
        rsx = self.sb(es, "rsx", [128, 8], F32)
        rsx_r = R()
        py = [self.ps(es, f"py{i}", [128, 512], F32) for i in range(4)]
        py_r = [R() for _ in range(4)]
        NG = NB // 4

        def front(g):
            Y, Y_r = Ys[g % 2], Ys_r[g % 2]
            for jj in range(4):
                b = 4 * g + jj
                pr = [py[(b % 2) * 2], py[(b % 2) * 2 + 1]]
                pr_r = [py_r[(b % 2) * 2], py_r[(b % 2) * 2 + 1]]
                lhs = get_lhs(b)
                for hf in range(2):
                    for c in range(8):
                        la, lr = lhs[c]
                        cx.op("pe", lambda la=la, c=c, hf=hf, pr=pr: nc.tensor.matmul(
                            pr[hf][:], lhsT=la, rhs=w[:, c, hf * 512:(hf + 1) * 512], start=(c == 0), stop=(c == 7)),
                            reads=[lr, w_r], writes=[pr_r[hf]])
                for hf in range(2):
                    cx.op("act", lambda hf=hf, jj=jj, pr=pr, Y=Y: nc.scalar.copy(
                        out=Y[:, jj, hf * 512:(hf + 1) * 512], in_=pr[hf][:]),
                        reads=[pr_r[hf]], writes=[Y_r[jj]])
                cx.op("act", lambda jj=jj, Y=Y, g=g: nc.scalar.activation(
                    out=junk[:], in_=Y[:, jj, :], func=AF.Square, accum_out=ssx[g % 2][:, jj:jj + 1]),
                    reads=[Y_r[jj]], writes=[junk_r, ssx_r[g % 2]])

        def back(g):
            Y, Y_r = Ys[g % 2], Ys_r[g % 2]
            wdt = 8 if (want_next and g > 0) else 4
            self.rstd_from_ss(ssx[g % 2][:, 0:wdt], rsx[:, 0:wdt], ssx_r[g % 2], rsx_r, D)
            if wdt == 8:
                cx.op("dve", lambda: nc.vector.tensor_copy(out=self.rstdn[:, 4 * (g - 1):4 * g], in_=rsx[:, 4:8]),
                      reads=[rsx_r], writes=[self.rstdn_r[g - 1]])
            for jj in range(4):
                b = 4 * g + jj
                xj, xj_r = xb.get(b)
                cx.op("dve", lambda jj=jj, Y=Y: nc.vector.scalar_tensor_tensor(
                    out=Y[:, jj, :], in0=Y[:, jj, :], scalar=rsx[:, jj:jj + 1], in1=g2[:], op0=ALU.mult, op1=ALU.mult),
                    reads=[Y_r[jj], rsx_r, g2_r], writes=[Y_r[jj]])
                cx.op("dve", lambda jj=jj, xj=xj, Y=Y: nc.vector.tensor_tensor(out=xj, in0=Y[:, jj, :], in1=xj, op=ALU.add),
                      reads=[Y_r[jj], xj_r], writes=[xj_r])
                if want_next:
                    cx.op("act", lambda jj=jj, xj=xj, g=g: nc.scalar.activation(
                        out=junk2[:], in_=xj, func=AF.Square, accum_out=ssx[(g + 1) % 2][:, 4 + jj:5 + jj]),
                        reads=[xj_r], writes=[junk2_r, ssx_r[(g + 1) % 2]])
                cx.dma("sp", dstv[b], xj, reads=[xj_r], writes=[self.res[dst][b]])

        front(0)
        for g in range(NG):
            if g + 1 < NG:
                front(g + 1)
            back(g)
        if want_next:
            self.rstd_from_ss(ssx[NG % 2][:, 4:8], rsx[:, 4:8], ssx_r[NG % 2], rsx_r, D)
            cx.op("dve", lambda: nc.vector.tensor_copy(out=self.rstdn[:, 4 * (NG - 1):4 * NG], in_=rsx[:, 4:8]),
                  reads=[rsx_r], writes=[self.rstdn_r[NG - 1]])

    def stats_prepass(self, src):
        with ExitStack() as es:
            xp = XStream(self, es, "xp", src, 4)
            junk = self.sb(es, "junkp", [128, D], BF16)
            junk_r = Res()
            for b in range(self.NB):
                xj, xj_r = xp.get(b)
                self.next_stats(xj, xj_r, junk, junk_r, b)

    def phase_ffn(self, l, src, dst):
        nc, cx = self.nc, self.cx
        S, NT, NB = self.S, self.NT, self.NB
        dstv = self.dram[dst].rearrange("(b p) d -> b p d", p=128)
        wg_d = self.dram["w_ffn_gate"][l].rearrange("(c p) n -> p c n", p=128)
        wu_d = self.dram["w_ffn_up"][l].rearrange("(c p) n -> p c n", p=128)
        wd_d = self.dram["w_ffn_down"][l].rearrange("(f p) n -> p f n", p=128)
        R = Res
        with ExitStack() as es:
            wg = self.sb(es, "wg", [128, 8, D_FF], BF16)
            wu = self.sb(es, "wu", [128, 8, D_FF], BF16)
            wd = self.sb(es, "wd", [128, NF, D], BF16)
            g1 = self.sb(es, "g1", [128, D], F32)
            g2 = self.sb(es, "g2", [128, D], F32)
            xa = XStream(self, es, "xa", src, 2)
            xb = XStream(self, es, "xb", src, 2)
            hn = [self.sb(es, f"hn{i}", [128, D], BF16) for i in range(2)]
            hnTs = [self.sb(es, f"hnT{i}", [128, 8, 512], BF16) for i in range(2)]
            hT = self.sb(es, "hT", [128, NF, 512], BF16)
            sg = [self.sb(es, "sg0", [128, 512], F32)] * 2
            tmp = self.sb(es, "tmp", [128, D], F32)
            junk = self.sb(es, "junk", [128, D], BF16)
            ss2 = self.sb(es, "ss2", [128, 4], F32)
            rstd1 = self.sb(es, "rstd1", [128, 1], F32)
            pT = [self.ps(es, f"pT{i}", [128, D], BF16) for i in range(2)]
            pg = [self.ps(es, f"pg{i}", [128, 512], F32) for i in range(2)]
            pu = [self.ps(es, f"pu{i}", [128, 512], F32) for i in range(2)]
            py = [self.ps(es, f"py{i}", [128, 512], F32) for i in range(2)]
            wg_r = [R() for _ in range(6)]
            wu_r = [R() for _ in range(6)]
            wd_r = [R() for _ in range(NF)]
            g1_r, g2_r = R(), R()
            tmp_r, junk_r, ss2_r, rstd1_r = (R() for _ in range(4))
            hnTs_r = [R(), R()]
            hn_r = [R(), R()]
            hT_r = [R() for _ in range(NF)]
            sg_r = [R()] * 2
            pT_r = [R(), R()]
            pg_r = [R(), R()]
            pu_r = [R(), R()]
            py_r = [R(), R()]

            self.load_bcast("sp", g1[:], g1_r, self.dram["ln_ffn_pre"][l])
            self.load_bcast("sp", g2[:], g2_r, self.dram["ln_ffn_post"][l])
            xa.get(0)
            for gi in range(6):
                c0 = gi * 512
                c1 = min(D_FF, c0 + 512)
                cx.dma("pool", wg[:, :, c0:c1], wg_d[:, :, c0:c1], writes=[wg_r[gi]])
                cx.dma("pool", wu[:, :, c0:c1], wu_d[:, :, c0:c1], writes=[wu_r[gi]])
            for f in range(NF):
                cx.dma("pool", wd[:, f, :], wd_d[:, f, :], writes=[wd_r[f]])

            for j in range(4):
                self.pre_hn(xa, j, g1, g1_r, hn[j % 2], hn_r[j % 2])
                self.pre_tr(hn[j % 2], hn_r[j % 2], pT[j % 2], pT_r[j % 2], hnTs[0], hnTs_r[0], j)
            for t in range(NT):
                hnT, hnT_r = hnTs[t % 2], hnTs_r[t % 2]
                for f in range(NF):
                    gi = f // 4
                    G, U = pg[f % 2], pu[f % 2]
                    G_r, U_r = pg_r[f % 2], pu_r[f % 2]
                    for c in range(8):
                        cx.op("pe", lambda c=c, G=G: nc.tensor.matmul(G[:], lhsT=wg[:, c, f * 128:(f + 1) * 128],
                                                                       rhs=hnT[:, c, :], start=(c == 0), stop=(c == 7)),
                              reads=[wg_r[gi], hnT_r], writes=[G_r])
                    for c in range(8):
                        cx.op("pe", lambda c=c, U=U: nc.tensor.matmul(U[:], lhsT=wu[:, c, f * 128:(f + 1) * 128],
                                                                       rhs=hnT[:, c, :], start=(c == 0), stop=(c == 7)),
                              reads=[wu_r[gi], hnT_r], writes=[U_r])
                    sgt, sgt_r = sg[f % 2], sg_r[f % 2]
                    cx.op("act", lambda G=G, sgt=sgt: nc.scalar.activation(out=sgt[:], in_=G[:], func=AF.Silu),
                          reads=[G_r], writes=[sgt_r])
                    cx.op("dve", lambda U=U, sgt=sgt, f=f: nc.vector.tensor_tensor(out=hT[:, f, :], in0=sgt[:], in1=U[:],
                                                                                     op=ALU.mult),
                          reads=[sgt_r, U_r], writes=[hT_r[f]])
                    if t + 1 < NT and f >= 4 and f < 20:
                        j = (f - 4) // 4
                        if (f - 4) % 4 == 0:
                            self.pre_hn(xa, 4 * (t + 1) + j, g1, g1_r, hn[j % 2], hn_r[j % 2])
                        elif (f - 4) % 4 == 2:
                            self.pre_tr(hn[j % 2], hn_r[j % 2], pT[j % 2], pT_r[j % 2],
                                        hnTs[(t + 1) % 2], hnTs_r[(t + 1) % 2], j)
                for j in range(4):
                    b = 4 * t + j
                    for hf in range(2):
                        for f in range(NF):
                            cx.op("pe", lambda f=f, hf=hf, j=j: nc.tensor.matmul(
                                py[hf][:], lhsT=hT[:, f, j * 128:(j + 1) * 128], rhs=wd[:, f, hf * 512:(hf + 1) * 512],
                                start=(f == 0), stop=(f == NF - 1)),
                                reads=[hT_r[f], wd_r[f]], writes=[py_r[hf]])
                    xj, xj_r = xb.get(b)
                    self.postnorm_residual([py[0][:], py[1][:]], py_r, xj, xj_r, g2, g2_r, junk, junk_r,
                                           ss2, ss2_r, rstd1, rstd1_r, tmp, tmp_r, b=b, want_next=(dst != "out"))
                    cx.dma("sp", dstv[b], xj, reads=[xj_r], writes=[self.res[dst][b]])


    def phase_fnet(self, l, src, dst):
        nc, cx = self.nc, self.cx
        S, NB = self.S, self.NB
        NTH = S // 512
        o = l // 2
        wf_d = self.dram["w_fourier_out"][o].rearrange("(c p) n -> p c n", p=128)
        R = Res
        sc1 = 1.0 / math.sqrt(S)
        sc2 = 1.0 / 16.0
        with ExitStack() as es0:
            FT = self.sb(es0, "FTall", [128, 8, S], BF16)
            FT_r = [R() for _ in range(8)]
            with ExitStack() as es:
                hnA = self.sb(es, "hnA", [128, NB, D], BF16)
                hnA_r = [R() for _ in range(NB)]
                cd = self.sb(es, "cd", [128, 3, 2, 256], BF16)
                cd_r = R()
                g1 = self.sb(es, "g1", [128, D], F32)
                g1_r = R()
                xa = XStream(self, es, "xa", src, 4)
                tabs = [self.sb(es, f"tab{i}", [128, NB, 257], BF16) for i in range(3)]
                tab_r = [R() for _ in range(3)]
                YT = self.sb(es, "YT", [128, 2, 8, 258], BF16)
                YT_r = [[R() for _ in range(8)] for _ in range(2)]
                pY = [self.ps(es, f"pY{i}", [128, 512], F32) for i in range(4)]
                pY_r = [R() for _ in range(4)]
                pF = [self.ps(es, f"pF{i}", [128, 512], F32) for i in range(4)]
                pF_r = [R() for _ in range(4)]

                self.load_bcast("sp", g1[:], g1_r, self.dram["ln_mix_pre"][l])
                cx.dma("sp", cd[:], self.dram["c_cdft"], writes=[cd_r])
                xa.get(0)
                tabsrc = (self.dram["c_dftc"], self.dram["c_dfts"])

                def load_tab(u):
                    if u < 2 * NTH:
                        cx.dma("sp", tabs[u % 3][:], tabsrc[u % 2][u // 2], writes=[tab_r[u % 3]])

                load_tab(0)
                load_tab(1)
                for b in range(NB):
                    xj, xj_r = xa.get(b)
                    cx.op("dve", lambda xj=xj, b=b: nc.vector.scalar_tensor_tensor(
                        out=hnA[:, b, :], in0=xj, scalar=self.rstdn[:, b:b + 1], in1=g1[:], op0=ALU.mult, op1=ALU.mult),
                        reads=[xj_r, self.rstdn_r[b // 4], g1_r], writes=[hnA_r[b]])
                ev = 0
                for st in range(NTH):
                    w = 257 if st == NTH - 1 else 256
                    if st == 0 and NTH > 1:
                        banks = pY + pF
                        banks_r = pY_r + pF_r
                        load_tab(2)
                        for b in range(NB):
                            for a16 in range(16):
                                c, cs = a16 // 2, a16 % 2
                                bk, bk_r = banks[a16 // 2], banks_r[a16 // 2]
                                off = (a16 % 2) * 256
                                cx.op("pe", lambda b=b, c=c, cs=cs, bk=bk, off=off, a16=a16: nc.tensor.matmul(
                                    bk[:, off:off + 256], lhsT=hnA[:, b, c * 128:(c + 1) * 128], rhs=tabs[cs][:, b, 0:256],
                                    start=(b == 0 and a16 % 2 == 0), stop=(b == NB - 1), skip_group_check=True),
                                    reads=[hnA_r[b], tab_r[cs]], writes=[bk_r])
                        load_tab(3)
                        for a16 in range(16):
                            c, cs = a16 // 2, a16 % 2
                            bk, bk_r = banks[a16 // 2], banks_r[a16 // 2]
                            off = (a16 % 2) * 256
                            if (a16 // 2) % 2 == 0:
                                cx.op("act", lambda bk=bk, off=off, cs=cs, c=c: nc.scalar.activation(
                                    out=YT[:, cs, c, 0:256], in_=bk[:, off:off + 256], func=AF.Copy, scale=sc1),
                                    reads=[bk_r], writes=[YT_r[cs][c]])
                            else:
                                cx.op("dve", lambda bk=bk, off=off, cs=cs, c=c: nc.vector.tensor_scalar(
                                    out=YT[:, cs, c, 0:256], in0=bk[:, off:off + 256], scalar1=sc1, scalar2=None, op0=ALU.mult),
                                    reads=[bk_r], writes=[YT_r[cs][c]])
                    for cs in range(2):
                        if st == 0 and NTH > 1:
                            break
                        u = 2 * st + cs
                        i = u % 3
                        load_tab(u + 2)
                        for c in range(8):
                            P, P_r = pY[(c % 2) * 2 + cs], pY_r[(c % 2) * 2 + cs]
                            for b in range(NB):
                                cx.op("pe", lambda b=b, c=c, i=i, P=P: nc.tensor.matmul(
                                    P[:, 0:w], lhsT=hnA[:, b, c * 128:(c + 1) * 128], rhs=tabs[i][:, b, 0:w],
                                    start=(b == 0), stop=(b == NB - 1)),
                                    reads=[hnA_r[b], tab_r[i]], writes=[P_r])
                            if ev % 2 == 0:
                                cx.op("act", lambda P=P, cs=cs, c=c: nc.scalar.activation(
                                    out=YT[:, cs, c, 0:w], in_=P[:, 0:w], func=AF.Copy, scale=sc1),
                                    reads=[P_r], writes=[YT_r[cs][c]])
                            else:
                                cx.op("dve", lambda P=P, cs=cs, c=c: nc.vector.tensor_scalar(
                                    out=YT[:, cs, c, 0:w], in0=P[:, 0:w], scalar1=sc1, scalar2=None, op0=ALU.mult),
                                    reads=[P_r], writes=[YT_r[cs][c]])
                            ev += 1
                    for c2 in range(8):
                        g, half = c2 // 2, c2 % 2
                        for mir in range(2):
                            P, P_r = pF[(c2 % 2) * 2 + mir], pF_r[(c2 % 2) * 2 + mir]
                            k = 0
                            for cs in range(2):
                                ti = 0 if cs == 0 else (1 + mir)
                                for dh in range(2):
                                    cx.op("pe", lambda cs=cs, dh=dh, P=P, ti=ti, k=k: nc.tensor.matmul(
                                        P[:, 0:w], lhsT=cd[:, ti, dh, half * 128:(half + 1) * 128],
                                        rhs=YT[:, cs, 2 * g + dh, 0:w], start=(k == 0), stop=(k == 3)),
                                        reads=[cd_r, YT_r[cs][2 * g + dh]], writes=[P_r])
                                    k += 1
                            if mir == 0:
                                lo = st * 256
                                cx.op("act", lambda P=P, c2=c2, lo=lo: nc.scalar.activation(
                                    out=FT[:, c2, lo:lo + w], in_=P[:, 0:w], func=AF.Copy, scale=sc2),
                                    reads=[P_r], writes=[FT_r[c2]])
                            else:
                                i0 = 1 if st == 0 else 0
                                n = 256 - i0
                                t_hi = S - 256 * st - i0
                                dsto = FT[:, c2, t_hi - n + 1:t_hi + 1]
                                cx.op("dve", lambda P=P, dsto=dsto, i0=i0, n=n: nc.vector.tensor_scalar(
                                    out=dsto[:, ::-1], in0=P[:, i0:i0 + n], scalar1=sc2, scalar2=None, op0=ALU.mult),
                                    reads=[P_r], writes=[FT_r[c2]])
                cx.barrier()
            with ExitStack() as es:
                wf = self.sb(es, "wf", [128, 8, D], BF16)
                wf_r = R()
                g2 = self.sb(es, "g2", [128, D], F32)
                g2_r = R()
                cx.dma("pool", wf[:], wf_d, writes=[wf_r])
                self.load_bcast("sp", g2[:], g2_r, self.dram["ln_mix_post"][l])

                def get_lhs(b):
                    return [(FT[:, c, b * 128:(b + 1) * 128], FT_r[c]) for c in range(8)]

                self.proj_post_stage(es, get_lhs, wf, wf_r, g2, g2_r, src, dst, want_next=(dst != "out"))

    def phase_even(self, l, src, dst):
        cx = self.cx
        S = self.S
        with ExitStack() as es1:
            qT = self.sb(es1, "qT", [128, 4, S], BF16)
            kaug = [self.sb(es1, f"kaug{m}", [128, 4, S], BF16) for m in range(2)]
            vaug = self.sb(es1, "vaug", [128, self.NB, 4, VA], BF16)
            qT_r = [Res() for _ in range(self.NT)]
            kaug_r = [Res() for _ in range(self.NT)]
            v_r = [Res() for _ in range(self.NB)]
            with self.nc.named_scope(f"even_proj{l}"):
                self.even_proj(l, src, qT, qT_r, kaug, kaug_r, vaug, v_r)
                cx.barrier()
            with self.nc.named_scope(f"even_attn{l}"):
                self.even_attn(l, qT, qT_r, kaug, kaug_r, vaug, v_r)
                cx.barrier()
        with ExitStack() as es2:
            recT = self.sb(es2, "recT", [128, 4, S], BF16)
            recT_r = [Res() for _ in range(4)]
            with self.nc.named_scope(f"even_lru{l}"):
                self.even_lru(l, recT, recT_r)
                cx.barrier()
            with self.nc.named_scope(f"even_out{l}"):
                self.even_out(l, src, dst, recT, recT_r)
                cx.barrier()

    def even_proj(self, l, src, qT, qT_r, kaug, kaug_r, vaug, v_r):
        nc, cx = self.nc, self.cx
        S, NT, NB = self.S, self.NT, self.NB
        e = l // 2
        R = Res
        win_d = self.dram["w_in"][e].rearrange("(c p) n -> p c n", p=128)
        with ExitStack() as es:
            win = self.sb(es, "win", [128, 8, IN_COLS], BF16)
            win_r = [R() for _ in range(5)]
            g1 = self.sb(es, "g1", [128, D], F32)
            g1_r = R()
            xa = XStream(self, es, "xa", src, 2)
            hn = [self.sb(es, f"hn{i}", [128, D], BF16) for i in range(2)]
            hn_r = [R(), R()]
            hnTs = [self.sb(es, f"hnT{i}", [128, 8, 512], BF16) for i in range(2)]
            hnTs_r = [R(), R()]
            stg = [self.sb(es, f"stg{i}", [128, 512], F32) for i in range(3)]
            stg_r = [R() for _ in range(3)]
            stg_i = [0]
            pT = [self.ps(es, f"pT{i}", [128, D], BF16) for i in range(2)]
            pT_r = [R(), R()]
            pp = [self.ps(es, f"pp{i}", [128, 512], F32) for i in range(6)]
            pp_r = [R() for _ in range(6)]

            self.load_bcast("sp", g1[:], g1_r, self.dram["ln_mix_pre"][l])
            xa.get(0)
            for gi in range(5):
                cx.dma("pool", win[:, :, gi * 512:(gi + 1) * 512], win_d[:, :, gi * 512:(gi + 1) * 512],
                       writes=[win_r[gi]])
            cx.op("dve", lambda: nc.vector.memset(vaug[:, :, :, DV:VA], 1.0), writes=v_r)
            cx.op("pool", lambda: nc.gpsimd.memset(kaug[0][64:128, :, :], 0.0), writes=kaug_r)
            cx.op("pool", lambda: nc.gpsimd.memset(kaug[1][0:64, :, :], 0.0), writes=kaug_r)
            for h in range(NH):
                cx.dma("pool", kaug[0][64:68, h, :], self.dram["c_kaug"], writes=kaug_r)
                cx.dma("pool", kaug[1][0:4, h, :], self.dram["c_kaug"], writes=kaug_r)
            k = 0
            for j in range(4):
                self.pre_hn(xa, j, g1, g1_r, hn[j % 2], hn_r[j % 2])
                self.pre_tr(hn[j % 2], hn_r[j % 2], pT[j % 2], pT_r[j % 2], hnTs[0], hnTs_r[0], j)
            gcount = [0]

            def interleave(t):
                gi_ = gcount[0]
                gcount[0] += 1
                if t + 1 >= NT or gi_ < 2 or gi_ >= 18:
                    return
                j = (gi_ - 2) // 4
                if (gi_ - 2) % 4 == 0:
                    self.pre_hn(xa, 4 * (t + 1) + j, g1, g1_r, hn[j % 2], hn_r[j % 2])
                elif (gi_ - 2) % 4 == 2:
                    self.pre_tr(hn[j % 2], hn_r[j % 2], pT[j % 2], pT_r[j % 2], hnTs[(t + 1) % 2], hnTs_r[(t + 1) % 2], j)

            for t in range(NT):
                hnT, hnT_r = hnTs[t % 2], hnTs_r[t % 2]
                gcount[0] = 0
                tsl = slice(t * 512, (t + 1) * 512)
                for grp in (0, 1, 3, 4):
                    for m in range(4):
                        col = grp * 512 + m * 128
                        P, P_r = pp[k % 6], pp_r[k % 6]
                        for c in range(8):
                            cx.op("pe", lambda c=c, P=P, col=col: nc.tensor.matmul(
                                P[:], lhsT=win[:, c, col:col + 128], rhs=hnT[:, c, :], start=(c == 0), stop=(c == 7)),
                                reads=[win_r[grp], hnT_r], writes=[P_r])
                        if grp == 1:
                            if k % 2 == 0:
                                cx.op("act", lambda P=P, m=m: nc.scalar.copy(out=kaug[0][0:64, m, tsl], in_=P[0:64, :]),
                                      reads=[P_r], writes=[kaug_r[t]])
                                cx.op("act", lambda P=P, m=m: nc.scalar.copy(out=kaug[1][64:128, m, tsl], in_=P[64:128, :]),
                                      reads=[P_r], writes=[kaug_r[t]])
                            else:
                                cx.op("dve", lambda P=P, m=m: nc.vector.tensor_copy(out=kaug[0][0:64, m, tsl], in_=P[0:64, :]),
                                      reads=[P_r], writes=[kaug_r[t]])
                                cx.op("dve", lambda P=P, m=m: nc.vector.tensor_copy(out=kaug[1][64:128, m, tsl], in_=P[64:128, :]),
                                      reads=[P_r], writes=[kaug_r[t]])
                            k += 1
                            interleave(t)
                            continue
                        if grp == 0:
                            dst_ap, dst_r = qT[:, m, tsl], qT_r[t]
                        else:
                            si = stg_i[0] % 3
                            stg_i[0] += 1
                            dst_ap, dst_r = stg[si][:], stg_r[si]
                        if k % 2 == 0:
                            cx.op("act", lambda P=P, dst_ap=dst_ap: nc.scalar.copy(out=dst_ap, in_=P[:]),
                                  reads=[P_r], writes=[dst_r])
                        else:
                            cx.op("dve", lambda P=P, dst_ap=dst_ap: nc.vector.tensor_copy(out=dst_ap, in_=P[:]),
                                  reads=[P_r], writes=[dst_r])
                        if grp >= 3:
                            nm = "xr_s" if grp == 3 else "yg_s"
                            cx.dma("sp", self.dram[nm][m * 128:(m + 1) * 128, tsl], dst_ap,
                                   reads=[dst_r], writes=[self.res[nm][m][t]])
                        k += 1
                        interleave(t)
                for j in range(4):
                    b = 4 * t + j
                    P, P_r = pp[k % 6], pp_r[k % 6]
                    for c in range(8):
                        cx.op("pe", lambda c=c, P=P, j=j: nc.tensor.matmul(
                            P[:], lhsT=hnT[:, c, j * 128:(j + 1) * 128], rhs=win[:, c, 1024:1536],
                            start=(c == 0), stop=(c == 7)),
                            reads=[win_r[2], hnT_r], writes=[P_r])
                    srcv = P[:].rearrange("p (h d) -> p h d", h=4)
                    if k % 2 == 0:
                        cx.op("act", lambda srcv=srcv, b=b: nc.scalar.copy(out=vaug[:, b, :, 0:DV], in_=srcv),
                              reads=[P_r], writes=[v_r[b]])
                    else:
                        cx.op("dve", lambda srcv=srcv, b=b: nc.vector.tensor_copy(out=vaug[:, b, :, 0:DV], in_=srcv),
                              reads=[P_r], writes=[v_r[b]])
                    k += 1
                    interleave(t)

    def even_attn(self, l, qT, qT_r, kaug, kaug_r, vaug, v_r):
        nc, cx = self.nc, self.cx
        S, NT, NB = self.S, self.NT, self.NB
        e = l // 2
        lam_init = 0.8 - 0.6 * math.exp(-0.3 * l)
        R = Res
        NACC = 8 * VA
        with ExitStack() as es:
            pS2 = [self.ps(es, f"pS{i}", [128, 2, 512], F32) for i in range(2)]
            pS2_r = [R(), R()]
            pA = [self.ps(es, f"pA{i}", [128, 512], F32) for i in range(3)]
            pA_r = [R() for _ in range(3)]
            pTr = self.ps(es, "pTr", [128, D], BF16)
            pTr_r = R()
            dist = self.sb(es, "dist", [128, 4, 512], F32)
            dist_r = R()
            lqk = self.sb(es, "lqk", [128, 4, QK], F32)
            lqk_r = R()
            lsm = self.sb(es, "lsm", [128, 8], F32)
            lsm_r = R()
            lam = self.sb(es, "lam", [128, 1], F32)
            lam_r = R()
            gsub = self.sb(es, "gsub", [128, DV], F32)
            gsub_r = R()
            qz = [self.sb(es, f"qz{i}", [128, 2, 2, 512], BF16) for i in range(2)]
            qzd_r = [R(), R()]
            qza_r = [R(), R()]
            PT2 = [self.sb(es, f"PT{i}", [128, 2, 512], BF16) for i in range(3)]
            PT2_r = [R() for _ in range(3)]
            dtmp = [self.sb(es, f"dtmp{i}", [128, 2, 512], F32) for i in range(2)]
            dtmp_r = [R(), R()]
            acc = [self.sb(es, f"acc{i}", [128, NACC], F32) for i in range(2)]
            acc_r = [R(), R()]
            rl = self.sb(es, "rl", [128, 2, 4], F32)
            rl_r = R()
            c1 = self.sb(es, "c1", [128, 4], F32)
            c1_r = R()
            Ot = self.sb(es, "Ot", [128, 4, DV], F32)
            Ot_r = [R() for _ in range(4)]
            tO = self.sb(es, "tO", [128, DV], F32)
            tO_r = R()
            jk = self.sb(es, "jk", [128, DV], F32)
            jk_r = R()
            ssa = self.sb(es, "ssa", [128, 4], F32)
            ssa_r = R()
            rsa = self.sb(es, "rsa", [128, 4], F32)
            rsa_r = R()
            ab = self.sb(es, "ab", [128, 4, DV], BF16)
            ab_r = [R() for _ in range(4)]
            ats = [self.sb(es, f"ats{i}", [128, 512], BF16) for i in range(2)]
            ats_r = [R(), R()]

            cx.dma("sp", dist[:], self.dram["c_dist"], writes=[dist_r])
            for i, nm in enumerate(("lambda_q1", "lambda_k1", "lambda_q2", "lambda_k2")):
                self.load_bcast("sp", lqk[:, i, :], lqk_r, self.dram[nm][e])
            self.load_bcast("sp", gsub[:], gsub_r, self.dram["attn_subln"][e])
            for i in range(2):
                cx.op("pool", lambda i=i: nc.gpsimd.memset(qz[i][:], 0.0), writes=[qzd_r[i], qza_r[i]])
            for i in range(2):
                cx.op("dve", lambda i=i: nc.vector.scalar_tensor_tensor(
                    out=lqk[:, 2 * i, :], in0=lqk[:, 2 * i, :], scalar=1.0, in1=lqk[:, 2 * i + 1, :],
                    op0=ALU.mult, op1=ALU.mult, accum_out=lsm[:, i:i + 1]),
                    reads=[lqk_r], writes=[lqk_r, lsm_r])
            cx.op("act", lambda: nc.scalar.activation(out=lsm[:, 2:4], in_=lsm[:, 0:2], func=AF.Exp),
                  reads=[lsm_r], writes=[lsm_r])
            cx.op("dve", lambda: nc.vector.tensor_tensor(out=lsm[:, 4:5], in0=lsm[:, 2:3], in1=lsm[:, 3:4], op=ALU.subtract),
                  reads=[lsm_r], writes=[lsm_r])
            cx.op("dve", lambda: nc.vector.tensor_scalar(out=lam[:], in0=lsm[:, 4:5], scalar1=lam_init, scalar2=None,
                                                         op0=ALU.add),
                  reads=[lsm_r], writes=[lam_r])
            cx.op("dve", lambda: nc.vector.tensor_scalar(out=gsub[:], in0=gsub[:], scalar1=(1.0 - lam_init), scalar2=None,
                                                         op0=ALU.mult),
                  reads=[gsub_r], writes=[gsub_r])

            pairs = [(h, j) for h in range(NH) for j in range(NT)]

            def build_qz(pi):
                h, j = pairs[pi]
                Z = qz[pi % 2]
                tsl = slice(j * 512, (j + 1) * 512)
                for v in range(2):
                    cx.dma("sp", Z[0:64, 0, v, :], qT[0:64, h, tsl], reads=[qT_r[j]], writes=[qzd_r[pi % 2]])
                    cx.dma("sp", Z[64:128, 1, v, :], qT[64:128, h, tsl], reads=[qT_r[j]], writes=[qzd_r[pi % 2]])
                    cx.dma("sp", Z[64:68, 0, v, :], self.dram["c_qaug"][h, v, :, tsl], writes=[qza_r[pi % 2]])
                    cx.dma("sp", Z[0:4, 1, v, :], self.dram["c_qaug"][h, v, :, tsl], writes=[qza_r[pi % 2]])

            steps = []
            for pi, (h, j) in enumerate(pairs):
                order = [i for i in range(NB) if not (4 * j <= i < 4 * j + 4)] + list(range(4 * j, 4 * j + 4))
                for n, i in enumerate(order):
                    steps.append((pi, i, n == 0, n == NB - 1))

            def emit_qk(s):
                pi, i, first, last = steps[s]
                h, j = pairs[pi]
                Z = qz[pi % 2]
                v = 1 if i >= 4 * j + 4 else 0
                P = pS2[s % 2]
                for m in range(2):
                    cx.op("pe", lambda m=m: nc.tensor.matmul(
                        P[:, m, :], lhsT=kaug[m][:, h, i * 128:(i + 1) * 128], rhs=Z[:, m, v, :], start=True, stop=True),
                        reads=[kaug_r[i // 4], qzd_r[pi % 2], qza_r[pi % 2]], writes=[pS2_r[s % 2]])

            def emit_exp(s):
                pi, i, first, last = steps[s]
                h, j = pairs[pi]
                P, P_r = pS2[s % 2], pS2_r[s % 2]
                T, T_r = PT2[s % 3], PT2_r[s % 3]
                if 4 * j <= i < 4 * j + 4:
                    r = i - 4 * j
                    dt_, dt_r = dtmp[s % 2], dtmp_r[s % 2]
                    for m in range(2):
                        cx.op("dve", lambda m=m: nc.vector.scalar_tensor_tensor(
                            out=dt_[:, m, :], in0=dist[:, r, :], scalar=-8.0 * SLOPES[h], in1=P[:, m, :],
                            op0=ALU.mult, op1=ALU.add),
                            reads=[dist_r, P_r], writes=[dt_r])
                    cx.op("act", lambda: nc.scalar.activation(out=T[:], in_=dt_[:], func=AF.Exp, scale=0.125),
                          reads=[dt_r], writes=[T_r])
                else:
                    cx.op("act", lambda: nc.scalar.activation(out=T[:], in_=P[:], func=AF.Exp, scale=0.125),
                          reads=[P_r], writes=[T_r])

            def emit_pv(s):
                pi, i, first, last = steps[s]
                h, j = pairs[pi]
                T, T_r = PT2[s % 3], PT2_r[s % 3]
                for m in range(2):
                    for qs in range(4):
                        a = m * 4 + qs
                        bk, bk_r = pA[a // 3], pA_r[a // 3]
                        o = (a % 3) * VA
                        cx.op("pe", lambda m=m, qs=qs, bk=bk, o=o, a=a: nc.tensor.matmul(
                            bk[:, o:o + VA], lhsT=T[:, m, qs * 128:(qs + 1) * 128], rhs=vaug[:, i, h, :],
                            start=(first and a % 3 == 0), stop=last, skip_group_check=True),
                            reads=[T_r, v_r[i]], writes=[bk_r])

            def finish_pair(pi):
                h, j = pairs[pi]
                A, A_r = acc[pi % 2], acc_r[pi % 2]
                A4 = A[:].rearrange("p (m q v) -> p m q v", m=2, q=4)
                for bki, n in ((0, 3 * VA), (1, 3 * VA), (2, 2 * VA)):
                    cx.op("dve", lambda bki=bki, n=n: nc.vector.tensor_copy(
                        out=A[:, bki * 3 * VA:bki * 3 * VA + n], in_=pA[bki][:, 0:n]),
                        reads=[pA_r[bki]], writes=[A_r])
                cx.op("dve", lambda: nc.vector.reciprocal(out=rl[:], in_=A4[:, :, :, DV]),
                      reads=[A_r], writes=[rl_r])
                cx.op("dve", lambda: nc.vector.tensor_scalar(out=c1[:], in0=rl[:, 1, :], scalar1=lam[:], scalar2=None,
                                                             op0=ALU.mult),
                      reads=[rl_r, lam_r], writes=[c1_r])
                for qs in range(4):
                    cx.op("dve", lambda qs=qs: nc.vector.tensor_scalar(out=tO[:], in0=A4[:, 1, qs, 0:DV],
                                                                       scalar1=c1[:, qs:qs + 1], scalar2=None, op0=ALU.mult),
                          reads=[A_r, c1_r], writes=[tO_r])
                    cx.op("dve", lambda qs=qs: nc.vector.scalar_tensor_tensor(
                        out=Ot[:, qs, :], in0=A4[:, 0, qs, 0:DV], scalar=rl[:, 0, qs:qs + 1], in1=tO[:],
                        op0=ALU.mult, op1=ALU.subtract),
                        reads=[A_r, rl_r, tO_r], writes=[Ot_r[qs]])
                    cx.op("dve", lambda qs=qs: nc.vector.scalar_tensor_tensor(
                        out=jk[:], in0=Ot[:, qs, :], scalar=1.0, in1=Ot[:, qs, :], op0=ALU.mult, op1=ALU.mult,
                        accum_out=ssa[:, qs:qs + 1]),
                        reads=[Ot_r[qs]], writes=[jk_r, ssa_r])
                self.rstd_from_ss(ssa[:], rsa[:], ssa_r, rsa_r, DV)
                for qs in range(4):
                    cx.op("dve", lambda qs=qs: nc.vector.scalar_tensor_tensor(
                        out=ab[:, qs, :], in0=Ot[:, qs, :], scalar=rsa[:, qs:qs + 1], in1=gsub[:],
                        op0=ALU.mult, op1=ALU.mult),
                        reads=[Ot_r[qs], rsa_r, gsub_r], writes=[ab_r[qs]])

            def finish_pair_b(pi):
                h, j = pairs[pi]
                for qs in range(4):
                    cx.op("pe", lambda qs=qs: nc.tensor.transpose(
                        out=pTr[:, qs * 128:(qs + 1) * 128], in_=ab[:, qs, :], identity=self.ident[:]),
                        reads=[ab_r[qs], self.ident_r], writes=[pTr_r])
                st_, st_r = ats[pi % 2], ats_r[pi % 2]
                cx.op("dve", lambda: nc.vector.tensor_copy(out=st_[:], in_=pTr[:, 0:512]),
                      reads=[pTr_r], writes=[st_r])
                cx.dma("sp", self.dram["attn_s"][h, :, j * 512:(j + 1) * 512], st_[:], reads=[st_r],
                       writes=[self.res["attn_s"][h][j]])

            build_qz(0)
            emit_qk(0)
            nst = len(steps)
            if nst > 1:
                emit_qk(1)
            pending = None
            for s in range(nst):
                pi, i, first, last = steps[s]
                if first and pi + 1 < len(pairs):
                    build_qz(pi + 1)
                emit_exp(s)
                if s + 2 < nst:
                    emit_qk(s + 2)
                emit_pv(s)
                if pending is not None and s >= pending[1]:
                    finish_pair_b(pending[0])
                    pending = None
                if last:
                    finish_pair(pi)
                    pending = (pi, s + min(12, NB // 2))
            if pending is not None:
                finish_pair_b(pending[0])

    def even_lru(self, l, recT, recT_r):
        nc, cx = self.nc, self.cx
        S, NT = self.S, self.NT
        e = l // 2
        R = Res
        with ExitStack() as es:
            xr = self.sb(es, "xr", [128, S + 4], F32)
            xcs = [self.sb(es, f"xc{i}", [128, S], F32) for i in range(2)]
            xcb = self.sb(es, "xcb", [128, S], BF16)
            yg = self.sb(es, "yg", [128, S], F32)
            ra = [self.sb(es, f"ra{d}", [128, S], F32) for d in range(2)]
            iu = [self.sb(es, f"iu{d}", [128, S], F32) for d in range(2)]
            t1 = [self.sb(es, f"t1{d}", [128, S], F32) for d in range(2)]
            hh = t1
            xr_r, yg_r = R(), R()
            xcs_r = [R(), R()]
            xcb_r = R()
            ra_r, iu_r, t1_r = ([R(), R()] for _ in range(3))
            hh_r = t1_r
            wai = self.sb(es, "wai", [128, 2, 2, 4, 128], BF16)
            wai_r = R()
            cw = self.sb(es, "cw", [128, 4, 4], F32)
            cb = self.sb(es, "cb", [128, 4], F32)
            bai = self.sb(es, "bai", [128, 2, 2, 4], F32)
            lml = self.sb(es, "lml", [128, 2, 2, 4], F32)
            prm_r = R()
            pG = [self.ps(es, f"pG{i}", [128, 512], F32) for i in range(4)]
            pG_r = [R() for _ in range(4)]
            with nc.allow_non_contiguous_dma(reason="tiny per-channel parameter vectors"):
                def ldc(dst_ap, src_vec):
                    cx.dma("sp", dst_ap, src_vec.rearrange("(c p) -> p c", p=128), writes=[prm_r])
                for tap in range(4):
                    ldc(cw[:, :, tap], self.dram["conv_w"][e][tap])
                ldc(cb[:], self.dram["conv_b"][e])
                for d in range(2):
                    ldc(bai[:, 0, d, :], self.dram["lru_b_a"][e][d])
                    ldc(bai[:, 1, d, :], self.dram["lru_b_i"][e][d])
                    ldc(lml[:, 0, d, :], self.dram["lru_lambda"][e][d])
            for d in range(2):
                cx.dma("pool", wai[:, 0, d], self.dram["lru_w_a"][e][d].rearrange("h i j -> i h j"), writes=[wai_r])
                cx.dma("pool", wai[:, 1, d], self.dram["lru_w_i"][e][d].rearrange("h i j -> i h j"), writes=[wai_r])
            cx.op("act", lambda: nc.scalar.activation(out=lml[:, 0], in_=lml[:, 0], func=AF.Exp, scale=-1.0),
                  reads=[prm_r], writes=[prm_r])
            cx.op("act", lambda: nc.scalar.activation(out=lml[:, 0], in_=lml[:, 0], func=AF.Ln, bias=1.0),
                  reads=[prm_r], writes=[prm_r])
            cx.op("dve", lambda: nc.vector.tensor_scalar(out=lml[:, 1], in0=lml[:, 0], scalar1=-16.0, scalar2=None, op0=ALU.mult),
                  reads=[prm_r], writes=[prm_r])
            cx.op("dve", lambda: nc.vector.tensor_scalar(out=lml[:, 0], in0=lml[:, 0], scalar1=-8.0, scalar2=None, op0=ALU.mult),
                  reads=[prm_r], writes=[prm_r])
            cx.op("dve", lambda: nc.vector.memset(xr[:, 0:2], 0.0), writes=[xr_r])
            cx.op("dve", lambda: nc.vector.memset(xr[:, S + 2:S + 4], 0.0), writes=[xr_r])
            k = 0

            def conv_chunk(c):
                xc, xc_r = xcs[c % 2], xcs_r[c % 2]
                rows = slice(c * 128, (c + 1) * 128)
                cx.dma("sp", xr[:, 2:S + 2], self.dram["xr_s"][rows, :], reads=self.res["xr_s"][c], writes=[xr_r])
                cx.op("act", lambda: nc.scalar.activation(out=xc[:], in_=xr[:, 2:S + 2], func=AF.Identity,
                                                          scale=cw[:, c, 2:3], bias=cb[:, c:c + 1]),
                      reads=[xr_r, prm_r], writes=[xc_r])
                for tap, off in ((0, 0), (1, 1), (3, 3)):
                    cx.op("dve", lambda tap=tap, off=off: nc.vector.scalar_tensor_tensor(
                        out=xc[:], in0=xr[:, off:off + S], scalar=cw[:, c, tap:tap + 1], in1=xc[:],
                        op0=ALU.mult, op1=ALU.add),
                        reads=[xr_r, prm_r, xc_r], writes=[xc_r])

            def cast_chunk(c):
                cx.op("act", lambda: nc.scalar.copy(out=xcb[:], in_=xcs[c % 2][:]), reads=[xcs_r[c % 2]], writes=[xcb_r])

            conv_chunk(0)
            cast_chunk(0)
            for c in range(4):
                xc, xc_r = xcs[c % 2], xcs_r[c % 2]
                rows = slice(c * 128, (c + 1) * 128)
                cx.dma("sp", yg[:], self.dram["yg_s"][rows, :], reads=self.res["yg_s"][c], writes=[yg_r])
                if c + 1 < 4:
                    conv_chunk(c + 1)
                for d in range(2):
                    for n in range(NT):
                        tsl = slice(n * 512, (n + 1) * 512)
                        for ai, (dstt, dstt_r) in enumerate(((ra[d], ra_r[d]), (iu[d], iu_r[d]))):
                            P, P_r = pG[k % 4], pG_r[k % 4]
                            k += 1
                            cx.op("pe", lambda P=P, ai=ai, d=d, tsl=tsl: nc.tensor.matmul(
                                P[:], lhsT=wai[:, ai, d, c, :], rhs=xcb[:, tsl], start=True, stop=True),
                                reads=[wai_r, xcb_r], writes=[P_r])
                            cx.op("act", lambda P=P, ai=ai, dstt=dstt, d=d, tsl=tsl: nc.scalar.activation(
                                out=dstt[:, tsl], in_=P[:], func=AF.Sigmoid, bias=bai[:, ai, d, c:c + 1]),
                                reads=[P_r, prm_r], writes=[dstt_r])
                    cx.op("act", lambda d=d: nc.scalar.activation(out=t1[d][:], in_=ra[d][:], func=AF.Exp,
                                                                  scale=lml[:, 1, d, c:c + 1]),
                          reads=[ra_r[d], prm_r], writes=[t1_r[d]])
                    cx.op("act", lambda d=d: nc.scalar.activation(out=t1[d][:], in_=t1[d][:], func=AF.Sqrt, scale=-1.0, bias=1.0),
                          reads=[t1_r[d]], writes=[t1_r[d]])
                    cx.op("act", lambda d=d: nc.scalar.activation(out=ra[d][:], in_=ra[d][:], func=AF.Exp,
                                                                  scale=lml[:, 0, d, c:c + 1]),
                          reads=[ra_r[d], prm_r], writes=[ra_r[d]])
                    cx.op("dve", lambda d=d: nc.vector.tensor_tensor(out=iu[d][:], in0=iu[d][:], in1=xc[:], op=ALU.mult),
                          reads=[iu_r[d], xc_r], writes=[iu_r[d]])
                if c + 1 < 4:
                    cast_chunk(c + 1)
                cx.op("act", lambda: nc.scalar.activation(out=yg[:], in_=yg[:], func=AF.Gelu_apprx_tanh),
                      reads=[yg_r], writes=[yg_r])
                for d in range(2):
                    cx.op("dve", lambda d=d: nc.vector.tensor_tensor(out=iu[d][:], in0=iu[d][:], in1=t1[d][:], op=ALU.mult),
                          reads=[iu_r[d], t1_r[d]], writes=[iu_r[d]])
                    if d == 0:
                        cx.op("dve", lambda: nc.vector.tensor_tensor_scan(
                            out=hh[0][:], data0=ra[0][:], data1=iu[0][:], initial=0.0, op0=ALU.mult, op1=ALU.add),
                            reads=[ra_r[0], iu_r[0]], writes=[hh_r[0]])
                    else:
                        cx.op("dve", lambda: nc.vector.tensor_tensor_scan(
                            out=hh[1][:, ::-1], data0=ra[1][:, ::-1], data1=iu[1][:, ::-1], initial=0.0,
                            op0=ALU.mult, op1=ALU.add),
                            reads=[ra_r[1], iu_r[1]], writes=[hh_r[1]])
                cx.op("dve", lambda: nc.vector.tensor_tensor(out=hh[0][:], in0=hh[0][:], in1=hh[1][:], op=ALU.add),
                      reads=[hh_r[0], hh_r[1]], writes=[hh_r[0]])
                cx.op("dve", lambda: nc.vector.tensor_tensor(out=recT[:, c, :], in0=hh[0][:], in1=yg[:], op=ALU.mult),
                      reads=[hh_r[0], yg_r], writes=[recT_r[c]])

    def even_out(self, l, src, dst, recT, recT_r):
        nc, cx = self.nc, self.cx
        S = self.S
        e = l // 2
        R = Res
        wo_d = self.dram["w_mix_out"][e].rearrange("(c p) n -> p c n", p=128)
        with ExitStack() as es:
            wo = self.sb(es, "wo", [128, 8, D], BF16)
            wo_r = R()
            attnT = self.sb(es, "attnT", [128, 4, S], BF16)
            attnT_r = [R() for _ in range(4)]
            g2 = self.sb(es, "g2", [128, D], F32)
            g2_r = R()
            cx.dma("pool", wo[:], wo_d, writes=[wo_r])
            for h in range(4):
                cx.dma("sp", attnT[:, h, :], self.dram["attn_s"][h], reads=self.res["attn_s"][h], writes=[attnT_r[h]])
            self.load_bcast("sp", g2[:], g2_r, self.dram["ln_mix_post"][l])

            def get_lhs(b):
                bsl = slice(b * 128, (b + 1) * 128)
                return ([(attnT[:, c, bsl], attnT_r[c]) for c in range(4)]
                        + [(recT[:, c, bsl], recT_r[c]) for c in range(4)])

            self.proj_post_stage(es, get_lhs, wo, wo_r, g2, g2_r, src, dst, want_next=(dst != "out"))


class XStream:
    def __init__(self, prog, es, name, src, nbuf):
        self.p = prog
        self.src = src
        self.view = prog.dram[src].rearrange("(b p) d -> b p d", p=128)
        self.bufs = [prog.sb(es, f"{name}{i}", [128, D], F32) for i in range(nbuf)]
        self.res = [Res() for _ in range(nbuf)]
        self.loaded = set()
        self.n = nbuf

    def _load(self, b):
        i = b % self.n
        self.p.cx.dma("sp", self.bufs[i][:], self.view[b], reads=[self.p.res[self.src][b]], writes=[self.res[i]])
        self.loaded.add(b)

    def get(self, b):
        for bb in range(b, min(b + self.n, self.p.NB)):
            if bb not in self.loaded:
                self._load(bb)
        i = b % self.n
        return self.bufs[i][:], self.res[i]


ALIBI_COLS = 512


def _consts(S):
    c = {}
    bf = ml_dtypes.bfloat16
    c["c_ident"] = np.eye(128, dtype=np.float32).astype(bf)
    p = np.arange(128)[:, None, None]
    r = np.arange(4)[None, :, None]
    q = np.arange(512)[None, None, :]
    c["c_dist"] = (2.0 * np.maximum((128 * r + p) - q, 0)).astype(np.float32)
    pos = np.arange(S)
    hi, lo = (pos // 64).astype(np.float64), (pos % 64).astype(np.float64)
    c["c_kaug"] = np.stack([np.ones(S), np.ones(S), 64.0 * hi, lo]).astype(np.float32).astype(bf)
    qa = np.zeros((NH, 2, 4, S), np.float64)
    for h in range(NH):
        s8 = 8.0 * SLOPES[h]
        left = np.stack([-s8 * 64.0 * hi, -s8 * lo, s8 * np.ones(S), s8 * np.ones(S)])
        qa[h, 0] = left
        qa[h, 1] = -left
    c["c_qaug"] = qa.astype(np.float32).astype(bf)
    NB, NTH = S // 128, S // 512
    s_in = (np.arange(NB)[None, :, None] * 128 + np.arange(128)[:, None, None]).astype(np.int64)
    tc = np.empty((NTH, 128, NB, 257), bf)
    ts = np.empty((NTH, 128, NB, 257), bf)
    for st in range(NTH):
        s_out = st * 256 + np.arange(257, dtype=np.int64)[None, None, :]
        ang = (2.0 * np.pi / S) * ((s_in * s_out) % S).astype(np.float64)
        tc[st] = np.cos(ang).astype(np.float32).astype(bf)
        sn = np.sin(ang)
        sn[np.broadcast_to(((s_in * s_out) % S) * 2 % S == 0, sn.shape)] = 0.0
        ts[st] = sn.astype(np.float32).astype(bf)
    c["c_dftc"] = tc
    c["c_dfts"] = ts
    d_in = (np.arange(2)[None, :, None] * 128 + np.arange(128)[:, None, None]).astype(np.int64)
    d_out = np.arange(256, dtype=np.int64)[None, None, :]
    ang = (2.0 * np.pi / 256) * ((d_in * d_out) % 256).astype(np.float64)
    cdt = np.stack([np.cos(ang), -np.sin(ang), np.sin(ang)], axis=1)
    c["c_cdft"] = cdt.astype(np.float32).astype(bf)
    return c


def _alibi_table():
    t = np.zeros((128, ALIBI_COLS), np.float64)
    p = np.arange(128, dtype=np.float64)
    for h in range(NH):
        sl = SLOPES[h]
        for dl in range(1, 32):
            t[:, h * 32 + dl] = sl * (p - 128.0 * dl)
        for dr in range(1, 32):
            t[:, 128 + h * 32 + dr] = -sl * (128.0 * dr + p - 127.0)
        for qs in range(4):
            t[:, 256 + qs * 4 + h] = -sl * (128.0 * qs + p)
            t[:, 272 + qs * 4 + h] = -sl * (511.0 - 128.0 * qs - p)
    return t.astype(np.float32)


_CACHE = {}


def run(inputs_per_core, S, plan, trace=False):
    key = (S, tuple(plan))
    if key not in _CACHE:
        _CACHE[key] = Prog(S, plan)
        _CACHE[key].build()
    prog = _CACHE[key]
    consts = _consts(S)
    in_maps = []
    for m in inputs_per_core:
        mm = dict(consts)
        mm.update(m)
        in_maps.append(mm)
    res = run_bass_kernel_spmd(prog.nc, in_maps, core_ids=list(range(len(in_maps))), trace=trace)
    return res


def full_plan():
    plan = []
    for l in range(DEPTH):
        src = "x" if l == 0 else "xs"
        if l % 2 == 0:
            plan.append(("even", l, src, "xs"))
        else:
            plan.append(("fnet", l, src, "xs"))
        plan.append(("ffn", l, "xs", "out" if l == DEPTH - 1 else "xs"))
    return plan


def kernel(**inputs):
    x = np.ascontiguousarray(inputs["x"], dtype=np.float32)
    B, S, _ = x.shape
    shared = {k: np.ascontiguousarray(v, dtype=np.float32) for k, v in inputs.items() if k != "x"}
    per_core = []
    for b in range(B):
        m = dict(shared)
        m["x"] = x[b]
        per_core.append(m)
    res = run(per_core, S, full_plan())
    return np.stack([r["out"] for r in res.results], axis=0).astype(np.float32)
```

```python
import math
from contextlib import ExitStack

import numpy as np
import ml_dtypes

import concourse.bass as bass
import concourse.mybir as mybir
from concourse.bass_utils import run_bass_kernel_spmd

F32 = mybir.dt.float32
BF16 = mybir.dt.bfloat16
AF = mybir.ActivationFunctionType
ALU = mybir.AluOpType

D = 1024
DEPTH = 4
NH = 4
QK = 64
DV = 128
LRU_W = 512
IN_COLS = 2560
D_FF = 2816
NF = D_FF // 128
EPS = 1e-6
SLOPES = [2.0 ** (-8.0 * (h + 1) / NH) for h in range(NH)]
VA = DV + 1


class Res:
    __slots__ = ("w", "r")

    def __init__(self):
        self.w = {}
        self.r = {}


def _merge(dst, toks):
    for k, (sem, val) in toks.items():
        cur = dst.get(k)
        if cur is None or cur[1] < val:
            dst[k] = (sem, val)


class Ctx:
    def __init__(self, nc):
        self.nc = nc
        self.es = ExitStack()
        self.eng = {}
        for name, e in (("pe", nc.tensor), ("act", nc.scalar), ("dve", nc.vector),
                        ("pool", nc.gpsimd), ("sp", nc.sync)):
            sem = self.es.enter_context(nc.semaphore("sem_" + name))
            self.eng[name] = {"e": e, "sem": sem, "cnt": 0, "waited": {}, "key": "E" + name}
        self.dq = {}
        for q, n in (("sp", 24), ("pool", 12)):
            pool = []
            for i in range(n):
                sem = self.es.enter_context(nc.semaphore(f"dq_{q}_{i}"))
                pool.append([sem, 0, f"Q{q}{i}"])
            self.dq[q] = {"pool": pool, "idx": 0}
        self.n_inst = 0

    def _wait(self, E, key, sem, val):
        if E["waited"].get(key, 0) >= val:
            return
        E["e"].wait_ge(sem, val)
        E["waited"][key] = val
        self.n_inst += 1

    def _deps(self, E, reads, writes, skip_key=None):
        need = {}
        for r in reads:
            _merge(need, r.w)
        for w in writes:
            _merge(need, w.w)
            _merge(need, w.r)
        for k, (sem, val) in need.items():
            if k == skip_key:
                continue
            self._wait(E, k, sem, val)

    def _commit(self, key, tok, reads, writes):
        for r in reads:
            cur = r.r.get(key)
            if cur is None or cur[1] < tok[1]:
                r.r[key] = tok
        for w in writes:
            w.w = {key: tok}
            w.r = {}

    def op(self, eng, fn, reads=(), writes=()):
        E = self.eng[eng]
        self._deps(E, reads, writes, skip_key=E["key"] if eng == "pe" else None)
        ins = fn()
        E["cnt"] += 1
        ins.then_inc(E["sem"], 1)
        tok = (E["sem"], E["cnt"])
        self._commit(E["key"], tok, reads, writes)
        self.n_inst += 1
        return tok

    def dma(self, q, out, in_, reads=(), writes=()):
        E = self.eng[q]
        Q = self.dq[q]
        slot = Q["pool"][Q["idx"]]
        Q["idx"] = (Q["idx"] + 1) % len(Q["pool"])
        sem, cnt, key = slot
        if cnt > 0:
            self._wait(E, key, sem, cnt)
        self._deps(E, reads, writes)
        E["e"].dma_start(out=out, in_=in_).then_inc(sem, 16)
        slot[1] = cnt + 16
        tok = (sem, cnt + 16)
        self._commit(key, tok, reads, writes)
        self.n_inst += 1
        return tok

    def barrier(self, skip=()):
        toks = {}
        for name, E in self.eng.items():
            if E["cnt"] > 0:
                toks[E["key"]] = (E["sem"], E["cnt"])
        for q, Q in self.dq.items():
            for sem, cnt, key in Q["pool"]:
                if cnt > 0:
                    toks[key] = (sem, cnt)
        for name, E in self.eng.items():
            if name in skip:
                continue
            for k, (sem, val) in toks.items():
                if k == E["key"]:
                    continue
                self._wait(E, k, sem, val)
        for name in ("act", "dve", "pool"):
            E = self.eng[name]
            if name in skip:
                continue
            if E["cnt"] > 0:
                self._wait(E, E["key"], E["sem"], E["cnt"])


class Prog:
    def __init__(self, S, plan):
        self.S = S
        self.NB = S // 128
        self.NT = S // 512
        self.plan = plan
        nc = bass.Bass("TRN2", target_bir_lowering=False)
        self.nc = nc
        self.cx = Ctx(nc)
        self.dram = {}
        self._declare()

    def _dt(self, name, shape, dtype, kind):
        t = self.nc.dram_tensor(name, list(shape), dtype, kind=kind).ap()
        self.dram[name] = t
        return t

    def _declare(self):
        S = self.S
        ne, no = 2, 2
        I = "ExternalInput"
        self._dt("x", [S, D], F32, I)
        for n in ("ln_mix_pre", "ln_mix_post", "ln_ffn_pre", "ln_ffn_post"):
            self._dt(n, [DEPTH, D], F32, I)
        self._dt("w_in", [ne, D, IN_COLS], F32, I)
        self._dt("w_mix_out", [ne, D, D], F32, I)
        for n in ("lambda_q1", "lambda_k1", "lambda_q2", "lambda_k2"):
            self._dt(n, [ne, QK], F32, I)
        self._dt("attn_subln", [ne, DV], F32, I)
        self._dt("conv_w", [ne, 4, LRU_W], F32, I)
        self._dt("conv_b", [ne, LRU_W], F32, I)
        self._dt("lru_w_a", [ne, 2, 4, 128, 128], F32, I)
        self._dt("lru_b_a", [ne, 2, LRU_W], F32, I)
        self._dt("lru_w_i", [ne, 2, 4, 128, 128], F32, I)
        self._dt("lru_b_i", [ne, 2, LRU_W], F32, I)
        self._dt("lru_lambda", [ne, 2, LRU_W], F32, I)
        self._dt("w_fourier_out", [no, D, D], F32, I)
        self._dt("w_ffn_gate", [DEPTH, D, D_FF], F32, I)
        self._dt("w_ffn_up", [DEPTH, D, D_FF], F32, I)
        self._dt("w_ffn_down", [DEPTH, D_FF, D], F32, I)
        self._dt("c_ident", [128, 128], BF16, I)
        self._dt("c_qaug", [NH, 2, 4, S], BF16, I)
        self._dt("c_kaug", [4, S], BF16, I)
        self._dt("c_dist", [128, 4, 512], F32, I)
        self._dt("c_dftc", [S // 512, 128, S // 128, 257], BF16, I)
        self._dt("c_dfts", [S // 512, 128, S // 128, 257], BF16, I)
        self._dt("c_cdft", [128, 3, 2, 256], BF16, I)
        self._dt("out", [S, D], F32, "ExternalOutput")
        self._dt("xs", [S, D], F32, "Internal")
        self._dt("xr_s", [LRU_W, S], F32, "Internal")
        self._dt("yg_s", [LRU_W, S], F32, "Internal")
        self._dt("attn_s", [4, 128, S], BF16, "Internal")
        self.res = {}
        for n in ("x", "out", "xs"):
            self.res[n] = [Res() for _ in range(self.NB)]
        self.res["xr_s"] = [[Res() for _ in range(self.NT)] for _ in range(4)]
        self.res["yg_s"] = [[Res() for _ in range(self.NT)] for _ in range(4)]
        self.res["attn_s"] = [[Res() for _ in range(self.NT)] for _ in range(4)]

    def sb(self, es, name, shape, dtype):
        self._uid = getattr(self, "_uid", 0) + 1
        return es.enter_context(self.nc.sbuf_tensor(f"{name}_{self._uid}", list(shape), dtype))

    def ps(self, es, name, shape, dtype):
        self._uid = getattr(self, "_uid", 0) + 1
        return es.enter_context(self.nc.psum_tensor(f"{name}_{self._uid}", list(shape), dtype))

    def load_bcast(self, q, dst, dst_res, src_row):
        self.cx.dma(q, dst, src_row.partition_broadcast(128), writes=[dst_res])

    def rstd_from_ss(self, ss, rstd, ss_res, rstd_res, n):
        v = self.nc.vector
        cx = self.cx
        w = ss.shape[-1]
        sv = self.rs_scr[:, 0:w]
        st = self.rs_scr[:, 8:8 + w]
        R = self.rs_res
        I32 = mybir.dt.int32
        cx.op("dve", lambda: v.tensor_scalar(out=sv, in0=ss, scalar1=1.0 / n, scalar2=EPS,
                                             op0=ALU.mult, op1=ALU.add),
              reads=[ss_res], writes=[R])
        cx.op("dve", lambda: v.tensor_scalar(out=st.bitcast(I32), in0=sv.bitcast(I32), scalar1=1, scalar2=None,
                                             op0=ALU.logical_shift_right),
              reads=[R], writes=[R])
        cx.op("dve", lambda: v.tensor_scalar(out=rstd.bitcast(I32), in0=st.bitcast(I32), scalar1=-1.0,
                                             scalar2=float(0x5f3759df), op0=ALU.mult, op1=ALU.add),
              reads=[R], writes=[rstd_res])
        for _ in range(3):
            cx.op("dve", lambda: v.tensor_tensor(out=st, in0=rstd, in1=rstd, op=ALU.mult),
                  reads=[rstd_res], writes=[R])
            cx.op("dve", lambda: v.scalar_tensor_tensor(out=st, in0=st, scalar=-0.5, in1=sv, op0=ALU.mult, op1=ALU.mult),
                  reads=[R], writes=[R])
            cx.op("dve", lambda: v.scalar_tensor_tensor(out=rstd, in0=st, scalar=1.5, in1=rstd, op0=ALU.add, op1=ALU.mult),
                  reads=[R, rstd_res], writes=[rstd_res])

    def build(self):
        nc, cx = self.nc, self.cx
        with ExitStack() as es:
            self.ident = self.sb(es, "ident", [128, 128], BF16)
            self.ident_r = Res()
            self.rs_scr = self.sb(es, "rs_scr", [128, 16], F32)
            self.rs_res = Res()
            self.ssn = self.sb(es, "ssn", [128, self.NB], F32)
            self.rstdn = self.sb(es, "rstdn", [128, self.NB], F32)
            self.ssn_r = [Res() for _ in range(self.NT)]
            self.rstdn_r = [Res() for _ in range(self.NT)]
            with nc.named_scope("prepass"):
                self.stats_prepass(self.plan[0][2])
                pool_free = False
                cx.barrier(skip=("pool",) if pool_free else ())
            cx.dma("sp", self.ident[:], self.dram["c_ident"], writes=[self.ident_r])
            for ph in self.plan:
                kind = ph[0]
                if kind == "ffn":
                    with nc.named_scope(f"ffn{ph[1]}"):
                        self.phase_ffn(*ph[1:])
                elif kind == "fnet":
                    with nc.named_scope(f"fnet{ph[1]}"):
                        self.phase_fnet(*ph[1:])
                elif kind == "even":
                    self.phase_even(*ph[1:])
                else:
                    raise ValueError(kind)
                cx.barrier()
        self.cx.es.close()
        return nc

    def prenorm_T(self, xstream, b0, g_bc, g_r, hn, hn_r, ss, ss_r, rstd, rstd_r, pT, pT_r, hnT, hnT_r, nj=4):
        nc, cx = self.nc, self.cx

        def emit_hn(j):
            b = b0 + j
            xj, xj_r = xstream.get(b)
            H, H_r = hn[j % 2], hn_r[j % 2]
            cx.op("dve", lambda: nc.vector.scalar_tensor_tensor(
                out=H[:], in0=xj, scalar=self.rstdn[:, b:b + 1], in1=g_bc[:],
                op0=ALU.mult, op1=ALU.mult),
                reads=[xj_r, self.rstdn_r[b // 4], g_r], writes=[H_r])

        emit_hn(0)
        for j in range(nj):
            H, H_r = hn[j % 2], hn_r[j % 2]
            P, P_r = pT[j % 2], pT_r[j % 2]
            for c in range(8):
                cx.op("pe", lambda c=c, H=H, P=P: nc.tensor.transpose(
                    out=P[:, c * 128:(c + 1) * 128], in_=H[:, c * 128:(c + 1) * 128], identity=self.ident[:]),
                    reads=[H_r, self.ident_r], writes=[P_r])
            if j + 1 < nj:
                emit_hn(j + 1)
            src = P[:].rearrange("p (c t) -> p c t", c=8)
            dst = hnT[:, :, j * 128:(j + 1) * 128]
            cx.op("act", lambda src=src, dst=dst: nc.scalar.copy(out=dst, in_=src), reads=[P_r], writes=[hnT_r])

    def pre_hn(self, xstream, b, g_bc, g_r, H, H_r):
        nc, cx = self.nc, self.cx
        xj, xj_r = xstream.get(b)
        cx.op("dve", lambda: nc.vector.scalar_tensor_tensor(
            out=H[:], in0=xj, scalar=self.rstdn[:, b:b + 1], in1=g_bc[:], op0=ALU.mult, op1=ALU.mult),
            reads=[xj_r, self.rstdn_r[b // 4], g_r], writes=[H_r])

    def pre_tr(self, H, H_r, P, P_r, hnT, hnT_r, j):
        nc, cx = self.nc, self.cx
        for c in range(8):
            cx.op("pe", lambda c=c: nc.tensor.transpose(
                out=P[:, c * 128:(c + 1) * 128], in_=H[:, c * 128:(c + 1) * 128], identity=self.ident[:]),
                reads=[H_r, self.ident_r], writes=[P_r])
        src = P[:].rearrange("p (c t) -> p c t", c=8)
        dst = hnT[:, :, j * 128:(j + 1) * 128]
        cx.op("act", lambda: nc.scalar.copy(out=dst, in_=src), reads=[P_r], writes=[hnT_r])

    def next_stats(self, xj, xj_r, junk, junk_r, b):
        nc, cx = self.nc, self.cx
        g = b // 4
        cx.op("act", lambda: nc.scalar.activation(out=junk[:], in_=xj, func=AF.Square, accum_out=self.ssn[:, b:b + 1]),
              reads=[xj_r], writes=[junk_r, self.ssn_r[g]])
        if b % 4 == 3:
            self.rstd_from_ss(self.ssn[:, 4 * g:4 * g + 4], self.rstdn[:, 4 * g:4 * g + 4], self.ssn_r[g], self.rstdn_r[g], D)

    def postnorm_residual(self, ps_halves, ps_res, xj, xj_r, g_bc, g_r, junk, junk_r, ss2, ss2_r, rstd1, rstd1_r,
                          tmp, tmp_r, b=None, want_next=True):
        nc, cx = self.nc, self.cx
        for hf in range(2):
            cx.op("dve", lambda hf=hf: nc.vector.tensor_copy(out=tmp[:, hf * 512:(hf + 1) * 512], in_=ps_halves[hf]),
                  reads=[ps_res[hf]], writes=[tmp_r])
        cx.op("act", lambda: nc.scalar.activation(out=junk[:], in_=tmp[:], func=AF.Square, accum_out=ss2[:, 2:3]),
              reads=[tmp_r], writes=[junk_r, ss2_r])
        self.rstd_from_ss(ss2[:, 2:3], rstd1[:], ss2_r, rstd1_r, D)
        cx.op("dve", lambda: nc.vector.scalar_tensor_tensor(
            out=tmp[:], in0=tmp[:], scalar=rstd1[:], in1=g_bc[:], op0=ALU.mult, op1=ALU.mult),
            reads=[tmp_r, rstd1_r, g_r], writes=[tmp_r])
        cx.op("dve", lambda: nc.vector.tensor_tensor(out=xj, in0=tmp[:], in1=xj, op=ALU.add),
              reads=[tmp_r, xj_r], writes=[xj_r])
        if want_next and b is not None:
            self.next_stats(xj, xj_r, junk, junk_r, b)

    def proj_post_stage(self, es, get_lhs, w, w_r, g2, g2_r, src, dst, want_next):
        nc, cx = self.nc, self.cx
        NB = self.NB
        R = Res
        dstv = self.dram[dst].rearrange("(b p) d -> b p d", p=128)
        xb = XStream(self, es, "xb", src, 4)
        Ys = [self.sb(es, f"Y{i}", [128, 4, D], F32) for i in range(2)]
        Ys_r = [[R() for _ in range(4)] for _ in range(2)]
        junk = self.sb(es, "junk", [128, D], BF16)
        junk_r = R()
        junk2 = self.sb(es, "junk2", [128, D], BF16)
        junk2_r = R()
        ssx = [self.sb(es, f"ssx{i}", [128, 8], F32) for i in range(2)]
        ssx_r = [R(), R()]
        rsx = self.sb(es, "rsx", [128, 8], F32)
        rsx_r = R()
        py = [self.ps(es, f"py{i}", [128, 512], F32) for i in range(4)]
        py_r = [R() for _ in range(4)]
        NG = NB // 4

        def front(g):
            Y, Y_r = Ys[g % 2], Ys_r[g % 2]
            for jj in range(4):
                b = 4 * g + jj
                pr = [py[(b % 2) * 2], py[(b % 2) * 2 + 1]]
                pr_r = [py_r[(b % 2) * 2], py_r[(b % 2) * 2 + 1]]
                lhs = get_lhs(b)
                for hf in range(2):
                    for c in range(8):
                        la, lr = lhs[c]
                        cx.op("pe", lambda la=la, c=c, hf=hf, pr=pr: nc.tensor.matmul(
                            pr[hf][:], lhsT=la, rhs=w[:, c, hf * 512:(hf + 1) * 512], start=(c == 0), stop=(c == 7)),
                            reads=[lr, w_r], writes=[pr_r[hf]])
                for hf in range(2):
                    cx.op("act", lambda hf=hf, jj=jj, pr=pr, Y=Y: nc.scalar.copy(
                        out=Y[:, jj, hf * 512:(hf + 1) * 512], in_=pr[hf][:]),
                        reads=[pr_r[hf]], writes=[Y_r[jj]])
                cx.op("act", lambda jj=jj, Y=Y, g=g: nc.scalar.activation(
                    out=junk[:], in_=Y[:, jj, :], func=AF.Square, accum_out=ssx[g % 2][:, jj:jj + 1]),
                    reads=[Y_r[jj]], writes=[junk_r, ssx_r[g % 2]])

        def back(g):
            Y, Y_r = Ys[g % 2], Ys_r[g % 2]
            wdt = 8 if (want_next and g > 0) else 4
            self.rstd_from_ss(ssx[g % 2][:, 0:wdt], rsx[:, 0:wdt], ssx_r[g % 2], rsx_r, D)
            if wdt == 8:
                cx.op("dve", lambda: nc.vector.tensor_copy(out=self.rstdn[:, 4 * (g - 1):4 * g], in_=rsx[:, 4:8]),
                      reads=[rsx_r], writes=[self.rstdn_r[g - 1]])
            for jj in range(4):
                b = 4 * g + jj
                xj, xj_r = xb.get(b)
                cx.op("dve", lambda jj=jj, Y=Y: nc.vector.scalar_tensor_tensor(
                    out=Y[:, jj, :], in0=Y[:, jj, :], scalar=rsx[:, jj:jj + 1], in1=g2[:], op0=ALU.mult, op1=ALU.mult),
                    reads=[Y_r[jj], rsx_r, g2_r], writes=[Y_r[jj]])
                cx.op("dve", lambda jj=jj, xj=xj, Y=Y: nc.vector.tensor_tensor(out=xj, in0=Y[:, jj, :], in1=xj, op=ALU.add),
                      reads=[Y_r[jj], xj_r], writes=[xj_r])
                if want_next:
                    cx.op("act", lambda jj=jj, xj=xj, g=g: nc.scalar.activation(
                        out=junk2[:], in_=xj, func=AF.Square, accum_out=ssx[(g + 1) % 2][:, 4 + jj:5 + jj]),
                        reads=[xj_r], writes=[junk2_r, ssx_r[(g + 1) % 2]])
                cx.dma("sp", dstv[b], xj, reads=[xj_r], writes=[self.res[dst][b]])

        front(0)
        for g in range(NG):
            if g + 1 < NG:
                front(g + 1)
            back(g)
        if want_next:
            self.rstd_from_ss(ssx[NG % 2][:, 4:8], rsx[:, 4:8], ssx_r[NG % 2], rsx_r, D)
            cx.op("dve", lambda: nc.vector.tensor_copy(out=self.rstdn[:, 4 * (NG - 1):4 * NG], in_=rsx[:, 4:8]),
                  reads=[rsx_r], writes=[self.rstdn_r[NG - 1]])

    def stats_prepass(self, src):
        with ExitStack() as es:
            xp = XStream(self, es, "xp", src, 4)
            junk = self.sb(es, "junkp", [128, D], BF16)
            junk_r = Res()
            for b in range(self.NB):
                xj, xj_r = xp.get(b)
                self.next_stats(xj, xj_r, junk, junk_r, b)

    def phase_ffn(self, l, src, dst):
        nc, cx = self.nc, self.cx
        S, NT, NB = self.S, self.NT, self.NB
        dstv = self.dram[dst].rearrange("(b p) d -> b p d", p=128)
        wg_d = self.dram["w_ffn_gate"][l].rearrange("(c p) n -> p c n", p=128)
        wu_d = self.dram["w_ffn_up"][l].rearrange("(c p) n -> p c n", p=128)
        wd_d = self.dram["w_ffn_down"][l].rearrange("(f p) n -> p f n", p=128)
        R = Res
        with ExitStack() as es:
            wg = self.sb(es, "wg", [128, 8, D_FF], BF16)
            wu = self.sb(es, "wu", [128, 8, D_FF], BF16)
            wd = self.sb(es, "wd", [128, NF, D], BF16)
            g1 = self.sb(es, "g1", [128, D], F32)
            g2 = self.sb(es, "g2", [128, D], F32)
            xa = XStream(self, es, "xa", src, 2)
            xb = XStream(self, es, "xb", src, 2)
            hn = [self.sb(es, f"hn{i}", [128, D], BF16) for i in range(2)]
            hnTs = [self.sb(es, f"hnT{i}", [128, 8, 512], BF16) for i in range(2)]
            hT = self.sb(es, "hT", [128, NF, 512], BF16)
            sg = [self.sb(es, "sg0", [128, 512], F32)] * 2
            tmp = self.sb(es, "tmp", [128, D], F32)
            junk = self.sb(es, "junk", [128, D], BF16)
            ss2 = self.sb(es, "ss2", [128, 4], F32)
            rstd1 = self.sb(es, "rstd1", [128, 1], F32)
            pT = [self.ps(es, f"pT{i}", [128, D], BF16) for i in range(2)]
            pg = [self.ps(es, f"pg{i}", [128, 512], F32) for i in range(2)]
            pu = [self.ps(es, f"pu{i}", [128, 512], F32) for i in range(2)]
            py = [self.ps(es, f"py{i}", [128, 512], F32) for i in range(2)]
            wg_r = [R() for _ in range(6)]
            wu_r = [R() for _ in range(6)]
            wd_r = [R() for _ in range(NF)]
            g1_r, g2_r = R(), R()
            tmp_r, junk_r, ss2_r, rstd1_r = (R() for _ in range(4))
            hnTs_r = [R(), R()]
            hn_r = [R(), R()]
            hT_r = [R() for _ in range(NF)]
            sg_r = [R()] * 2
            pT_r = [R(), R()]
            pg_r = [R(), R()]
            pu_r = [R(), R()]
            py_r = [R(), R()]

            self.load_bcast("sp", g1[:], g1_r, self.dram["ln_ffn_pre"][l])
            self.load_bcast("sp", g2[:], g2_r, self.dram["ln_ffn_post"][l])
            xa.get(0)
            for gi in range(6):
                c0 = gi * 512
                c1 = min(D_FF, c0 + 512)
                cx.dma("pool", wg[:, :, c0:c1], wg_d[:, :, c0:c1], writes=[wg_r[gi]])
                cx.dma("pool", wu[:, :, c0:c1], wu_d[:, :, c0:c1], writes=[wu_r[gi]])
            for f in range(NF):
                cx.dma("pool", wd[:, f, :], wd_d[:, f, :], writes=[wd_r[f]])

            for j in range(4):
                self.pre_hn(xa, j, g1, g1_r, hn[j % 2], hn_r[j % 2])
                self.pre_tr(hn[j % 2], hn_r[j % 2], pT[j % 2], pT_r[j % 2], hnTs[0], hnTs_r[0], j)
            for t in range(NT):
                hnT, hnT_r = hnTs[t % 2], hnTs_r[t % 2]
                for f in range(NF):
                    gi = f // 4
                    G, U = pg[f % 2], pu[f % 2]
                    G_r, U_r = pg_r[f % 2], pu_r[f % 2]
                    for c in range(8):
                        cx.op("pe", lambda c=c, G=G: nc.tensor.matmul(G[:], lhsT=wg[:, c, f * 128:(f + 1) * 128],
                                                                       rhs=hnT[:, c, :], start=(c == 0), stop=(c == 7)),
                              reads=[wg_r[gi], hnT_r], writes=[G_r])
                    for c in range(8):
                        cx.op("pe", lambda c=c, U=U: nc.tensor.matmul(U[:], lhsT=wu[:, c, f * 128:(f + 1) * 128],
                                                                       rhs=hnT[:, c, :], start=(c == 0), stop=(c == 7)),
                              reads=[wu_r[gi], hnT_r], writes=[U_r])
                    sgt, sgt_r = sg[f % 2], sg_r[f % 2]
                    cx.op("act", lambda G=G, sgt=sgt: nc.scalar.activation(out=sgt[:], in_=G[:], func=AF.Silu),
                          reads=[G_r], writes=[sgt_r])
                    cx.op("dve", lambda U=U, sgt=sgt, f=f: nc.vector.tensor_tensor(out=hT[:, f, :], in0=sgt[:], in1=U[:],
                                                                                     op=ALU.mult),
                          reads=[sgt_r, U_r], writes=[hT_r[f]])
                    if t + 1 < NT and f >= 4 and f < 20:
                        j = (f - 4) // 4
                        if (f - 4) % 4 == 0:
                            self.pre_hn(xa, 4 * (t + 1) + j, g1, g1_r, hn[j % 2], hn_r[j % 2])
                        elif (f - 4) % 4 == 2:
                            self.pre_tr(hn[j % 2], hn_r[j % 2], pT[j % 2], pT_r[j % 2],
                                        hnTs[(t + 1) % 2], hnTs_r[(t + 1) % 2], j)
                for j in range(4):
                    b = 4 * t + j
                    for hf in range(2):
                        for f in range(NF):
                            cx.op("pe", lambda f=f, hf=hf, j=j: nc.tensor.matmul(
                                py[hf][:], lhsT=hT[:, f, j * 128:(j + 1) * 128], rhs=wd[:, f, hf * 512:(hf + 1) * 512],
                                start=(f == 0), stop=(f == NF - 1)),
                                reads=[hT_r[f], wd_r[f]], writes=[py_r[hf]])
                    xj, xj_r = xb.get(b)
                    self.postnorm_residual([py[0][:], py[1][:]], py_r, xj, xj_r, g2, g2_r, junk, junk_r,
                                           ss2, ss2_r, rstd1, rstd1_r, tmp, tmp_r, b=b, want_next=(dst != "out"))
                    cx.dma("sp", dstv[b], xj, reads=[xj_r], writes=[self.res[dst][b]])


    def phase_fnet(self, l, src, dst):
        nc, cx = self.nc, self.cx
        S, NB = self.S, self.NB
        NTH = S // 512
        o = l // 2
        wf_d = self.dram["w_fourier_out"][o].rearrange("(c p) n -> p c n", p=128)
        R = Res
        sc1 = 1.0 / math.sqrt(S)
        sc2 = 1.0 / 16.0
        with ExitStack() as es0:
            FT = self.sb(es0, "FTall", [128, 8, S], BF16)
            FT_r = [R() for _ in range(8)]
            with ExitStack() as es:
                hnA = self.sb(es, "hnA", [128, NB, D], BF16)
                hnA_r = [R() for _ in range(NB)]
                cd = self.sb(es, "cd", [128, 3, 2, 256], BF16)
                cd_r = R()
                g1 = self.sb(es, "g1", [128, D], F32)
                g1_r = R()
                xa = XStream(self, es, "xa", src, 4)
                tabs = [self.sb(es, f"tab{i}", [128, NB, 257], BF16) for i in range(3)]
                tab_r = [R() for _ in range(3)]
                YT = self.sb(es, "YT", [128, 2, 8, 258], BF16)
                YT_r = [[R() for _ in range(8)] for _ in range(2)]
                pY = [self.ps(es, f"pY{i}", [128, 512], F32) for i in range(4)]
                pY_r = [R() for _ in range(4)]
                pF = [self.ps(es, f"pF{i}", [128, 512], F32) for i in range(4)]
                pF_r = [R() for _ in range(4)]

                self.load_bcast("sp", g1[:], g1_r, self.dram["ln_mix_pre"][l])
                cx.dma("sp", cd[:], self.dram["c_cdft"], writes=[cd_r])
                xa.get(0)
                tabsrc = (self.dram["c_dftc"], self.dram["c_dfts"])

                def load_tab(u):
                    if u < 2 * NTH:
                        cx.dma("sp", tabs[u % 3][:], tabsrc[u % 2][u // 2], writes=[tab_r[u % 3]])

                load_tab(0)
                load_tab(1)
                for b in range(NB):
                    xj, xj_r = xa.get(b)
                    cx.op("dve", lambda xj=xj, b=b: nc.vector.scalar_tensor_tensor(
                        out=hnA[:, b, :], in0=xj, scalar=self.rstdn[:, b:b + 1], in1=g1[:], op0=ALU.mult, op1=ALU.mult),
                        reads=[xj_r, self.rstdn_r[b // 4], g1_r], writes=[hnA_r[b]])
                ev = 0
                for st in range(NTH):
                    w = 257 if st == NTH - 1 else 256
                    if st == 0 and NTH > 1:
                        banks = pY + pF
                        banks_r = pY_r + pF_r
                        load_tab(2)
                        for b in range(NB):
                            for a16 in range(16):
                                c, cs = a16 // 2, a16 % 2
                                bk, bk_r = banks[a16 // 2], banks_r[a16 // 2]
                                off = (a16 % 2) * 256
                                cx.op("pe", lambda b=b, c=c, cs=cs, bk=bk, off=off, a16=a16: nc.tensor.matmul(
                                    bk[:, off:off + 256], lhsT=hnA[:, b, c * 128:(c + 1) * 128], rhs=tabs[cs][:, b, 0:256],
                                    start=(b == 0 and a16 % 2 == 0), stop=(b == NB - 1), skip_group_check=True),
                                    reads=[hnA_r[b], tab_r[cs]], writes=[bk_r])
                        load_tab(3)
                        for a16 in range(16):
                            c, cs = a16 // 2, a16 % 2
                            bk, bk_r = banks[a16 // 2], banks_r[a16 // 2]
                            off = (a16 % 2) * 256
                            if (a16 // 2) % 2 == 0:
                                cx.op("act", lambda bk=bk, off=off, cs=cs, c=c: nc.scalar.activation(
                                    out=YT[:, cs, c, 0:256], in_=bk[:, off:off + 256], func=AF.Copy, scale=sc1),
                                    reads=[bk_r], writes=[YT_r[cs][c]])
                            else:
                                cx.op("dve", lambda bk=bk, off=off, cs=cs, c=c: nc.vector.tensor_scalar(
                                    out=YT[:, cs, c, 0:256], in0=bk[:, off:off + 256], scalar1=sc1, scalar2=None, op0=ALU.mult),
                                    reads=[bk_r], writes=[YT_r[cs][c]])
                    for cs in range(2):
                        if st == 0 and NTH > 1:
                            break
                        u = 2 * st + cs
                        i = u % 3
                        load_tab(u + 2)
                        for c in range(8):
                            P, P_r = pY[(c % 2) * 2 + cs], pY_r[(c % 2) * 2 + cs]
                            for b in range(NB):
                                cx.op("pe", lambda b=b, c=c, i=i, P=P: nc.tensor.matmul(
                                    P[:, 0:w], lhsT=hnA[:, b, c * 128:(c + 1) * 128], rhs=tabs[i][:, b, 0:w],
                                    start=(b == 0), stop=(b == NB - 1)),
                                    reads=[hnA_r[b], tab_r[i]], writes=[P_r])
                            if ev % 2 == 0:
                                cx.op("act", lambda P=P, cs=cs, c=c: nc.scalar.activation(
                                    out=YT[:, cs, c, 0:w], in_=P[:, 0:w], func=AF.Copy, scale=sc1),
                                    reads=[P_r], writes=[YT_r[cs][c]])
                            else:
                                cx.op("dve", lambda P=P, cs=cs, c=c: nc.vector.tensor_scalar(
                                    out=YT[:, cs, c, 0:w], in0=P[:, 0:w], scalar1=sc1, scalar2=None, op0=ALU.mult),
                                    reads=[P_r], writes=[YT_r[cs][c]])
                            ev += 1
                    for c2 in range(8):
                        g, half = c2 // 2, c2 % 2
                        for mir in range(2):
                            P, P_r = pF[(c2 % 2) * 2 + mir], pF_r[(c2 % 2) * 2 + mir]
                            k = 0
                            for cs in range(2):
                                ti = 0 if cs == 0 else (1 + mir)
                                for dh in range(2):
                                    cx.op("pe", lambda cs=cs, dh=dh, P=P, ti=ti, k=k: nc.tensor.matmul(
                                        P[:, 0:w], lhsT=cd[:, ti, dh, half * 128:(half + 1) * 128],
                                        rhs=YT[:, cs, 2 * g + dh, 0:w], start=(k == 0), stop=(k == 3)),
                                        reads=[cd_r, YT_r[cs][2 * g + dh]], writes=[P_r])
                                    k += 1
                            if mir == 0:
                                lo = st * 256
                                cx.op("act", lambda P=P, c2=c2, lo=lo: nc.scalar.activation(
                                    out=FT[:, c2, lo:lo + w], in_=P[:, 0:w], func=AF.Copy, scale=sc2),
                                    reads=[P_r], writes=[FT_r[c2]])
                            else:
                                i0 = 1 if st == 0 else 0
                                n = 256 - i0
                                t_hi = S - 256 * st - i0
                                dsto = FT[:, c2, t_hi - n + 1:t_hi + 1]
                                cx.op("dve", lambda P=P, dsto=dsto, i0=i0, n=n: nc.vector.tensor_scalar(
                                    out=dsto[:, ::-1], in0=P[:, i0:i0 + n], scalar1=sc2, scalar2=None, op0=ALU.mult),
                                    reads=[P_r], writes=[FT_r[c2]])
                cx.barrier()
            with ExitStack() as es:
                wf = self.sb(es, "wf", [128, 8, D], BF16)
                wf_r = R()
                g2 = self.sb(es, "g2", [128, D], F32)
                g2_r = R()
                cx.dma("pool", wf[:], wf_d, writes=[wf_r])
                self.load_bcast("sp", g2[:], g2_r, self.dram["ln_mix_post"][l])

                def get_lhs(b):
                    return [(FT[:, c, b * 128:(b + 1) * 128], FT_r[c]) for c in range(8)]

                self.proj_post_stage(es, get_lhs, wf, wf_r, g2, g2_r, src, dst, want_next=(dst != "out"))

    def phase_even(self, l, src, dst):
        cx = self.cx
        S = self.S
        with ExitStack() as es1:
            qT = self.sb(es1, "qT", [128, 4, S], BF16)
            kaug = [self.sb(es1, f"kaug{m}", [128, 4, S], BF16) for m in range(2)]
            vaug = self.sb(es1, "vaug", [128, self.NB, 4, VA], BF16)
            qT_r = [Res() for _ in range(self.NT)]
            kaug_r = [Res() for _ in range(self.NT)]
            v_r = [Res() for _ in range(self.NB)]
            with self.nc.named_scope(f"even_proj{l}"):
                self.even_proj(l, src, qT, qT_r, kaug, kaug_r, vaug, v_r)
                cx.barrier()
            with self.nc.named_scope(f"even_attn{l}"):
                self.even_attn(l, qT, qT_r, kaug, kaug_r, vaug, v_r)
                cx.barrier()
        with ExitStack() as es2:
            recT = self.sb(es2, "recT", [128, 4, S], BF16)
            recT_r = [Res() for _ in range(4)]
            with self.nc.named_scope(f"even_lru{l}"):
                self.even_lru(l, recT, recT_r)
                cx.barrier()
            with self.nc.named_scope(f"even_out{l}"):
                self.even_out(l, src, dst, recT, recT_r)
                cx.barrier()

    def even_proj(self, l, src, qT, qT_r, kaug, kaug_r, vaug, v_r):
        nc, cx = self.nc, self.cx
        S, NT, NB = self.S, self.NT, self.NB
        e = l // 2
        R = Res
        win_d = self.dram["w_in"][e].rearrange("(c p) n -> p c n", p=128)
        with ExitStack() as es:
            win = self.sb(es, "win", [128, 8, IN_COLS], BF16)
            win_r = [R() for _ in range(5)]
            g1 = self.sb(es, "g1", [128, D], F32)
            g1_r = R()
            xa = XStream(self, es, "xa", src, 2)
            hn = [self.sb(es, f"hn{i}", [128, D], BF16) for i in range(2)]
            hn_r = [R(), R()]
            hnTs = [self.sb(es, f"hnT{i}", [128, 8, 512], BF16) for i in range(2)]
            hnTs_r = [R(), R()]
            stg = [self.sb(es, f"stg{i}", [128, 512], F32) for i in range(3)]
            stg_r = [R() for _ in range(3)]
            stg_i = [0]
            pT = [self.ps(es, f"pT{i}", [128, D], BF16) for i in range(2)]
            pT_r = [R(), R()]
            pp = [self.ps(es, f"pp{i}", [128, 512], F32) for i in range(6)]
            pp_r = [R() for _ in range(6)]

            self.load_bcast("sp", g1[:], g1_r, self.dram["ln_mix_pre"][l])
            xa.get(0)
            for gi in range(5):
                cx.dma("pool", win[:, :, gi * 512:(gi + 1) * 512], win_d[:, :, gi * 512:(gi + 1) * 512],
                       writes=[win_r[gi]])
            cx.op("dve", lambda: nc.vector.memset(vaug[:, :, :, DV:VA], 1.0), writes=v_r)
            cx.op("pool", lambda: nc.gpsimd.memset(kaug[0][64:128, :, :], 0.0), writes=kaug_r)
            cx.op("pool", lambda: nc.gpsimd.memset(kaug[1][0:64, :, :], 0.0), writes=kaug_r)
            for h in range(NH):
                cx.dma("pool", kaug[0][64:68, h, :], self.dram["c_kaug"], writes=kaug_r)
                cx.dma("pool", kaug[1][0:4, h, :], self.dram["c_kaug"], writes=kaug_r)
            k = 0
            for j in range(4):
                self.pre_hn(xa, j, g1, g1_r, hn[j % 2], hn_r[j % 2])
                self.pre_tr(hn[j % 2], hn_r[j % 2], pT[j % 2], pT_r[j % 2], hnTs[0], hnTs_r[0], j)
            gcount = [0]

            def interleave(t):
                gi_ = gcount[0]
                gcount[0] += 1
                if t + 1 >= NT or gi_ < 2 or gi_ >= 18:
                    return
                j = (gi_ - 2) // 4
                if (gi_ - 2) % 4 == 0:
                    self.pre_hn(xa, 4 * (t + 1) + j, g1, g1_r, hn[j % 2], hn_r[j % 2])
                elif (gi_ - 2) % 4 == 2:
                    self.pre_tr(hn[j % 2], hn_r[j % 2], pT[j % 2], pT_r[j % 2], hnTs[(t + 1) % 2], hnTs_r[(t + 1) % 2], j)

            for t in range(NT):
                hnT, hnT_r = hnTs[t % 2], hnTs_r[t % 2]
                gcount[0] = 0
                tsl = slice(t * 512, (t + 1) * 512)
                for grp in (0, 1, 3, 4):
                    for m in range(4):
                        col = grp * 512 + m * 128
                        P, P_r = pp[k % 6], pp_r[k % 6]
                        for c in range(8):
                            cx.op("pe", lambda c=c, P=P, col=col: nc.tensor.matmul(
                                P[:], lhsT=win[:, c, col:col + 128], rhs=hnT[:, c, :], start=(c == 0), stop=(c == 7)),
                                reads=[win_r[grp], hnT_r], writes=[P_r])
                        if grp == 1:
                            if k % 2 == 0:
                                cx.op("act", lambda P=P, m=m: nc.scalar.copy(out=kaug[0][0:64, m, tsl], in_=P[0:64, :]),
                                      reads=[P_r], writes=[kaug_r[t]])
                                cx.op("act", lambda P=P, m=m: nc.scalar.copy(out=kaug[1][64:128, m, tsl], in_=P[64:128, :]),
                                      reads=[P_r], writes=[kaug_r[t]])
                            else:
                                cx.op("dve", lambda P=P, m=m: nc.vector.tensor_copy(out=kaug[0][0:64, m, tsl], in_=P[0:64, :]),
                                      reads=[P_r], writes=[kaug_r[t]])
                                cx.op("dve", lambda P=P, m=m: nc.vector.tensor_copy(out=kaug[1][64:128, m, tsl], in_=P[64:128, :]),
                                      reads=[P_r], writes=[kaug_r[t]])
                            k += 1
                            interleave(t)
                            continue
                        if grp == 0:
                            dst_ap, dst_r = qT[:, m, tsl], qT_r[t]
                        else:
                            si = stg_i[0] % 3
                            stg_i[0] += 1
                            dst_ap, dst_r = stg[si][:], stg_r[si]
                        if k % 2 == 0:
                            cx.op("act", lambda P=P, dst_ap=dst_ap: nc.scalar.copy(out=dst_ap, in_=P[:]),
                                  reads=[P_r], writes=[dst_r])
                        else:
                            cx.op("dve", lambda P=P, dst_ap=dst_ap: nc.vector.tensor_copy(out=dst_ap, in_=P[:]),
                                  reads=[P_r], writes=[dst_r])
                        if grp >= 3:
                            nm = "xr_s" if grp == 3 else "yg_s"
                            cx.dma("sp", self.dram[nm][m * 128:(m + 1) * 128, tsl], dst_ap,
                                   reads=[dst_r], writes=[self.res[nm][m][t]])
                        k += 1
                        interleave(t)
                for j in range(4):
                    b = 4 * t + j
                    P, P_r = pp[k % 6], pp_r[k % 6]
                    for c in range(8):
                        cx.op("pe", lambda c=c, P=P, j=j: nc.tensor.matmul(
                            P[:], lhsT=hnT[:, c, j * 128:(j + 1) * 128], rhs=win[:, c, 1024:1536],
                            start=(c == 0), stop=(c == 7)),
                            reads=[win_r[2], hnT_r], writes=[P_r])
                    srcv = P[:].rearrange("p (h d) -> p h d", h=4)
                    if k % 2 == 0:
                        cx.op("act", lambda srcv=srcv, b=b: nc.scalar.copy(out=vaug[:, b, :, 0:DV], in_=srcv),
                              reads=[P_r], writes=[v_r[b]])
                    else:
                        cx.op("dve", lambda srcv=srcv, b=b: nc.vector.tensor_copy(out=vaug[:, b, :, 0:DV], in_=srcv),
                              reads=[P_r], writes=[v_r[b]])
                    k += 1
                    interleave(t)

    def even_attn(self, l, qT, qT_r, kaug, kaug_r, vaug, v_r):
        nc, cx = self.nc, self.cx
        S, NT, NB = self.S, self.NT, self.NB
        e = l // 2
        lam_init = 0.8 - 0.6 * math.exp(-0.3 * l)
        R = Res
        NACC = 8 * VA
        with ExitStack() as es:
            pS2 = [self.ps(es, f"pS{i}", [128, 2, 512], F32) for i in range(2)]
            pS2_r = [R(), R()]
            pA = [self.ps(es, f"pA{i}", [128, 512], F32) for i in range(3)]
            pA_r = [R() for _ in range(3)]
            pTr = self.ps(es, "pTr", [128, D], BF16)
            pTr_r = R()
            dist = self.sb(es, "dist", [128, 4, 512], F32)
            dist_r = R()
            lqk = self.sb(es, "lqk", [128, 4, QK], F32)
            lqk_r = R()
            lsm = self.sb(es, "lsm", [128, 8], F32)
            lsm_r = R()
            lam = self.sb(es, "lam", [128, 1], F32)
            lam_r = R()
            gsub = self.sb(es, "gsub", [128, DV], F32)
            gsub_r = R()
            qz = [self.sb(es, f"qz{i}", [128, 2, 2, 512], BF16) for i in range(2)]
            qzd_r = [R(), R()]
            qza_r = [R(), R()]
            PT2 = [self.sb(es, f"PT{i}", [128, 2, 512], BF16) for i in range(4)]
            PT2_r = [R() for _ in range(4)]
            dtmp = [self.sb(es, f"dtmp{i}", [128, 2, 512], F32) for i in range(2)]
            dtmp_r = [R(), R()]
            acc = [self.sb(es, f"acc{i}", [128, NACC], F32) for i in range(2)]
            acc_r = [R(), R()]
            rl = self.sb(es, "rl", [128, 2, 4], F32)
            rl_r = R()
            c1 = self.sb(es, "c1", [128, 4], F32)
            c1_r = R()
            Ot = self.sb(es, "Ot", [128, 4, DV], F32)
            Ot_r = [R() for _ in range(4)]
            tO = self.sb(es, "tO", [128, DV], F32)
            tO_r = R()
            jk = self.sb(es, "jk", [128, DV], F32)
            jk_r = R()
            ssa = self.sb(es, "ssa", [128, 4], F32)
            ssa_r = R()
            rsa = self.sb(es, "rsa", [128, 4], F32)
            rsa_r = R()
            ab = self.sb(es, "ab", [128, 4, DV], BF16)
            ab_r = [R() for _ in range(4)]
            ats = [self.sb(es, f"ats{i}", [128, 512], BF16) for i in range(2)]
            ats_r = [R(), R()]

            cx.dma("sp", dist[:], self.dram["c_dist"], writes=[dist_r])
            for i, nm in enumerate(("lambda_q1", "lambda_k1", "lambda_q2", "lambda_k2")):
                self.load_bcast("sp", lqk[:, i, :], lqk_r, self.dram[nm][e])
            self.load_bcast("sp", gsub[:], gsub_r, self.dram["attn_subln"][e])
            for i in range(2):
                cx.op("pool", lambda i=i: nc.gpsimd.memset(qz[i][:], 0.0), writes=[qzd_r[i], qza_r[i]])
            for i in range(2):
                cx.op("dve", lambda i=i: nc.vector.scalar_tensor_tensor(
                    out=lqk[:, 2 * i, :], in0=lqk[:, 2 * i, :], scalar=1.0, in1=lqk[:, 2 * i + 1, :],
                    op0=ALU.mult, op1=ALU.mult, accum_out=lsm[:, i:i + 1]),
                    reads=[lqk_r], writes=[lqk_r, lsm_r])
            cx.op("act", lambda: nc.scalar.activation(out=lsm[:, 2:4], in_=lsm[:, 0:2], func=AF.Exp),
                  reads=[lsm_r], writes=[lsm_r])
            cx.op("dve", lambda: nc.vector.tensor_tensor(out=lsm[:, 4:5], in0=lsm[:, 2:3], in1=lsm[:, 3:4], op=ALU.subtract),
                  reads=[lsm_r], writes=[lsm_r])
            cx.op("dve", lambda: nc.vector.tensor_scalar(out=lam[:], in0=lsm[:, 4:5], scalar1=lam_init, scalar2=None,
                                                         op0=ALU.add),
                  reads=[lsm_r], writes=[lam_r])
            cx.op("dve", lambda: nc.vector.tensor_scalar(out=gsub[:], in0=gsub[:], scalar1=(1.0 - lam_init), scalar2=None,
                                                         op0=ALU.mult),
                  reads=[gsub_r], writes=[gsub_r])

            pairs = [(h, j) for h in range(NH) for j in range(NT)]

            def build_qz(pi):
                h, j = pairs[pi]
                Z = qz[pi % 2]
                tsl = slice(j * 512, (j + 1) * 512)
                for v in range(2):
                    cx.dma("sp", Z[0:64, 0, v, :], qT[0:64, h, tsl], reads=[qT_r[j]], writes=[qzd_r[pi % 2]])
                    cx.dma("sp", Z[64:128, 1, v, :], qT[64:128, h, tsl], reads=[qT_r[j]], writes=[qzd_r[pi % 2]])
                    cx.dma("sp", Z[64:68, 0, v, :], self.dram["c_qaug"][h, v, :, tsl], writes=[qza_r[pi % 2]])
                    cx.dma("sp", Z[0:4, 1, v, :], self.dram["c_qaug"][h, v, :, tsl], writes=[qza_r[pi % 2]])

            steps = []
            for pi, (h, j) in enumerate(pairs):
                order = [i for i in range(NB) if not (4 * j <= i < 4 * j + 4)] + list(range(4 * j, 4 * j + 4))
                for n, i in enumerate(order):
                    steps.append((pi, i, n == 0, n == NB - 1))

            def emit_qk(s):
                pi, i, first, last = steps[s]
                h, j = pairs[pi]
                Z = qz[pi % 2]
                v = 1 if i >= 4 * j + 4 else 0
                P = pS2[s % 2]
                for m in range(2):
                    cx.op("pe", lambda m=m: nc.tensor.matmul(
                        P[:, m, :], lhsT=kaug[m][:, h, i * 128:(i + 1) * 128], rhs=Z[:, m, v, :], start=True, stop=True),
                        reads=[kaug_r[i // 4], qzd_r[pi % 2], qza_r[pi % 2]], writes=[pS2_r[s % 2]])

            def emit_exp(s):
                pi, i, first, last = steps[s]
                h, j = pairs[pi]
                P, P_r = pS2[s % 2], pS2_r[s % 2]
                T, T_r = PT2[s % 4], PT2_r[s % 4]
                if 4 * j <= i < 4 * j + 4:
                    r = i - 4 * j
                    dt_, dt_r = dtmp[s % 2], dtmp_r[s % 2]
                    for m in range(2):
                        cx.op("dve", lambda m=m: nc.vector.scalar_tensor_tensor(
                            out=dt_[:, m, :], in0=dist[:, r, :], scalar=-8.0 * SLOPES[h], in1=P[:, m, :],
                            op0=ALU.mult, op1=ALU.add),
                            reads=[dist_r, P_r], writes=[dt_r])
                    cx.op("act", lambda: nc.scalar.activation(out=T[:], in_=dt_[:], func=AF.Exp, scale=0.125),
                          reads=[dt_r], writes=[T_r])
                else:
                    cx.op("act", lambda: nc.scalar.activation(out=T[:], in_=P[:], func=AF.Exp, scale=0.125),
                          reads=[P_r], writes=[T_r])

            def emit_pv(s):
                pi, i, first, last = steps[s]
                h, j = pairs[pi]
                T, T_r = PT2[s % 4], PT2_r[s % 4]
                for m in range(2):
                    for qs in range(4):
                        a = m * 4 + qs
                        bk, bk_r = pA[a // 3], pA_r[a // 3]
                        o = (a % 3) * VA
                        cx.op("pe", lambda m=m, qs=qs, bk=bk, o=o, a=a: nc.tensor.matmul(
                            bk[:, o:o + VA], lhsT=T[:, m, qs * 128:(qs + 1) * 128], rhs=vaug[:, i, h, :],
                            start=(first and a % 3 == 0), stop=last, skip_group_check=True),
                            reads=[T_r, v_r[i]], writes=[bk_r])

            def finish_pair(pi):
                h, j = pairs[pi]
                A, A_r = acc[pi % 2], acc_r[pi % 2]
                A4 = A[:].rearrange("p (m q v) -> p m q v", m=2, q=4)
                for bki, n in ((0, 3 * VA), (1, 3 * VA), (2, 2 * VA)):
                    cx.op("dve", lambda bki=bki, n=n: nc.vector.tensor_copy(
                        out=A[:, bki * 3 * VA:bki * 3 * VA + n], in_=pA[bki][:, 0:n]),
                        reads=[pA_r[bki]], writes=[A_r])
                cx.op("dve", lambda: nc.vector.reciprocal(out=rl[:], in_=A4[:, :, :, DV]),
                      reads=[A_r], writes=[rl_r])
                cx.op("dve", lambda: nc.vector.tensor_scalar(out=c1[:], in0=rl[:, 1, :], scalar1=lam[:], scalar2=None,
                                                             op0=ALU.mult),
                      reads=[rl_r, lam_r], writes=[c1_r])
                for qs in range(4):
                    cx.op("dve", lambda qs=qs: nc.vector.tensor_scalar(out=tO[:], in0=A4[:, 1, qs, 0:DV],
                                                                       scalar1=c1[:, qs:qs + 1], scalar2=None, op0=ALU.mult),
                          reads=[A_r, c1_r], writes=[tO_r])
                    cx.op("dve", lambda qs=qs: nc.vector.scalar_tensor_tensor(
                        out=Ot[:, qs, :], in0=A4[:, 0, qs, 0:DV], scalar=rl[:, 0, qs:qs + 1], in1=tO[:],
                        op0=ALU.mult, op1=ALU.subtract),
                        reads=[A_r, rl_r, tO_r], writes=[Ot_r[qs]])
                    cx.op("dve", lambda qs=qs: nc.vector.scalar_tensor_tensor(
                        out=jk[:], in0=Ot[:, qs, :], scalar=1.0, in1=Ot[:, qs, :], op0=ALU.mult, op1=ALU.mult,
                        accum_out=ssa[:, qs:qs + 1]),
                        reads=[Ot_r[qs]], writes=[jk_r, ssa_r])
                self.rstd_from_ss(ssa[:], rsa[:], ssa_r, rsa_r, DV)
                for qs in range(4):
                    cx.op("dve", lambda qs=qs: nc.vector.scalar_tensor_tensor(
                        out=ab[:, qs, :], in0=Ot[:, qs, :], scalar=rsa[:, qs:qs + 1], in1=gsub[:],
                        op0=ALU.mult, op1=ALU.mult),
                        reads=[Ot_r[qs], rsa_r, gsub_r], writes=[ab_r[qs]])

            def finish_pair_b(pi):
                h, j = pairs[pi]
                for qs in range(4):
                    cx.op("pe", lambda qs=qs: nc.tensor.transpose(
                        out=pTr[:, qs * 128:(qs + 1) * 128], in_=ab[:, qs, :], identity=self.ident[:]),
                        reads=[ab_r[qs], self.ident_r], writes=[pTr_r])
                st_, st_r = ats[pi % 2], ats_r[pi % 2]
                cx.op("dve", lambda: nc.vector.tensor_copy(out=st_[:], in_=pTr[:, 0:512]),
                      reads=[pTr_r], writes=[st_r])
                cx.dma("sp", self.dram["attn_s"][h, :, j * 512:(j + 1) * 512], st_[:], reads=[st_r],
                       writes=[self.res["attn_s"][h][j]])

            build_qz(0)
            emit_qk(0)
            nst = len(steps)
            if nst > 1:
                emit_qk(1)
            pending = None
            for s in range(nst):
                pi, i, first, last = steps[s]
                if first and pi + 1 < len(pairs):
                    build_qz(pi + 1)
                emit_exp(s)
                if s + 2 < nst:
                    emit_qk(s + 2)
                emit_pv(s)
                if pending is not None and s >= pending[1]:
                    finish_pair_b(pending[0])
                    pending = None
                if last:
                    finish_pair(pi)
                    pending = (pi, s + min(12, NB // 2))
            if pending is not None:
                finish_pair_b(pending[0])

    def even_lru(self, l, recT, recT_r):
        nc, cx = self.nc, self.cx
        S, NT = self.S, self.NT
        e = l // 2
        R = Res
        with ExitStack() as es:
            xr = self.sb(es, "xr", [128, S + 4], F32)
            xcs = [self.sb(es, f"xc{i}", [128, S], F32) for i in range(2)]
            xcb = self.sb(es, "xcb", [128, S], BF16)
            yg = self.sb(es, "yg", [128, S], F32)
            ra = [self.sb(es, f"ra{d}", [128, S], F32) for d in range(2)]
            iu = [self.sb(es, f"iu{d}", [128, S], F32) for d in range(2)]
            t1 = [self.sb(es, f"t1{d}", [128, S], F32) for d in range(2)]
            hh = t1
            xr_r, yg_r = R(), R()
            xcs_r = [R(), R()]
            xcb_r = R()
            ra_r, iu_r, t1_r = ([R(), R()] for _ in range(3))
            hh_r = t1_r
            wai = self.sb(es, "wai", [128, 2, 2, 4, 128], BF16)
            wai_r = R()
            cw = self.sb(es, "cw", [128, 4, 4], F32)
            cb = self.sb(es, "cb", [128, 4], F32)
            bai = self.sb(es, "bai", [128, 2, 2, 4], F32)
            lml = self.sb(es, "lml", [128, 2, 2, 4], F32)
            prm_r = R()
            pG = [self.ps(es, f"pG{i}", [128, 512], F32) for i in range(4)]
            pG_r = [R() for _ in range(4)]
            with nc.allow_non_contiguous_dma(reason="tiny per-channel parameter vectors"):
                def ldc(dst_ap, src_vec):
                    cx.dma("sp", dst_ap, src_vec.rearrange("(c p) -> p c", p=128), writes=[prm_r])
                for tap in range(4):
                    ldc(cw[:, :, tap], self.dram["conv_w"][e][tap])
                ldc(cb[:], self.dram["conv_b"][e])
                for d in range(2):
                    ldc(bai[:, 0, d, :], self.dram["lru_b_a"][e][d])
                    ldc(bai[:, 1, d, :], self.dram["lru_b_i"][e][d])
                    ldc(lml[:, 0, d, :], self.dram["lru_lambda"][e][d])
            for d in range(2):
                cx.dma("pool", wai[:, 0, d], self.dram["lru_w_a"][e][d].rearrange("h i j -> i h j"), writes=[wai_r])
                cx.dma("pool", wai[:, 1, d], self.dram["lru_w_i"][e][d].rearrange("h i j -> i h j"), writes=[wai_r])
            cx.op("act", lambda: nc.scalar.activation(out=lml[:, 0], in_=lml[:, 0], func=AF.Exp, scale=-1.0),
                  reads=[prm_r], writes=[prm_r])
            cx.op("act", lambda: nc.scalar.activation(out=lml[:, 0], in_=lml[:, 0], func=AF.Ln, bias=1.0),
                  reads=[prm_r], writes=[prm_r])
            cx.op("dve", lambda: nc.vector.tensor_scalar(out=lml[:, 1], in0=lml[:, 0], scalar1=-16.0, scalar2=None, op0=ALU.mult),
                  reads=[prm_r], writes=[prm_r])
            cx.op("dve", lambda: nc.vector.tensor_scalar(out=lml[:, 0], in0=lml[:, 0], scalar1=-8.0, scalar2=None, op0=ALU.mult),
                  reads=[prm_r], writes=[prm_r])
            cx.op("dve", lambda: nc.vector.memset(xr[:, 0:2], 0.0), writes=[xr_r])
            cx.op("dve", lambda: nc.vector.memset(xr[:, S + 2:S + 4], 0.0), writes=[xr_r])
            k = 0

            def conv_chunk(c):
                xc, xc_r = xcs[c % 2], xcs_r[c % 2]
                rows = slice(c * 128, (c + 1) * 128)
                cx.dma("sp", xr[:, 2:S + 2], self.dram["xr_s"][rows, :], reads=self.res["xr_s"][c], writes=[xr_r])
                cx.op("dve", lambda: nc.vector.tensor_scalar(out=xc[:], in0=xr[:, 2:S + 2], scalar1=cw[:, c, 2:3],
                                                             scalar2=cb[:, c:c + 1], op0=ALU.mult, op1=ALU.add),
                      reads=[xr_r, prm_r], writes=[xc_r])
                for tap, off in ((0, 0), (1, 1), (3, 3)):
                    cx.op("dve", lambda tap=tap, off=off: nc.vector.scalar_tensor_tensor(
                        out=xc[:], in0=xr[:, off:off + S], scalar=cw[:, c, tap:tap + 1], in1=xc[:],
                        op0=ALU.mult, op1=ALU.add),
                        reads=[xr_r, prm_r, xc_r], writes=[xc_r])

            def cast_chunk(c):
                cx.op("act", lambda: nc.scalar.copy(out=xcb[:], in_=xcs[c % 2][:]), reads=[xcs_r[c % 2]], writes=[xcb_r])

            conv_chunk(0)
            cast_chunk(0)
            for c in range(4):
                xc, xc_r = xcs[c % 2], xcs_r[c % 2]
                rows = slice(c * 128, (c + 1) * 128)
                cx.dma("sp", yg[:], self.dram["yg_s"][rows, :], reads=self.res["yg_s"][c], writes=[yg_r])
                if c + 1 < 4:
                    conv_chunk(c + 1)
                for d in range(2):
                    for n in range(NT):
                        tsl = slice(n * 512, (n + 1) * 512)
                        for ai, (dstt, dstt_r) in enumerate(((ra[d], ra_r[d]), (iu[d], iu_r[d]))):
                            P, P_r = pG[k % 4], pG_r[k % 4]
                            k += 1
                            cx.op("pe", lambda P=P, ai=ai, d=d, tsl=tsl: nc.tensor.matmul(
                                P[:], lhsT=wai[:, ai, d, c, :], rhs=xcb[:, tsl], start=True, stop=True),
                                reads=[wai_r, xcb_r], writes=[P_r])
                            cx.op("act", lambda P=P, ai=ai, dstt=dstt, d=d, tsl=tsl: nc.scalar.activation(
                                out=dstt[:, tsl], in_=P[:], func=AF.Sigmoid, bias=bai[:, ai, d, c:c + 1]),
                                reads=[P_r, prm_r], writes=[dstt_r])
                    cx.op("act", lambda d=d: nc.scalar.activation(out=t1[d][:], in_=ra[d][:], func=AF.Exp,
                                                                  scale=lml[:, 1, d, c:c + 1]),
                          reads=[ra_r[d], prm_r], writes=[t1_r[d]])
                    cx.op("act", lambda d=d: nc.scalar.activation(out=t1[d][:], in_=t1[d][:], func=AF.Sqrt, scale=-1.0, bias=1.0),
                          reads=[t1_r[d]], writes=[t1_r[d]])
                    cx.op("act", lambda d=d: nc.scalar.activation(out=ra[d][:], in_=ra[d][:], func=AF.Exp,
                                                                  scale=lml[:, 0, d, c:c + 1]),
                          reads=[ra_r[d], prm_r], writes=[ra_r[d]])
                    cx.op("dve", lambda d=d: nc.vector.tensor_tensor(out=iu[d][:], in0=iu[d][:], in1=xc[:], op=ALU.mult),
                          reads=[iu_r[d], xc_r], writes=[iu_r[d]])
                if c + 1 < 4:
                    cast_chunk(c + 1)
                cx.op("act", lambda: nc.scalar.activation(out=yg[:], in_=yg[:], func=AF.Gelu_apprx_tanh),
                      reads=[yg_r], writes=[yg_r])
                for d in range(2):
                    cx.op("dve", lambda d=d: nc.vector.tensor_tensor(out=iu[d][:], in0=iu[d][:], in1=t1[d][:], op=ALU.mult),
                          reads=[iu_r[d], t1_r[d]], writes=[iu_r[d]])
                    if d == 0:
                        cx.op("dve", lambda: nc.vector.tensor_tensor_scan(
                            out=hh[0][:], data0=ra[0][:], data1=iu[0][:], initial=0.0, op0=ALU.mult, op1=ALU.add),
                            reads=[ra_r[0], iu_r[0]], writes=[hh_r[0]])
                    else:
                        cx.op("dve", lambda: nc.vector.tensor_tensor_scan(
                            out=hh[1][:, ::-1], data0=ra[1][:, ::-1], data1=iu[1][:, ::-1], initial=0.0,
                            op0=ALU.mult, op1=ALU.add),
                            reads=[ra_r[1], iu_r[1]], writes=[hh_r[1]])
                cx.op("dve", lambda: nc.vector.tensor_tensor(out=hh[0][:], in0=hh[0][:], in1=hh[1][:], op=ALU.add),
                      reads=[hh_r[0], hh_r[1]], writes=[hh_r[0]])
                cx.op("dve", lambda: nc.vector.tensor_tensor(out=recT[:, c, :], in0=hh[0][:], in1=yg[:], op=ALU.mult),
                      reads=[hh_r[0], yg_r], writes=[recT_r[c]])

    def even_out(self, l, src, dst, recT, recT_r):
        nc, cx = self.nc, self.cx
        S = self.S
        e = l // 2
        R = Res
        wo_d = self.dram["w_mix_out"][e].rearrange("(c p) n -> p c n", p=128)
        with ExitStack() as es:
            wo = self.sb(es, "wo", [128, 8, D], BF16)
            wo_r = R()
            attnT = self.sb(es, "attnT", [128, 4, S], BF16)
            attnT_r = [R() for _ in range(4)]
            g2 = self.sb(es, "g2", [128, D], F32)
            g2_r = R()
            cx.dma("pool", wo[:], wo_d, writes=[wo_r])
            for h in range(4):
                cx.dma("sp", attnT[:, h, :], self.dram["attn_s"][h], reads=self.res["attn_s"][h], writes=[attnT_r[h]])
            self.load_bcast("sp", g2[:], g2_r, self.dram["ln_mix_post"][l])

            def get_lhs(b):
                bsl = slice(b * 128, (b + 1) * 128)
                return ([(attnT[:, c, bsl], attnT_r[c]) for c in range(4)]
                        + [(recT[:, c, bsl], recT_r[c]) for c in range(4)])

            self.proj_post_stage(es, get_lhs, wo, wo_r, g2, g2_r, src, dst, want_next=(dst != "out"))


class XStream:
    def __init__(self, prog, es, name, src, nbuf):
        self.p = prog
        self.src = src
        self.view = prog.dram[src].rearrange("(b p) d -> b p d", p=128)
        self.bufs = [prog.sb(es, f"{name}{i}", [128, D], F32) for i in range(nbuf)]
        self.res = [Res() for _ in range(nbuf)]
        self.loaded = set()
        self.n = nbuf

    def _load(self, b):
        i = b % self.n
        self.p.cx.dma("sp", self.bufs[i][:], self.view[b], reads=[self.p.res[self.src][b]], writes=[self.res[i]])
        self.loaded.add(b)

    def get(self, b):
        for bb in range(b, min(b + self.n, self.p.NB)):
            if bb not in self.loaded:
                self._load(bb)
        i = b % self.n
        return self.bufs[i][:], self.res[i]


ALIBI_COLS = 512


def _consts(S):
    c = {}
    bf = ml_dtypes.bfloat16
    c["c_ident"] = np.eye(128, dtype=np.float32).astype(bf)
    p = np.arange(128)[:, None, None]
    r = np.arange(4)[None, :, None]
    q = np.arange(512)[None, None, :]
    c["c_dist"] = (2.0 * np.maximum((128 * r + p) - q, 0)).astype(np.float32)
    pos = np.arange(S)
    hi, lo = (pos // 64).astype(np.float64), (pos % 64).astype(np.float64)
    c["c_kaug"] = np.stack([np.ones(S), np.ones(S), 64.0 * hi, lo]).astype(np.float32).astype(bf)
    qa = np.zeros((NH, 2, 4, S), np.float64)
    for h in range(NH):
        s8 = 8.0 * SLOPES[h]
        left = np.stack([-s8 * 64.0 * hi, -s8 * lo, s8 * np.ones(S), s8 * np.ones(S)])
        qa[h, 0] = left
        qa[h, 1] = -left
    c["c_qaug"] = qa.astype(np.float32).astype(bf)
    NB, NTH = S // 128, S // 512
    s_in = (np.arange(NB)[None, :, None] * 128 + np.arange(128)[:, None, None]).astype(np.int64)
    tc = np.empty((NTH, 128, NB, 257), bf)
    ts = np.empty((NTH, 128, NB, 257), bf)
    for st in range(NTH):
        s_out = st * 256 + np.arange(257, dtype=np.int64)[None, None, :]
        ang = (2.0 * np.pi / S) * ((s_in * s_out) % S).astype(np.float64)
        tc[st] = np.cos(ang).astype(np.float32).astype(bf)
        sn = np.sin(ang)
        sn[np.broadcast_to(((s_in * s_out) % S) * 2 % S == 0, sn.shape)] = 0.0
        ts[st] = sn.astype(np.float32).astype(bf)
    c["c_dftc"] = tc
    c["c_dfts"] = ts
    d_in = (np.arange(2)[None, :, None] * 128 + np.arange(128)[:, None, None]).astype(np.int64)
    d_out = np.arange(256, dtype=np.int64)[None, None, :]
    ang = (2.0 * np.pi / 256) * ((d_in * d_out) % 256).astype(np.float64)
    cdt = np.stack([np.cos(ang), -np.sin(ang), np.sin(ang)], axis=1)
    c["c_cdft"] = cdt.astype(np.float32).astype(bf)
    return c


def _alibi_table():
    t = np.zeros((128, ALIBI_COLS), np.float64)
    p = np.arange(128, dtype=np.float64)
    for h in range(NH):
        sl = SLOPES[h]
        for dl in range(1, 32):
            t[:, h * 32 + dl] = sl * (p - 128.0 * dl)
        for dr in range(1, 32):
            t[:, 128 + h * 32 + dr] = -sl * (128.0 * dr + p - 127.0)
        for qs in range(4):
            t[:, 256 + qs * 4 + h] = -sl * (128.0 * qs + p)
            t[:, 272 + qs * 4 + h] = -sl * (511.0 - 128.0 * qs - p)
    return t.astype(np.float32)


_CACHE = {}


def run(inputs_per_core, S, plan, trace=False):
    key = (S, tuple(plan))
    if key not in _CACHE:
        _CACHE[key] = Prog(S, plan)
        _CACHE[key].build()
    prog = _CACHE[key]
    consts = _consts(S)
    in_maps = []
    for m in inputs_per_core:
        mm = dict(consts)
        mm.update(m)
        in_maps.append(mm)
    res = run_bass_kernel_spmd(prog.nc, in_maps, core_ids=list(range(len(in_maps))), trace=trace)
    return res


def full_plan():
    plan = []
    for l in range(DEPTH):
        src = "x" if l == 0 else "xs"
        if l % 2 == 0:
            plan.append(("even", l, src, "xs"))
        else:
            plan.append(("fnet", l, src, "xs"))
        plan.append(("ffn", l, "xs", "out" if l == DEPTH - 1 else "xs"))
    return plan


def kernel(**inputs):
    x = np.ascontiguousarray(inputs["x"], dtype=np.float32)
    B, S, _ = x.shape
    shared = {k: np.ascontiguousarray(v, dtype=np.float32) for k, v in inputs.items() if k != "x"}
    per_core = []
    for b in range(B):
        m = dict(shared)
        m["x"] = x[b]
        per_core.append(m)
    res = run(per_core, S, full_plan())
    return np.stack([r["out"] for r in res.results], axis=0).astype(np.float32)
```

```python
import math
from contextlib import ExitStack

import numpy as np
import ml_dtypes

import concourse.bass as bass
import concourse.mybir as mybir
from concourse.bass_utils import run_bass_kernel_spmd

F32 = mybir.dt.float32
BF16 = mybir.dt.bfloat16
AF = mybir.ActivationFunctionType
ALU = mybir.AluOpType

D = 1024
DEPTH = 4
NH = 4
QK = 64
DV = 128
LRU_W = 512
IN_COLS = 2560
D_FF = 2816
NF = D_FF // 128
EPS = 1e-6
SLOPES = [2.0 ** (-8.0 * (h + 1) / NH) for h in range(NH)]
VA = DV + 1


class Res:
    __slots__ = ("w", "r")

    def __init__(self):
        self.w = {}
        self.r = {}


def _merge(dst, toks):
    for k, (sem, val) in toks.items():
        cur = dst.get(k)
        if cur is None or cur[1] < val:
            dst[k] = (sem, val)


class Ctx:
    def __init__(self, nc):
        self.nc = nc
        self.es = ExitStack()
        self.eng = {}
        for name, e in (("pe", nc.tensor), ("act", nc.scalar), ("dve", nc.vector),
                        ("pool", nc.gpsimd), ("sp", nc.sync)):
            sem = self.es.enter_context(nc.semaphore("sem_" + name))
            self.eng[name] = {"e": e, "sem": sem, "cnt": 0, "waited": {}, "key": "E" + name}
        self.dq = {}
        for q, n in (("sp", 24), ("pool", 12)):
            pool = []
            for i in range(n):
                sem = self.es.enter_context(nc.semaphore(f"dq_{q}_{i}"))
                pool.append([sem, 0, f"Q{q}{i}"])
            self.dq[q] = {"pool": pool, "idx": 0}
        self.n_inst = 0

    def _wait(self, E, key, sem, val):
        if E["waited"].get(key, 0) >= val:
            return
        E["e"].wait_ge(sem, val)
        E["waited"][key] = val
        self.n_inst += 1

    def _deps(self, E, reads, writes, skip_key=None):
        need = {}
        for r in reads:
            _merge(need, r.w)
        for w in writes:
            _merge(need, w.w)
            _merge(need, w.r)
        for k, (sem, val) in need.items():
            if k == skip_key:
                continue
            self._wait(E, k, sem, val)

    def _commit(self, key, tok, reads, writes):
        for r in reads:
            cur = r.r.get(key)
            if cur is None or cur[1] < tok[1]:
                r.r[key] = tok
        for w in writes:
            w.w = {key: tok}
            w.r = {}

    def op(self, eng, fn, reads=(), writes=()):
        E = self.eng[eng]
        self._deps(E, reads, writes, skip_key=E["key"] if eng == "pe" else None)
        ins = fn()
        E["cnt"] += 1
        ins.then_inc(E["sem"], 1)
        tok = (E["sem"], E["cnt"])
        self._commit(E["key"], tok, reads, writes)
        self.n_inst += 1
        return tok

    def dma(self, q, out, in_, reads=(), writes=()):
        E = self.eng[q]
        Q = self.dq[q]
        slot = Q["pool"][Q["idx"]]
        Q["idx"] = (Q["idx"] + 1) % len(Q["pool"])
        sem, cnt, key = slot
        if cnt > 0:
            self._wait(E, key, sem, cnt)
        self._deps(E, reads, writes)
        E["e"].dma_start(out=out, in_=in_).then_inc(sem, 16)
        slot[1] = cnt + 16
        tok = (sem, cnt + 16)
        self._commit(key, tok, reads, writes)
        self.n_inst += 1
        return tok

    def barrier(self, skip=()):
        toks = {}
        for name, E in self.eng.items():
            if E["cnt"] > 0:
                toks[E["key"]] = (E["sem"], E["cnt"])
        for q, Q in self.dq.items():
            for sem, cnt, key in Q["pool"]:
                if cnt > 0:
                    toks[key] = (sem, cnt)
        for name, E in self.eng.items():
            if name in skip:
                continue
            for k, (sem, val) in toks.items():
                if k == E["key"]:
                    continue
                self._wait(E, k, sem, val)
        for name in ("act", "dve", "pool"):
            E = self.eng[name]
            if name in skip:
                continue
            if E["cnt"] > 0:
                self._wait(E, E["key"], E["sem"], E["cnt"])


class Prog:
    def __init__(self, S, plan):
        self.S = S
        self.NB = S // 128
        self.NT = S // 512
        self.plan = plan
        nc = bass.Bass("TRN2", target_bir_lowering=False)
        self.nc = nc
        self.cx = Ctx(nc)
        self.dram = {}
        self._declare()

    def _dt(self, name, shape, dtype, kind):
        t = self.nc.dram_tensor(name, list(shape), dtype, kind=kind).ap()
        self.dram[name] = t
        return t

    def _declare(self):
        S = self.S
        ne, no = 2, 2
        I = "ExternalInput"
        self._dt("x", [S, D], F32, I)
        for n in ("ln_mix_pre", "ln_mix_post", "ln_ffn_pre", "ln_ffn_post"):
            self._dt(n, [DEPTH, D], F32, I)
        self._dt("w_in", [ne, D, IN_COLS], F32, I)
        self._dt("w_mix_out", [ne, D, D], F32, I)
        for n in ("lambda_q1", "lambda_k1", "lambda_q2", "lambda_k2"):
            self._dt(n, [ne, QK], F32, I)
        self._dt("attn_subln", [ne, DV], F32, I)
        self._dt("conv_w", [ne, 4, LRU_W], F32, I)
        self._dt("conv_b", [ne, LRU_W], F32, I)
        self._dt("lru_w_a", [ne, 2, 4, 128, 128], F32, I)
        self._dt("lru_b_a", [ne, 2, LRU_W], F32, I)
        self._dt("lru_w_i", [ne, 2, 4, 128, 128], F32, I)
        self._dt("lru_b_i", [ne, 2, LRU_W], F32, I)
        self._dt("lru_lambda", [ne, 2, LRU_W], F32, I)
        self._dt("w_fourier_out", [no, D, D], F32, I)
        self._dt("w_ffn_gate", [DEPTH, D, D_FF], F32, I)
        self._dt("w_ffn_up", [DEPTH, D, D_FF], F32, I)
        self._dt("w_ffn_down", [DEPTH, D_FF, D], F32, I)
        self._dt("c_ident", [128, 128], BF16, I)
        self._dt("c_qaug", [NH, 2, 4, S], BF16, I)
        self._dt("c_kaug", [4, S], BF16, I)
        self._dt("c_dist", [128, 4, 512], F32, I)
        self._dt("c_dftc", [S // 512, 128, S // 128, 257], BF16, I)
        self._dt("c_dfts", [S // 512, 128, S // 128, 257], BF16, I)
        self._dt("c_cdft", [128, 3, 2, 256], BF16, I)
        self._dt("out", [S, D], F32, "ExternalOutput")
        self._dt("xs", [S, D], F32, "Internal")
        self._dt("xr_s", [LRU_W, S], F32, "Internal")
        self._dt("yg_s", [LRU_W, S], F32, "Internal")
        self._dt("attn_s", [4, 128, S], BF16, "Internal")
        self.res = {}
        for n in ("x", "out", "xs"):
            self.res[n] = [Res() for _ in range(self.NB)]
        self.res["xr_s"] = [[Res() for _ in range(self.NT)] for _ in range(4)]
        self.res["yg_s"] = [[Res() for _ in range(self.NT)] for _ in range(4)]
        self.res["attn_s"] = [[Res() for _ in range(self.NT)] for _ in range(4)]

    def sb(self, es, name, shape, dtype):
        self._uid = getattr(self, "_uid", 0) + 1
        return es.enter_context(self.nc.sbuf_tensor(f"{name}_{self._uid}", list(shape), dtype))

    def ps(self, es, name, shape, dtype):
        self._uid = getattr(self, "_uid", 0) + 1
        return es.enter_context(self.nc.psum_tensor(f"{name}_{self._uid}", list(shape), dtype))

    def load_bcast(self, q, dst, dst_res, src_row):
        self.cx.dma(q, dst, src_row.partition_broadcast(128), writes=[dst_res])

    def rstd_from_ss(self, ss, rstd, ss_res, rstd_res, n):
        v = self.nc.vector
        cx = self.cx
        w = ss.shape[-1]
        sv = self.rs_scr[:, 0:w]
        st = self.rs_scr[:, 8:8 + w]
        R = self.rs_res
        I32 = mybir.dt.int32
        cx.op("dve", lambda: v.tensor_scalar(out=sv, in0=ss, scalar1=1.0 / n, scalar2=EPS,
                                             op0=ALU.mult, op1=ALU.add),
              reads=[ss_res], writes=[R])
        cx.op("dve", lambda: v.tensor_scalar(out=st.bitcast(I32), in0=sv.bitcast(I32), scalar1=1, scalar2=None,
                                             op0=ALU.logical_shift_right),
              reads=[R], writes=[R])
        cx.op("dve", lambda: v.tensor_scalar(out=rstd.bitcast(I32), in0=st.bitcast(I32), scalar1=-1.0,
                                             scalar2=float(0x5f3759df), op0=ALU.mult, op1=ALU.add),
              reads=[R], writes=[rstd_res])
        for _ in range(3):
            cx.op("dve", lambda: v.tensor_tensor(out=st, in0=rstd, in1=rstd, op=ALU.mult),
                  reads=[rstd_res], writes=[R])
            cx.op("dve", lambda: v.scalar_tensor_tensor(out=st, in0=st, scalar=-0.5, in1=sv, op0=ALU.mult, op1=ALU.mult),
                  reads=[R], writes=[R])
            cx.op("dve", lambda: v.scalar_tensor_tensor(out=rstd, in0=st, scalar=1.5, in1=rstd, op0=ALU.add, op1=ALU.mult),
                  reads=[R, rstd_res], writes=[rstd_res])

    def build(self):
        nc, cx = self.nc, self.cx
        with ExitStack() as es:
            self.ident = self.sb(es, "ident", [128, 128], BF16)
            self.ident_r = Res()
            self.rs_scr = self.sb(es, "rs_scr", [128, 16], F32)
            self.rs_res = Res()
            self.ssn = self.sb(es, "ssn", [128, self.NB], F32)
            self.rstdn = self.sb(es, "rstdn", [128, self.NB], F32)
            self.ssn_r = [Res() for _ in range(self.NT)]
            self.rstdn_r = [Res() for _ in range(self.NT)]
            with nc.named_scope("prepass"):
                self.stats_prepass(self.plan[0][2])
                pool_free = False
                cx.barrier(skip=("pool",) if pool_free else ())
            cx.dma("sp", self.ident[:], self.dram["c_ident"], writes=[self.ident_r])
            for ph in self.plan:
                kind = ph[0]
                if kind == "ffn":
                    with nc.named_scope(f"ffn{ph[1]}"):
                        self.phase_ffn(*ph[1:])
                elif kind == "fnet":
                    with nc.named_scope(f"fnet{ph[1]}"):
                        self.phase_fnet(*ph[1:])
                elif kind == "even":
                    self.phase_even(*ph[1:])
                else:
                    raise ValueError(kind)
                cx.barrier()
        self.cx.es.close()
        return nc

    def prenorm_T(self, xstream, b0, g_bc, g_r, hn, hn_r, ss, ss_r, rstd, rstd_r, pT, pT_r, hnT, hnT_r, nj=4):
        nc, cx = self.nc, self.cx

        def emit_hn(j):
            b = b0 + j
            xj, xj_r = xstream.get(b)
            H, H_r = hn[j % 2], hn_r[j % 2]
            cx.op("dve", lambda: nc.vector.scalar_tensor_tensor(
                out=H[:], in0=xj, scalar=self.rstdn[:, b:b + 1], in1=g_bc[:],
                op0=ALU.mult, op1=ALU.mult),
                reads=[xj_r, self.rstdn_r[b // 4], g_r], writes=[H_r])

        emit_hn(0)
        for j in range(nj):
            H, H_r = hn[j % 2], hn_r[j % 2]
            P, P_r = pT[j % 2], pT_r[j % 2]
            for c in range(8):
                cx.op("pe", lambda c=c, H=H, P=P: nc.tensor.transpose(
                    out=P[:, c * 128:(c + 1) * 128], in_=H[:, c * 128:(c + 1) * 128], identity=self.ident[:]),
                    reads=[H_r, self.ident_r], writes=[P_r])
            if j + 1 < nj:
                emit_hn(j + 1)
            src = P[:].rearrange("p (c t) -> p c t", c=8)
            dst = hnT[:, :, j * 128:(j + 1) * 128]
            cx.op("act", lambda src=src, dst=dst: nc.scalar.copy(out=dst, in_=src), reads=[P_r], writes=[hnT_r])

    def pre_hn(self, xstream, b, g_bc, g_r, H, H_r):
        nc, cx = self.nc, self.cx
        xj, xj_r = xstream.get(b)
        cx.op("dve", lambda: nc.vector.scalar_tensor_tensor(
            out=H[:], in0=xj, scalar=self.rstdn[:, b:b + 1], in1=g_bc[:], op0=ALU.mult, op1=ALU.mult),
            reads=[xj_r, self.rstdn_r[b // 4], g_r], writes=[H_r])

    def pre_tr(self, H, H_r, P, P_r, hnT, hnT_r, j):
        nc, cx = self.nc, self.cx
        for c in range(8):
            cx.op("pe", lambda c=c: nc.tensor.transpose(
                out=P[:, c * 128:(c + 1) * 128], in_=H[:, c * 128:(c + 1) * 128], identity=self.ident[:]),
                reads=[H_r, self.ident_r], writes=[P_r])
        src = P[:].rearrange("p (c t) -> p c t", c=8)
        dst = hnT[:, :, j * 128:(j + 1) * 128]
        cx.op("act", lambda: nc.scalar.copy(out=dst, in_=src), reads=[P_r], writes=[hnT_r])

    def next_stats(self, xj, xj_r, junk, junk_r, b):
        nc, cx = self.nc, self.cx
        g = b // 4
        cx.op("act", lambda: nc.scalar.activation(out=junk[:], in_=xj, func=AF.Square, accum_out=self.ssn[:, b:b + 1]),
              reads=[xj_r], writes=[junk_r, self.ssn_r[g]])
        if b % 4 == 3:
            self.rstd_from_ss(self.ssn[:, 4 * g:4 * g + 4], self.rstdn[:, 4 * g:4 * g + 4], self.ssn_r[g], self.rstdn_r[g], D)

    def postnorm_residual(self, ps_halves, ps_res, xj, xj_r, g_bc, g_r, junk, junk_r, ss2, ss2_r, rstd1, rstd1_r,
                          tmp, tmp_r, b=None, want_next=True):
        nc, cx = self.nc, self.cx
        for hf in range(2):
            cx.op("dve", lambda hf=hf: nc.vector.tensor_copy(out=tmp[:, hf * 512:(hf + 1) * 512], in_=ps_halves[hf]),
                  reads=[ps_res[hf]], writes=[tmp_r])
        cx.op("act", lambda: nc.scalar.activation(out=junk[:], in_=tmp[:], func=AF.Square, accum_out=ss2[:, 2:3]),
              reads=[tmp_r], writes=[junk_r, ss2_r])
        self.rstd_from_ss(ss2[:, 2:3], rstd1[:], ss2_r, rstd1_r, D)
        cx.op("dve", lambda: nc.vector.scalar_tensor_tensor(
            out=tmp[:], in0=tmp[:], scalar=rstd1[:], in1=g_bc[:], op0=ALU.mult, op1=ALU.mult),
            reads=[tmp_r, rstd1_r, g_r], writes=[tmp_r])
        cx.op("dve", lambda: nc.vector.tensor_tensor(out=xj, in0=tmp[:], in1=xj, op=ALU.add),
              reads=[tmp_r, xj_r], writes=[xj_r])
        if want_next and b is not None:
            self.next_stats(xj, xj_r, junk, junk_r, b)

    def proj_post_stage(self, es, get_lhs, w, w_r, g2, g2_r, src, dst, want_next):
        nc, cx = self.nc, self.cx
        NB = self.NB
        R = Res
        dstv = self.dram[dst].rearrange("(b p) d -> b p d", p=128)
        xb = XStream(self, es, "xb", src, 4)
        Ys = [self.sb(es, f"Y{i}", [128, 4, D], F32) for i in range(2)]
        Ys_r = [[R() for _ in range(4)] for _ in range(2)]
        junk = self.sb(es, "junk", [128, D], BF16)
        junk_r = R()
        junk2 = self.sb(es, "junk2", [128, D], BF16)
        junk2_r = R()
        ssx = [self.sb(es, f"ssx{i}", [128, 8], F32) for i in range(2)]
        ssx_r = [R(), R()]
        rsx = self.sb(es, "rsx", [128, 8], F32)
        rsx_r = R()
        py = [self.ps(es, f"py{i}", [128, 512], F32) for i in range(4)]
        py_r = [R() for _ in range(4)]
        NG = NB // 4

        def front(g):
            Y, Y_r = Ys[g % 2], Ys_r[g % 2]
            for jj in range(4):
                b = 4 * g + jj
                pr = [py[(b % 2) * 2], py[(b % 2) * 2 + 1]]
                pr_r = [py_r[(b % 2) * 2], py_r[(b % 2) * 2 + 1]]
                lhs = get_lhs(b)
                for hf in range(2):
                    for c in range(8):
                        la, lr = lhs[c]
                        cx.op("pe", lambda la=la, c=c, hf=hf, pr=pr: nc.tensor.matmul(
                            pr[hf][:], lhsT=la, rhs=w[:, c, hf * 512:(hf + 1) * 512], start=(c == 0), stop=(c == 7)),
                            reads=[lr, w_r], writes=[pr_r[hf]])
                for hf in range(2):
                    cx.op("act", lambda hf=hf, jj=jj, pr=pr, Y=Y: nc.scalar.copy(
                        out=Y[:, jj, hf * 512:(hf + 1) * 512], in_=pr[hf][:]),
                        reads=[pr_r[hf]], writes=[Y_r[jj]])
                cx.op("act", lambda jj=jj, Y=Y, g=g: nc.scalar.activation(
                    out=junk[:], in_=Y[:, jj, :], func=AF.Square, accum_out=ssx[g % 2][:, jj:jj + 1]),
                    reads=[Y_r[jj]], writes=[junk_r, ssx_r[g % 2]])

        def back(g):
            Y, Y_r = Ys[g % 2], Ys_r[g % 2]
            wdt = 8 if (want_next and g > 0) else 4
            self.rstd_from_ss(ssx[g % 2][:, 0:wdt], rsx[:, 0:wdt], ssx_r[g % 2], rsx_r, D)
            if wdt == 8:
                cx.op("dve", lambda: nc.vector.tensor_copy(out=self.rstdn[:, 4 * (g - 1):4 * g], in_=rsx[:, 4:8]),
                      reads=[rsx_r], writes=[self.rstdn_r[g - 1]])
            for jj in range(4):
                b = 4 * g + jj
                xj, xj_r = xb.get(b)
                cx.op("dve", lambda jj=jj, Y=Y: nc.vector.scalar_tensor_tensor(
                    out=Y[:, jj, :], in0=Y[:, jj, :], scalar=rsx[:, jj:jj + 1], in1=g2[:], op0=ALU.mult, op1=ALU.mult),
                    reads=[Y_r[jj], rsx_r, g2_r], writes=[Y_r[jj]])
                cx.op("dve", lambda jj=jj, xj=xj, Y=Y: nc.vector.tensor_tensor(out=xj, in0=Y[:, jj, :], in1=xj, op=ALU.add),
                      reads=[Y_r[jj], xj_r], writes=[xj_r])
                if want_next:
                    cx.op("act", lambda jj=jj, xj=xj, g=g: nc.scalar.activation(
                        out=junk2[:], in_=xj, func=AF.Square, accum_out=ssx[(g + 1) % 2][:, 4 + jj:5 + jj]),
                        reads=[xj_r], writes=[junk2_r, ssx_r[(g + 1) % 2]])
                cx.dma("pool", dstv[b], xj, reads=[xj_r], writes=[self.res[dst][b]])

        front(0)
        for g in range(NG):
            if g + 1 < NG:
                front(g + 1)
            back(g)
        if want_next:
            self.rstd_from_ss(ssx[NG % 2][:, 4:8], rsx[:, 4:8], ssx_r[NG % 2], rsx_r, D)
            cx.op("dve", lambda: nc.vector.tensor_copy(out=self.rstdn[:, 4 * (NG - 1):4 * NG], in_=rsx[:, 4:8]),
                  reads=[rsx_r], writes=[self.rstdn_r[NG - 1]])

    def stats_prepass(self, src):
        with ExitStack() as es:
            xp = XStream(self, es, "xp", src, 4)
            junk = self.sb(es, "junkp", [128, D], BF16)
            junk_r = Res()
            for b in range(self.NB):
                xj, xj_r = xp.get(b)
                self.next_stats(xj, xj_r, junk, junk_r, b)

    def phase_ffn(self, l, src, dst):
        nc, cx = self.nc, self.cx
        S, NT, NB = self.S, self.NT, self.NB
        dstv = self.dram[dst].rearrange("(b p) d -> b p d", p=128)
        wg_d = self.dram["w_ffn_gate"][l].rearrange("(c p) n -> p c n", p=128)
        wu_d = self.dram["w_ffn_up"][l].rearrange("(c p) n -> p c n", p=128)
        wd_d = self.dram["w_ffn_down"][l].rearrange("(f p) n -> p f n", p=128)
        R = Res
        with ExitStack() as es:
            wg = self.sb(es, "wg", [128, 8, D_FF], BF16)
            wu = self.sb(es, "wu", [128, 8, D_FF], BF16)
            wd = self.sb(es, "wd", [128, NF, D], BF16)
            g1 = self.sb(es, "g1", [128, D], F32)
            g2 = self.sb(es, "g2", [128, D], F32)
            xa = XStream(self, es, "xa", src, 2)
            xb = XStream(self, es, "xb", src, 2)
            hn = [self.sb(es, f"hn{i}", [128, D], BF16) for i in range(2)]
            hnTs = [self.sb(es, f"hnT{i}", [128, 8, 512], BF16) for i in range(2)]
            hT = self.sb(es, "hT", [128, NF, 512], BF16)
            sg = [self.sb(es, "sg0", [128, 512], F32)] * 2
            tmp = self.sb(es, "tmp", [128, D], F32)
            junk = self.sb(es, "junk", [128, D], BF16)
            ss2 = self.sb(es, "ss2", [128, 4], F32)
            rstd1 = self.sb(es, "rstd1", [128, 1], F32)
            pT = [self.ps(es, f"pT{i}", [128, D], BF16) for i in range(2)]
            pg = [self.ps(es, f"pg{i}", [128, 512], F32) for i in range(2)]
            pu = [self.ps(es, f"pu{i}", [128, 512], F32) for i in range(2)]
            py = [self.ps(es, f"py{i}", [128, 512], F32) for i in range(2)]
            wg_r = [R() for _ in range(6)]
            wu_r = [R() for _ in range(6)]
            wd_r = [R() for _ in range(NF)]
            g1_r, g2_r = R(), R()
            tmp_r, junk_r, ss2_r, rstd1_r = (R() for _ in range(4))
            hnTs_r = [R(), R()]
            hn_r = [R(), R()]
            hT_r = [R() for _ in range(NF)]
            sg_r = [R()] * 2
            pT_r = [R(), R()]
            pg_r = [R(), R()]
            pu_r = [R(), R()]
            py_r = [R(), R()]

            self.load_bcast("sp", g1[:], g1_r, self.dram["ln_ffn_pre"][l])
            self.load_bcast("sp", g2[:], g2_r, self.dram["ln_ffn_post"][l])
            xa.get(0)
            for gi in range(6):
                c0 = gi * 512
                c1 = min(D_FF, c0 + 512)
                cx.dma("pool", wg[:, :, c0:c1], wg_d[:, :, c0:c1], writes=[wg_r[gi]])
                cx.dma("pool", wu[:, :, c0:c1], wu_d[:, :, c0:c1], writes=[wu_r[gi]])
            for f in range(NF):
                cx.dma("pool", wd[:, f, :], wd_d[:, f, :], writes=[wd_r[f]])

            for j in range(4):
                self.pre_hn(xa, j, g1, g1_r, hn[j % 2], hn_r[j % 2])
                self.pre_tr(hn[j % 2], hn_r[j % 2], pT[j % 2], pT_r[j % 2], hnTs[0], hnTs_r[0], j)
            for t in range(NT):
                hnT, hnT_r = hnTs[t % 2], hnTs_r[t % 2]
                for f in range(NF):
                    gi = f // 4
                    G, U = pg[f % 2], pu[f % 2]
                    G_r, U_r = pg_r[f % 2], pu_r[f % 2]
                    for c in range(8):
                        cx.op("pe", lambda c=c, G=G: nc.tensor.matmul(G[:], lhsT=wg[:, c, f * 128:(f + 1) * 128],
                                                                       rhs=hnT[:, c, :], start=(c == 0), stop=(c == 7)),
                              reads=[wg_r[gi], hnT_r], writes=[G_r])
                    for c in range(8):
                        cx.op("pe", lambda c=c, U=U: nc.tensor.matmul(U[:], lhsT=wu[:, c, f * 128:(f + 1) * 128],
                                                                       rhs=hnT[:, c, :], start=(c == 0), stop=(c == 7)),
                              reads=[wu_r[gi], hnT_r], writes=[U_r])
                    sgt, sgt_r = sg[f % 2], sg_r[f % 2]
                    cx.op("act", lambda G=G, sgt=sgt: nc.scalar.activation(out=sgt[:], in_=G[:], func=AF.Silu),
                          reads=[G_r], writes=[sgt_r])
                    cx.op("dve", lambda U=U, sgt=sgt, f=f: nc.vector.tensor_tensor(out=hT[:, f, :], in0=sgt[:], in1=U[:],
                                                                                     op=ALU.mult),
                          reads=[sgt_r, U_r], writes=[hT_r[f]])
                    if t + 1 < NT and f >= 4 and f < 20:
                        j = (f - 4) // 4
                        if (f - 4) % 4 == 0:
                            self.pre_hn(xa, 4 * (t + 1) + j, g1, g1_r, hn[j % 2], hn_r[j % 2])
                        elif (f - 4) % 4 == 2:
                            self.pre_tr(hn[j % 2], hn_r[j % 2], pT[j % 2], pT_r[j % 2],
                                        hnTs[(t + 1) % 2], hnTs_r[(t + 1) % 2], j)
                for j in range(4):
                    b = 4 * t + j
                    for hf in range(2):
                        for f in range(NF):
                            cx.op("pe", lambda f=f, hf=hf, j=j: nc.tensor.matmul(
                                py[hf][:], lhsT=hT[:, f, j * 128:(j + 1) * 128], rhs=wd[:, f, hf * 512:(hf + 1) * 512],
                                start=(f == 0), stop=(f == NF - 1)),
                                reads=[hT_r[f], wd_r[f]], writes=[py_r[hf]])
                    xj, xj_r = xb.get(b)
                    self.postnorm_residual([py[0][:], py[1][:]], py_r, xj, xj_r, g2, g2_r, junk, junk_r,
                                           ss2, ss2_r, rstd1, rstd1_r, tmp, tmp_r, b=b, want_next=(dst != "out"))
                    cx.dma("sp", dstv[b], xj, reads=[xj_r], writes=[self.res[dst][b]])


    def phase_fnet(self, l, src, dst):
        nc, cx = self.nc, self.cx
        S, NB = self.S, self.NB
        NTH = S // 512
        o = l // 2
        wf_d = self.dram["w_fourier_out"][o].rearrange("(c p) n -> p c n", p=128)
        R = Res
        sc1 = 1.0 / math.sqrt(S)
        sc2 = 1.0 / 16.0
        with ExitStack() as es0:
            FT = self.sb(es0, "FTall", [128, 8, S], BF16)
            FT_r = [R() for _ in range(8)]
            with ExitStack() as es:
                hnA = self.sb(es, "hnA", [128, NB, D], BF16)
                hnA_r = [R() for _ in range(NB)]
                cd = self.sb(es, "cd", [128, 3, 2, 256], BF16)
                cd_r = R()
                g1 = self.sb(es, "g1", [128, D], F32)
                g1_r = R()
                xa = XStream(self, es, "xa", src, 4)
                tabs = [self.sb(es, f"tab{i}", [128, NB, 257], BF16) for i in range(3)]
                tab_r = [R() for _ in range(3)]
                YT = self.sb(es, "YT", [128, 2, 8, 258], BF16)
                YT_r = [[R() for _ in range(8)] for _ in range(2)]
                pY = [self.ps(es, f"pY{i}", [128, 512], F32) for i in range(4)]
                pY_r = [R() for _ in range(4)]
                pF = [self.ps(es, f"pF{i}", [128, 512], F32) for i in range(4)]
                pF_r = [R() for _ in range(4)]

                self.load_bcast("sp", g1[:], g1_r, self.dram["ln_mix_pre"][l])
                cx.dma("sp", cd[:], self.dram["c_cdft"], writes=[cd_r])
                xa.get(0)
                tabsrc = (self.dram["c_dftc"], self.dram["c_dfts"])

                def load_tab(u):
                    if u < 2 * NTH:
                        cx.dma("sp", tabs[u % 3][:], tabsrc[u % 2][u // 2], writes=[tab_r[u % 3]])

                load_tab(0)
                load_tab(1)
                for b in range(NB):
                    xj, xj_r = xa.get(b)
                    cx.op("dve", lambda xj=xj, b=b: nc.vector.scalar_tensor_tensor(
                        out=hnA[:, b, :], in0=xj, scalar=self.rstdn[:, b:b + 1], in1=g1[:], op0=ALU.mult, op1=ALU.mult),
                        reads=[xj_r, self.rstdn_r[b // 4], g1_r], writes=[hnA_r[b]])
                ev = 0
                for st in range(NTH):
                    w = 257 if st == NTH - 1 else 256
                    if st == 0 and NTH > 1:
                        banks = pY + pF
                        banks_r = pY_r + pF_r
                        load_tab(2)
                        for b in range(NB):
                            for a16 in range(16):
                                c, cs = a16 // 2, a16 % 2
                                bk, bk_r = banks[a16 // 2], banks_r[a16 // 2]
                                off = (a16 % 2) * 256
                                cx.op("pe", lambda b=b, c=c, cs=cs, bk=bk, off=off, a16=a16: nc.tensor.matmul(
                                    bk[:, off:off + 256], lhsT=hnA[:, b, c * 128:(c + 1) * 128], rhs=tabs[cs][:, b, 0:256],
                                    start=(b == 0 and a16 % 2 == 0), stop=(b == NB - 1), skip_group_check=True),
                                    reads=[hnA_r[b], tab_r[cs]], writes=[bk_r])
                        load_tab(3)
                        for a16 in range(16):
                            c, cs = a16 // 2, a16 % 2
                            bk, bk_r = banks[a16 // 2], banks_r[a16 // 2]
                            off = (a16 % 2) * 256
                            if (a16 // 2) % 2 == 0:
                                cx.op("act", lambda bk=bk, off=off, cs=cs, c=c: nc.scalar.activation(
                                    out=YT[:, cs, c, 0:256], in_=bk[:, off:off + 256], func=AF.Copy, scale=sc1),
                                    reads=[bk_r], writes=[YT_r[cs][c]])
                            else:
                                cx.op("dve", lambda bk=bk, off=off, cs=cs, c=c: nc.vector.tensor_scalar(
                                    out=YT[:, cs, c, 0:256], in0=bk[:, off:off + 256], scalar1=sc1, scalar2=None, op0=ALU.mult),
                                    reads=[bk_r], writes=[YT_r[cs][c]])
                    for cs in range(2):
                        if st == 0 and NTH > 1:
                            break
                        u = 2 * st + cs
                        i = u % 3
                        load_tab(u + 2)
                        for c in range(8):
                            P, P_r = pY[(c % 2) * 2 + cs], pY_r[(c % 2) * 2 + cs]
                            for b in range(NB):
                                cx.op("pe", lambda b=b, c=c, i=i, P=P: nc.tensor.matmul(
                                    P[:, 0:w], lhsT=hnA[:, b, c * 128:(c + 1) * 128], rhs=tabs[i][:, b, 0:w],
                                    start=(b == 0), stop=(b == NB - 1)),
                                    reads=[hnA_r[b], tab_r[i]], writes=[P_r])
                            if ev % 2 == 0:
                                cx.op("act", lambda P=P, cs=cs, c=c: nc.scalar.activation(
                                    out=YT[:, cs, c, 0:w], in_=P[:, 0:w], func=AF.Copy, scale=sc1),
                                    reads=[P_r], writes=[YT_r[cs][c]])
                            else:
                                cx.op("dve", lambda P=P, cs=cs, c=c: nc.vector.tensor_scalar(
                                    out=YT[:, cs, c, 0:w], in0=P[:, 0:w], scalar1=sc1, scalar2=None, op0=ALU.mult),
                                    reads=[P_r], writes=[YT_r[cs][c]])
                            ev += 1
                    for c2 in range(8):
                        g, half = c2 // 2, c2 % 2
                        for mir in range(2):
                            P, P_r = pF[(c2 % 2) * 2 + mir], pF_r[(c2 % 2) * 2 + mir]
                            k = 0
                            for cs in range(2):
                                ti = 0 if cs == 0 else (1 + mir)
                                for dh in range(2):
                                    cx.op("pe", lambda cs=cs, dh=dh, P=P, ti=ti, k=k: nc.tensor.matmul(
                                        P[:, 0:w], lhsT=cd[:, ti, dh, half * 128:(half + 1) * 128],
                                        rhs=YT[:, cs, 2 * g + dh, 0:w], start=(k == 0), stop=(k == 3)),
                                        reads=[cd_r, YT_r[cs][2 * g + dh]], writes=[P_r])
                                    k += 1
                            if mir == 0:
                                lo = st * 256
                                cx.op("act", lambda P=P, c2=c2, lo=lo: nc.scalar.activation(
                                    out=FT[:, c2, lo:lo + w], in_=P[:, 0:w], func=AF.Copy, scale=sc2),
                                    reads=[P_r], writes=[FT_r[c2]])
                            else:
                                i0 = 1 if st == 0 else 0
                                n = 256 - i0
                                t_hi = S - 256 * st - i0
                                dsto = FT[:, c2, t_hi - n + 1:t_hi + 1]
                                cx.op("dve", lambda P=P, dsto=dsto, i0=i0, n=n: nc.vector.tensor_scalar(
                                    out=dsto[:, ::-1], in0=P[:, i0:i0 + n], scalar1=sc2, scalar2=None, op0=ALU.mult),
                                    reads=[P_r], writes=[FT_r[c2]])
                cx.barrier()
            with ExitStack() as es:
                wf = self.sb(es, "wf", [128, 8, D], BF16)
                wf_r = R()
                g2 = self.sb(es, "g2", [128, D], F32)
                g2_r = R()
                cx.dma("pool", wf[:], wf_d, writes=[wf_r])
                self.load_bcast("sp", g2[:], g2_r, self.dram["ln_mix_post"][l])

                def get_lhs(b):
                    return [(FT[:, c, b * 128:(b + 1) * 128], FT_r[c]) for c in range(8)]

                self.proj_post_stage(es, get_lhs, wf, wf_r, g2, g2_r, src, dst, want_next=(dst != "out"))

    def phase_even(self, l, src, dst):
        cx = self.cx
        S = self.S
        with ExitStack() as es1:
            qT = self.sb(es1, "qT", [128, 4, S], BF16)
            kaug = [self.sb(es1, f"kaug{m}", [128, 4, S], BF16) for m in range(2)]
            vaug = self.sb(es1, "vaug", [128, self.NB, 4, VA], BF16)
            qT_r = [Res() for _ in range(self.NT)]
            kaug_r = [Res() for _ in range(self.NT)]
            v_r = [Res() for _ in range(self.NB)]
            with self.nc.named_scope(f"even_proj{l}"):
                self.even_proj(l, src, qT, qT_r, kaug, kaug_r, vaug, v_r)
                cx.barrier()
            with self.nc.named_scope(f"even_attn{l}"):
                self.even_attn(l, qT, qT_r, kaug, kaug_r, vaug, v_r)
                cx.barrier()
        with ExitStack() as es2:
            recT = self.sb(es2, "recT", [128, 4, S], BF16)
            recT_r = [Res() for _ in range(4)]
            with self.nc.named_scope(f"even_lru{l}"):
                self.even_lru(l, recT, recT_r)
                cx.barrier()
            with self.nc.named_scope(f"even_out{l}"):
                self.even_out(l, src, dst, recT, recT_r)
                cx.barrier()

    def even_proj(self, l, src, qT, qT_r, kaug, kaug_r, vaug, v_r):
        nc, cx = self.nc, self.cx
        S, NT, NB = self.S, self.NT, self.NB
        e = l // 2
        R = Res
        win_d = self.dram["w_in"][e].rearrange("(c p) n -> p c n", p=128)
        with ExitStack() as es:
            win = self.sb(es, "win", [128, 8, IN_COLS], BF16)
            win_r = [R() for _ in range(5)]
            g1 = self.sb(es, "g1", [128, D], F32)
            g1_r = R()
            xa = XStream(self, es, "xa", src, 2)
            hn = [self.sb(es, f"hn{i}", [128, D], BF16) for i in range(2)]
            hn_r = [R(), R()]
            hnTs = [self.sb(es, f"hnT{i}", [128, 8, 512], BF16) for i in range(2)]
            hnTs_r = [R(), R()]
            stg = [self.sb(es, f"stg{i}", [128, 512], F32) for i in range(3)]
            stg_r = [R() for _ in range(3)]
            stg_i = [0]
            pT = [self.ps(es, f"pT{i}", [128, D], BF16) for i in range(2)]
            pT_r = [R(), R()]
            pp = [self.ps(es, f"pp{i}", [128, 512], F32) for i in range(6)]
            pp_r = [R() for _ in range(6)]

            self.load_bcast("sp", g1[:], g1_r, self.dram["ln_mix_pre"][l])
            xa.get(0)
            for gi in range(5):
                cx.dma("pool", win[:, :, gi * 512:(gi + 1) * 512], win_d[:, :, gi * 512:(gi + 1) * 512],
                       writes=[win_r[gi]])
            cx.op("dve", lambda: nc.vector.memset(vaug[:, :, :, DV:VA], 1.0), writes=v_r)
            cx.op("pool", lambda: nc.gpsimd.memset(kaug[0][64:128, :, :], 0.0), writes=kaug_r)
            cx.op("pool", lambda: nc.gpsimd.memset(kaug[1][0:64, :, :], 0.0), writes=kaug_r)
            for h in range(NH):
                cx.dma("pool", kaug[0][64:68, h, :], self.dram["c_kaug"], writes=kaug_r)
                cx.dma("pool", kaug[1][0:4, h, :], self.dram["c_kaug"], writes=kaug_r)
            k = 0
            for j in range(4):
                self.pre_hn(xa, j, g1, g1_r, hn[j % 2], hn_r[j % 2])
                self.pre_tr(hn[j % 2], hn_r[j % 2], pT[j % 2], pT_r[j % 2], hnTs[0], hnTs_r[0], j)
            gcount = [0]

            def interleave(t):
                gi_ = gcount[0]
                gcount[0] += 1
                if t + 1 >= NT or gi_ < 2 or gi_ >= 18:
                    return
                j = (gi_ - 2) // 4
                if (gi_ - 2) % 4 == 0:
                    self.pre_hn(xa, 4 * (t + 1) + j, g1, g1_r, hn[j % 2], hn_r[j % 2])
                elif (gi_ - 2) % 4 == 2:
                    self.pre_tr(hn[j % 2], hn_r[j % 2], pT[j % 2], pT_r[j % 2], hnTs[(t + 1) % 2], hnTs_r[(t + 1) % 2], j)

            for t in range(NT):
                hnT, hnT_r = hnTs[t % 2], hnTs_r[t % 2]
                gcount[0] = 0
                tsl = slice(t * 512, (t + 1) * 512)
                for grp in (0, 1, 3, 4):
                    for m in range(4):
                        col = grp * 512 + m * 128
                        P, P_r = pp[k % 6], pp_r[k % 6]
                        for c in range(8):
                            cx.op("pe", lambda c=c, P=P, col=col: nc.tensor.matmul(
                                P[:], lhsT=win[:, c, col:col + 128], rhs=hnT[:, c, :], start=(c == 0), stop=(c == 7)),
                                reads=[win_r[grp], hnT_r], writes=[P_r])
                        if grp == 1:
                            if k % 2 == 0:
                                cx.op("act", lambda P=P, m=m: nc.scalar.copy(out=kaug[0][0:64, m, tsl], in_=P[0:64, :]),
                                      reads=[P_r], writes=[kaug_r[t]])
                                cx.op("act", lambda P=P, m=m: nc.scalar.copy(out=kaug[1][64:128, m, tsl], in_=P[64:128, :]),
                                      reads=[P_r], writes=[kaug_r[t]])
                            else:
                                cx.op("dve", lambda P=P, m=m: nc.vector.tensor_copy(out=kaug[0][0:64, m, tsl], in_=P[0:64, :]),
                                      reads=[P_r], writes=[kaug_r[t]])
                                cx.op("dve", lambda P=P, m=m: nc.vector.tensor_copy(out=kaug[1][64:128, m, tsl], in_=P[64:128, :]),
                                      reads=[P_r], writes=[kaug_r[t]])
                            k += 1
                            interleave(t)
                            continue
                        if grp == 0:
                            dst_ap, dst_r = qT[:, m, tsl], qT_r[t]
                        else:
                            si = stg_i[0] % 3
                            stg_i[0] += 1
                            dst_ap, dst_r = stg[si][:], stg_r[si]
                        if k % 2 == 0:
                            cx.op("act", lambda P=P, dst_ap=dst_ap: nc.scalar.copy(out=dst_ap, in_=P[:]),
                                  reads=[P_r], writes=[dst_r])
                        else:
                            cx.op("dve", lambda P=P, dst_ap=dst_ap: nc.vector.tensor_copy(out=dst_ap, in_=P[:]),
                                  reads=[P_r], writes=[dst_r])
                        if grp >= 3:
                            nm = "xr_s" if grp == 3 else "yg_s"
                            cx.dma("sp", self.dram[nm][m * 128:(m + 1) * 128, tsl], dst_ap,
                                   reads=[dst_r], writes=[self.res[nm][m][t]])
                        k += 1
                        interleave(t)
                for j in range(4):
                    b = 4 * t + j
                    P, P_r = pp[k % 6], pp_r[k % 6]
                    for c in range(8):
                        cx.op("pe", lambda c=c, P=P, j=j: nc.tensor.matmul(
                            P[:], lhsT=hnT[:, c, j * 128:(j + 1) * 128], rhs=win[:, c, 1024:1536],
                            start=(c == 0), stop=(c == 7)),
                            reads=[win_r[2], hnT_r], writes=[P_r])
                    srcv = P[:].rearrange("p (h d) -> p h d", h=4)
                    if k % 2 == 0:
                        cx.op("act", lambda srcv=srcv, b=b: nc.scalar.copy(out=vaug[:, b, :, 0:DV], in_=srcv),
                              reads=[P_r], writes=[v_r[b]])
                    else:
                        cx.op("dve", lambda srcv=srcv, b=b: nc.vector.tensor_copy(out=vaug[:, b, :, 0:DV], in_=srcv),
                              reads=[P_r], writes=[v_r[b]])
                    k += 1
                    interleave(t)

    def even_attn(self, l, qT, qT_r, kaug, kaug_r, vaug, v_r):
        nc, cx = self.nc, self.cx
        S, NT, NB = self.S, self.NT, self.NB
        e = l // 2
        lam_init = 0.8 - 0.6 * math.exp(-0.3 * l)
        R = Res
        NACC = 8 * VA
        with ExitStack() as es:
            pS2 = [self.ps(es, f"pS{i}", [128, 2, 512], F32) for i in range(2)]
            pS2_r = [R(), R()]
            pA = [self.ps(es, f"pA{i}", [128, 512], F32) for i in range(3)]
            pA_r = [R() for _ in range(3)]
            pTr = self.ps(es, "pTr", [128, D], BF16)
            pTr_r = R()
            dist = self.sb(es, "dist", [128, 4, 512], F32)
            dist_r = R()
            lqk = self.sb(es, "lqk", [128, 4, QK], F32)
            lqk_r = R()
            lsm = self.sb(es, "lsm", [128, 8], F32)
            lsm_r = R()
            lam = self.sb(es, "lam", [128, 1], F32)
            lam_r = R()
            gsub = self.sb(es, "gsub", [128, DV], F32)
            gsub_r = R()
            qz = [self.sb(es, f"qz{i}", [128, 2, 2, 512], BF16) for i in range(2)]
            qzd_r = [R(), R()]
            qza_r = [R(), R()]
            PT2 = [self.sb(es, f"PT{i}", [128, 2, 512], BF16) for i in range(3)]
            PT2_r = [R() for _ in range(3)]
            dtmp = [self.sb(es, f"dtmp{i}", [128, 2, 512], F32) for i in range(2)]
            dtmp_r = [R(), R()]
            acc = [self.sb(es, f"acc{i}", [128, NACC], F32) for i in range(2)]
            acc_r = [R(), R()]
            rl = self.sb(es, "rl", [128, 2, 4], F32)
            rl_r = R()
            c1 = self.sb(es, "c1", [128, 4], F32)
            c1_r = R()
            Ot = self.sb(es, "Ot", [128, 4, DV], F32)
            Ot_r = [R() for _ in range(4)]
            tO = self.sb(es, "tO", [128, DV], F32)
            tO_r = R()
            jk = self.sb(es, "jk", [128, DV], F32)
            jk_r = R()
            ssa = self.sb(es, "ssa", [128, 4], F32)
            ssa_r = R()
            rsa = self.sb(es, "rsa", [128, 4], F32)
            rsa_r = R()
            ab = self.sb(es, "ab", [128, 4, DV], BF16)
            ab_r = [R() for _ in range(4)]
            ats = [self.sb(es, f"ats{i}", [128, 512], BF16) for i in range(2)]
            ats_r = [R(), R()]

            cx.dma("sp", dist[:], self.dram["c_dist"], writes=[dist_r])
            for i, nm in enumerate(("lambda_q1", "lambda_k1", "lambda_q2", "lambda_k2")):
                self.load_bcast("sp", lqk[:, i, :], lqk_r, self.dram[nm][e])
            self.load_bcast("sp", gsub[:], gsub_r, self.dram["attn_subln"][e])
            for i in range(2):
                cx.op("pool", lambda i=i: nc.gpsimd.memset(qz[i][:], 0.0), writes=[qzd_r[i], qza_r[i]])
            for i in range(2):
                cx.op("dve", lambda i=i: nc.vector.scalar_tensor_tensor(
                    out=lqk[:, 2 * i, :], in0=lqk[:, 2 * i, :], scalar=1.0, in1=lqk[:, 2 * i + 1, :],
                    op0=ALU.mult, op1=ALU.mult, accum_out=lsm[:, i:i + 1]),
                    reads=[lqk_r], writes=[lqk_r, lsm_r])
            cx.op("act", lambda: nc.scalar.activation(out=lsm[:, 2:4], in_=lsm[:, 0:2], func=AF.Exp),
                  reads=[lsm_r], writes=[lsm_r])
            cx.op("dve", lambda: nc.vector.tensor_tensor(out=lsm[:, 4:5], in0=lsm[:, 2:3], in1=lsm[:, 3:4], op=ALU.subtract),
                  reads=[lsm_r], writes=[lsm_r])
            cx.op("dve", lambda: nc.vector.tensor_scalar(out=lam[:], in0=lsm[:, 4:5], scalar1=lam_init, scalar2=None,
                                                         op0=ALU.add),
                  reads=[lsm_r], writes=[lam_r])
            cx.op("dve", lambda: nc.vector.tensor_scalar(out=gsub[:], in0=gsub[:], scalar1=(1.0 - lam_init), scalar2=None,
                                                         op0=ALU.mult),
                  reads=[gsub_r], writes=[gsub_r])

            pairs = [(h, j) for h in range(NH) for j in range(NT)]

            def build_qz(pi):
                h, j = pairs[pi]
                Z = qz[pi % 2]
                tsl = slice(j * 512, (j + 1) * 512)
                for v in range(2):
                    cx.dma("sp", Z[0:64, 0, v, :], qT[0:64, h, tsl], reads=[qT_r[j]], writes=[qzd_r[pi % 2]])
                    cx.dma("sp", Z[64:128, 1, v, :], qT[64:128, h, tsl], reads=[qT_r[j]], writes=[qzd_r[pi % 2]])
                    cx.dma("sp", Z[64:68, 0, v, :], self.dram["c_qaug"][h, v, :, tsl], writes=[qza_r[pi % 2]])
                    cx.dma("sp", Z[0:4, 1, v, :], self.dram["c_qaug"][h, v, :, tsl], writes=[qza_r[pi % 2]])

            steps = []
            for pi, (h, j) in enumerate(pairs):
                order = [i for i in range(NB) if not (4 * j <= i < 4 * j + 4)] + list(range(4 * j, 4 * j + 4))
                for n, i in enumerate(order):
                    steps.append((pi, i, n == 0, n == NB - 1))

            def emit_qk(s):
                pi, i, first, last = steps[s]
                h, j = pairs[pi]
                Z = qz[pi % 2]
                v = 1 if i >= 4 * j + 4 else 0
                P = pS2[s % 2]
                for m in range(2):
                    cx.op("pe", lambda m=m: nc.tensor.matmul(
                        P[:, m, :], lhsT=kaug[m][:, h, i * 128:(i + 1) * 128], rhs=Z[:, m, v, :], start=True, stop=True),
                        reads=[kaug_r[i // 4], qzd_r[pi % 2], qza_r[pi % 2]], writes=[pS2_r[s % 2]])

            def emit_exp(s):
                pi, i, first, last = steps[s]
                h, j = pairs[pi]
                P, P_r = pS2[s % 2], pS2_r[s % 2]
                T, T_r = PT2[s % 3], PT2_r[s % 3]
                if 4 * j <= i < 4 * j + 4:
                    r = i - 4 * j
                    dt_, dt_r = dtmp[s % 2], dtmp_r[s % 2]
                    for m in range(2):
                        cx.op("dve", lambda m=m: nc.vector.scalar_tensor_tensor(
                            out=dt_[:, m, :], in0=dist[:, r, :], scalar=-8.0 * SLOPES[h], in1=P[:, m, :],
                            op0=ALU.mult, op1=ALU.add),
                            reads=[dist_r, P_r], writes=[dt_r])
                    cx.op("act", lambda: nc.scalar.activation(out=T[:], in_=dt_[:], func=AF.Exp, scale=0.125),
                          reads=[dt_r], writes=[T_r])
                else:
                    cx.op("act", lambda: nc.scalar.activation(out=T[:], in_=P[:], func=AF.Exp, scale=0.125),
                          reads=[P_r], writes=[T_r])

            def emit_pv(s):
                pi, i, first, last = steps[s]
                h, j = pairs[pi]
                T, T_r = PT2[s % 3], PT2_r[s % 3]
                for m in range(2):
                    for qs in range(4):
                        a = m * 4 + qs
                        bk, bk_r = pA[a // 3], pA_r[a // 3]
                        o = (a % 3) * VA
                        cx.op("pe", lambda m=m, qs=qs, bk=bk, o=o, a=a: nc.tensor.matmul(
                            bk[:, o:o + VA], lhsT=T[:, m, qs * 128:(qs + 1) * 128], rhs=vaug[:, i, h, :],
                            start=(first and a % 3 == 0), stop=last, skip_group_check=True),
                            reads=[T_r, v_r[i]], writes=[bk_r])

            def finish_pair(pi):
                h, j = pairs[pi]
                A, A_r = acc[pi % 2], acc_r[pi % 2]
                A4 = A[:].rearrange("p (m q v) -> p m q v", m=2, q=4)
                for bki, n in ((0, 3 * VA), (1, 3 * VA), (2, 2 * VA)):
                    cx.op("dve", lambda bki=bki, n=n: nc.vector.tensor_copy(
                        out=A[:, bki * 3 * VA:bki * 3 * VA + n], in_=pA[bki][:, 0:n]),
                        reads=[pA_r[bki]], writes=[A_r])
                cx.op("dve", lambda: nc.vector.reciprocal(out=rl[:], in_=A4[:, :, :, DV]),
                      reads=[A_r], writes=[rl_r])
                cx.op("dve", lambda: nc.vector.tensor_scalar(out=c1[:], in0=rl[:, 1, :], scalar1=lam[:], scalar2=None,
                                                             op0=ALU.mult),
                      reads=[rl_r, lam_r], writes=[c1_r])
                for qs in range(4):
                    cx.op("dve", lambda qs=qs: nc.vector.tensor_scalar(out=tO[:], in0=A4[:, 1, qs, 0:DV],
                                                                       scalar1=c1[:, qs:qs + 1], scalar2=None, op0=ALU.mult),
                          reads=[A_r, c1_r], writes=[tO_r])
                    cx.op("dve", lambda qs=qs: nc.vector.scalar_tensor_tensor(
                        out=Ot[:, qs, :], in0=A4[:, 0, qs, 0:DV], scalar=rl[:, 0, qs:qs + 1], in1=tO[:],
                        op0=ALU.mult, op1=ALU.subtract),
                        reads=[A_r, rl_r, tO_r], writes=[Ot_r[qs]])
                    cx.op("dve", lambda qs=qs: nc.vector.scalar_tensor_tensor(
                        out=jk[:], in0=Ot[:, qs, :], scalar=1.0, in1=Ot[:, qs, :], op0=ALU.mult, op1=ALU.mult,
                        accum_out=ssa[:, qs:qs + 1]),
                        reads=[Ot_r[qs]], writes=[jk_r, ssa_r])
                self.rstd_from_ss(ssa[:], rsa[:], ssa_r, rsa_r, DV)
                for qs in range(4):
                    cx.op("dve", lambda qs=qs: nc.vector.scalar_tensor_tensor(
                        out=ab[:, qs, :], in0=Ot[:, qs, :], scalar=rsa[:, qs:qs + 1], in1=gsub[:],
                        op0=ALU.mult, op1=ALU.mult),
                        reads=[Ot_r[qs], rsa_r, gsub_r], writes=[ab_r[qs]])

            def finish_pair_b(pi):
                h, j = pairs[pi]
                for qs in range(4):
                    cx.op("pe", lambda qs=qs: nc.tensor.transpose(
                        out=pTr[:, qs * 128:(qs + 1) * 128], in_=ab[:, qs, :], identity=self.ident[:]),
                        reads=[ab_r[qs], self.ident_r], writes=[pTr_r])
                st_, st_r = ats[pi % 2], ats_r[pi % 2]
                cx.op("dve", lambda: nc.vector.tensor_copy(out=st_[:], in_=pTr[:, 0:512]),
                      reads=[pTr_r], writes=[st_r])
                cx.dma("sp", self.dram["attn_s"][h, :, j * 512:(j + 1) * 512], st_[:], reads=[st_r],
                       writes=[self.res["attn_s"][h][j]])

            build_qz(0)
            emit_qk(0)
            nst = len(steps)
            if nst > 1:
                emit_qk(1)
            pending = None
            for s in range(nst):
                pi, i, first, last = steps[s]
                if first and pi + 1 < len(pairs):
                    build_qz(pi + 1)
                emit_exp(s)
                if s + 2 < nst:
                    emit_qk(s + 2)
                emit_pv(s)
                if pending is not None and s >= pending[1]:
                    finish_pair_b(pending[0])
                    pending = None
                if last:
                    finish_pair(pi)
                    pending = (pi, s + min(12, NB // 2))
            if pending is not None:
                finish_pair_b(pending[0])

    def even_lru(self, l, recT, recT_r):
        nc, cx = self.nc, self.cx
        S, NT = self.S, self.NT
        e = l // 2
        R = Res
        with ExitStack() as es:
            xr = self.sb(es, "xr", [128, S + 4], F32)
            xcs = [self.sb(es, f"xc{i}", [128, S], F32) for i in range(2)]
            xcb = self.sb(es, "xcb", [128, S], BF16)
            yg = self.sb(es, "yg", [128, S], F32)
            ra = [self.sb(es, f"ra{d}", [128, S], F32) for d in range(2)]
            iu = [self.sb(es, f"iu{d}", [128, S], F32) for d in range(2)]
            t1 = [self.sb(es, f"t1{d}", [128, S], F32) for d in range(2)]
            hh = t1
            xr_r, yg_r = R(), R()
            xcs_r = [R(), R()]
            xcb_r = R()
            ra_r, iu_r, t1_r = ([R(), R()] for _ in range(3))
            hh_r = t1_r
            wai = self.sb(es, "wai", [128, 2, 2, 4, 128], BF16)
            wai_r = R()
            cw = self.sb(es, "cw", [128, 4, 4], F32)
            cb = self.sb(es, "cb", [128, 4], F32)
            bai = self.sb(es, "bai", [128, 2, 2, 4], F32)
            lml = self.sb(es, "lml", [128, 2, 2, 4], F32)
            prm_r = R()
            pG = [self.ps(es, f"pG{i}", [128, 512], F32) for i in range(4)]
            pG_r = [R() for _ in range(4)]
            with nc.allow_non_contiguous_dma(reason="tiny per-channel parameter vectors"):
                def ldc(dst_ap, src_vec):
                    cx.dma("sp", dst_ap, src_vec.rearrange("(c p) -> p c", p=128), writes=[prm_r])
                for tap in range(4):
                    ldc(cw[:, :, tap], self.dram["conv_w"][e][tap])
                ldc(cb[:], self.dram["conv_b"][e])
                for d in range(2):
                    ldc(bai[:, 0, d, :], self.dram["lru_b_a"][e][d])
                    ldc(bai[:, 1, d, :], self.dram["lru_b_i"][e][d])
                    ldc(lml[:, 0, d, :], self.dram["lru_lambda"][e][d])
            for d in range(2):
                cx.dma("pool", wai[:, 0, d], self.dram["lru_w_a"][e][d].rearrange("h i j -> i h j"), writes=[wai_r])
                cx.dma("pool", wai[:, 1, d], self.dram["lru_w_i"][e][d].rearrange("h i j -> i h j"), writes=[wai_r])
            cx.op("act", lambda: nc.scalar.activation(out=lml[:, 0], in_=lml[:, 0], func=AF.Exp, scale=-1.0),
                  reads=[prm_r], writes=[prm_r])
            cx.op("act", lambda: nc.scalar.activation(out=lml[:, 0], in_=lml[:, 0], func=AF.Ln, bias=1.0),
                  reads=[prm_r], writes=[prm_r])
            cx.op("dve", lambda: nc.vector.tensor_scalar(out=lml[:, 1], in0=lml[:, 0], scalar1=-16.0, scalar2=None, op0=ALU.mult),
                  reads=[prm_r], writes=[prm_r])
            cx.op("dve", lambda: nc.vector.tensor_scalar(out=lml[:, 0], in0=lml[:, 0], scalar1=-8.0, scalar2=None, op0=ALU.mult),
                  reads=[prm_r], writes=[prm_r])
            cx.op("dve", lambda: nc.vector.memset(xr[:, 0:2], 0.0), writes=[xr_r])
            cx.op("dve", lambda: nc.vector.memset(xr[:, S + 2:S + 4], 0.0), writes=[xr_r])
            k = 0

            def conv_chunk(c):
                xc, xc_r = xcs[c % 2], xcs_r[c % 2]
                rows = slice(c * 128, (c + 1) * 128)
                cx.dma("sp", xr[:, 2:S + 2], self.dram["xr_s"][rows, :], reads=self.res["xr_s"][c], writes=[xr_r])
                cx.op("dve", lambda: nc.vector.tensor_scalar(out=xc[:], in0=xr[:, 2:S + 2], scalar1=cw[:, c, 2:3],
                                                             scalar2=cb[:, c:c + 1], op0=ALU.mult, op1=ALU.add),
                      reads=[xr_r, prm_r], writes=[xc_r])
                for tap, off in ((0, 0), (1, 1), (3, 3)):
                    cx.op("dve", lambda tap=tap, off=off: nc.vector.scalar_tensor_tensor(
                        out=xc[:], in0=xr[:, off:off + S], scalar=cw[:, c, tap:tap + 1], in1=xc[:],
                        op0=ALU.mult, op1=ALU.add),
                        reads=[xr_r, prm_r, xc_r], writes=[xc_r])

            def cast_chunk(c):
                cx.op("act", lambda: nc.scalar.copy(out=xcb[:], in_=xcs[c % 2][:]), reads=[xcs_r[c % 2]], writes=[xcb_r])

            conv_chunk(0)
            cast_chunk(0)
            for c in range(4):
                xc, xc_r = xcs[c % 2], xcs_r[c % 2]
                rows = slice(c * 128, (c + 1) * 128)
                cx.dma("sp", yg[:], self.dram["yg_s"][rows, :], reads=self.res["yg_s"][c], writes=[yg_r])
                if c + 1 < 4:
                    conv_chunk(c + 1)
                for d in range(2):
                    for n in range(NT):
                        tsl = slice(n * 512, (n + 1) * 512)
                        for ai, (dstt, dstt_r) in enumerate(((ra[d], ra_r[d]), (iu[d], iu_r[d]))):
                            P, P_r = pG[k % 4], pG_r[k % 4]
                            k += 1
                            cx.op("pe", lambda P=P, ai=ai, d=d, tsl=tsl: nc.tensor.matmul(
                                P[:], lhsT=wai[:, ai, d, c, :], rhs=xcb[:, tsl], start=True, stop=True),
                                reads=[wai_r, xcb_r], writes=[P_r])
                            cx.op("act", lambda P=P, ai=ai, dstt=dstt, d=d, tsl=tsl: nc.scalar.activation(
                                out=dstt[:, tsl], in_=P[:], func=AF.Sigmoid, bias=bai[:, ai, d, c:c + 1]),
                                reads=[P_r, prm_r], writes=[dstt_r])
                    cx.op("act", lambda d=d: nc.scalar.activation(out=t1[d][:], in_=ra[d][:], func=AF.Exp,
                                                                  scale=lml[:, 1, d, c:c + 1]),
                          reads=[ra_r[d], prm_r], writes=[t1_r[d]])
                    cx.op("act", lambda d=d: nc.scalar.activation(out=t1[d][:], in_=t1[d][:], func=AF.Sqrt, scale=-1.0, bias=1.0),
                          reads=[t1_r[d]], writes=[t1_r[d]])
                    cx.op("act", lambda d=d: nc.scalar.activation(out=ra[d][:], in_=ra[d][:], func=AF.Exp,
                                                                  scale=lml[:, 0, d, c:c + 1]),
                          reads=[ra_r[d], prm_r], writes=[ra_r[d]])
                    cx.op("dve", lambda d=d: nc.vector.tensor_tensor(out=iu[d][:], in0=iu[d][:], in1=xc[:], op=ALU.mult),
                          reads=[iu_r[d], xc_r], writes=[iu_r[d]])
                if c + 1 < 4:
                    cast_chunk(c + 1)
                cx.op("act", lambda: nc.scalar.activation(out=yg[:], in_=yg[:], func=AF.Gelu_apprx_tanh),
                      reads=[yg_r], writes=[yg_r])
                for d in range(2):
                    cx.op("dve", lambda d=d: nc.vector.tensor_tensor(out=iu[d][:], in0=iu[d][:], in1=t1[d][:], op=ALU.mult),
                          reads=[iu_r[d], t1_r[d]], writes=[iu_r[d]])
                    if d == 0:
                        cx.op("dve", lambda: nc.vector.tensor_tensor_scan(
                            out=hh[0][:], data0=ra[0][:], data1=iu[0][:], initial=0.0, op0=ALU.mult, op1=ALU.add),
                            reads=[ra_r[0], iu_r[0]], writes=[hh_r[0]])
                    else:
                        cx.op("dve", lambda: nc.vector.tensor_tensor_scan(
                            out=hh[1][:, ::-1], data0=ra[1][:, ::-1], data1=iu[1][:, ::-1], initial=0.0,
                            op0=ALU.mult, op1=ALU.add),
                            reads=[ra_r[1], iu_r[1]], writes=[hh_r[1]])
                cx.op("dve", lambda: nc.vector.tensor_tensor(out=hh[0][:], in0=hh[0][:], in1=hh[1][:], op=ALU.add),
                      reads=[hh_r[0], hh_r[1]], writes=[hh_r[0]])
                cx.op("dve", lambda: nc.vector.tensor_tensor(out=recT[:, c, :], in0=hh[0][:], in1=yg[:], op=ALU.mult),
                      reads=[hh_r[0], yg_r], writes=[recT_r[c]])

    def even_out(self, l, src, dst, recT, recT_r):
        nc, cx = self.nc, self.cx
        S = self.S
        e = l // 2
        R = Res
        wo_d = self.dram["w_mix_out"][e].rearrange("(c p) n -> p c n", p=128)
        with ExitStack() as es:
            wo = self.sb(es, "wo", [128, 8, D], BF16)
            wo_r = R()
            attnT = self.sb(es, "attnT", [128, 4, S], BF16)
            attnT_r = [R() for _ in range(4)]
            g2 = self.sb(es, "g2", [128, D], F32)
            g2_r = R()
            cx.dma("pool", wo[:], wo_d, writes=[wo_r])
            for h in range(4):
                cx.dma("sp", attnT[:, h, :], self.dram["attn_s"][h], reads=self.res["attn_s"][h], writes=[attnT_r[h]])
            self.load_bcast("sp", g2[:], g2_r, self.dram["ln_mix_post"][l])

            def get_lhs(b):
                bsl = slice(b * 128, (b + 1) * 128)
                return ([(attnT[:, c, bsl], attnT_r[c]) for c in range(4)]
                        + [(recT[:, c, bsl], recT_r[c]) for c in range(4)])

            self.proj_post_stage(es, get_lhs, wo, wo_r, g2, g2_r, src, dst, want_next=(dst != "out"))


class XStream:
    def __init__(self, prog, es, name, src, nbuf):
        self.p = prog
        self.src = src
        self.view = prog.dram[src].rearrange("(b p) d -> b p d", p=128)
        self.bufs = [prog.sb(es, f"{name}{i}", [128, D], F32) for i in range(nbuf)]
        self.res = [Res() for _ in range(nbuf)]
        self.loaded = set()
        self.n = nbuf

    def _load(self, b):
        i = b % self.n
        self.p.cx.dma("sp", self.bufs[i][:], self.view[b], reads=[self.p.res[self.src][b]], writes=[self.res[i]])
        self.loaded.add(b)

    def get(self, b):
        for bb in range(b, min(b + self.n, self.p.NB)):
            if bb not in self.loaded:
                self._load(bb)
        i = b % self.n
        return self.bufs[i][:], self.res[i]


ALIBI_COLS = 512


def _consts(S):
    c = {}
    bf = ml_dtypes.bfloat16
    c["c_ident"] = np.eye(128, dtype=np.float32).astype(bf)
    p = np.arange(128)[:, None, None]
    r = np.arange(4)[None, :, None]
    q = np.arange(512)[None, None, :]
    c["c_dist"] = (2.0 * np.maximum((128 * r + p) - q, 0)).astype(np.float32)
    pos = np.arange(S)
    hi, lo = (pos // 64).astype(np.float64), (pos % 64).astype(np.float64)
    c["c_kaug"] = np.stack([np.ones(S), np.ones(S), 64.0 * hi, lo]).astype(np.float32).astype(bf)
    qa = np.zeros((NH, 2, 4, S), np.float64)
    for h in range(NH):
        s8 = 8.0 * SLOPES[h]
        left = np.stack([-s8 * 64.0 * hi, -s8 * lo, s8 * np.ones(S), s8 * np.ones(S)])
        qa[h, 0] = left
        qa[h, 1] = -left
    c["c_qaug"] = qa.astype(np.float32).astype(bf)
    NB, NTH = S // 128, S // 512
    s_in = (np.arange(NB)[None, :, None] * 128 + np.arange(128)[:, None, None]).astype(np.int64)
    tc = np.empty((NTH, 128, NB, 257), bf)
    ts = np.empty((NTH, 128, NB, 257), bf)
    for st in range(NTH):
        s_out = st * 256 + np.arange(257, dtype=np.int64)[None, None, :]
        ang = (2.0 * np.pi / S) * ((s_in * s_out) % S).astype(np.float64)
        tc[st] = np.cos(ang).astype(np.float32).astype(bf)
        sn = np.sin(ang)
        sn[np.broadcast_to(((s_in * s_out) % S) * 2 % S == 0, sn.shape)] = 0.0
        ts[st] = sn.astype(np.float32).astype(bf)
    c["c_dftc"] = tc
    c["c_dfts"] = ts
    d_in = (np.arange(2)[None, :, None] * 128 + np.arange(128)[:, None, None]).astype(np.int64)
    d_out = np.arange(256, dtype=np.int64)[None, None, :]
    ang = (2.0 * np.pi / 256) * ((d_in * d_out) % 256).astype(np.float64)
    cdt = np.stack([np.cos(ang), -np.sin(ang), np.sin(ang)], axis=1)
    c["c_cdft"] = cdt.astype(np.float32).astype(bf)
    return c


def _alibi_table():
    t = np.zeros((128, ALIBI_COLS), np.float64)
    p = np.arange(128, dtype=np.float64)
    for h in range(NH):
        sl = SLOPES[h]
        for dl in range(1, 32):
            t[:, h * 32 + dl] = sl * (p - 128.0 * dl)
        for dr in range(1, 32):
            t[:, 128 + h * 32 + dr] = -sl * (128.0 * dr + p - 127.0)
        for qs in range(4):
            t[:, 256 + qs * 4 + h] = -sl * (128.0 * qs + p)
            t[:, 272 + qs * 4 + h] = -sl * (511.0 - 128.0 * qs - p)
    return t.astype(np.float32)


_CACHE = {}


def run(inputs_per_core, S, plan, trace=False):
    key = (S, tuple(plan))
    if key not in _CACHE:
        _CACHE[key] = Prog(S, plan)
        _CACHE[key].build()
    prog = _CACHE[key]
    consts = _consts(S)
    in_maps = []
    for m in inputs_per_core:
        mm = dict(consts)
        mm.update(m)
        in_maps.append(mm)
    res = run_bass_kernel_spmd(prog.nc, in_maps, core_ids=list(range(len(in_maps))), trace=trace)
    return res


def full_plan():
    plan = []
    for l in range(DEPTH):
        src = "x" if l == 0 else "xs"
        if l % 2 == 0:
            plan.append(("even", l, src, "xs"))
        else:
            plan.append(("fnet", l, src, "xs"))
        plan.append(("ffn", l, "xs", "out" if l == DEPTH - 1 else "xs"))
    return plan


def kernel(**inputs):
    x = np.ascontiguousarray(inputs["x"], dtype=np.float32)
    B, S, _ = x.shape
    shared = {k: np.ascontiguousarray(v, dtype=np.float32) for k, v in inputs.items() if k != "x"}
    per_core = []
    for b in range(B):
        m = dict(shared)
        m["x"] = x[b]
        per_core.append(m)
    res = run(per_core, S, full_plan())
    return np.stack([r["out"] for r in res.results], axis=0).astype(np.float32)
```
